# Optimizing a Trainium2 kernel written in Bass

```python
import math
import jax, jax.numpy as jnp
from jax import lax
import numpy as np

D_MODEL = 1024
BATCH = 4
SEQ = 8192
DEPTH = 2

HEAD_DIM = 64
Q_HEADS = 8
KV_HEADS = 2
GROUP = Q_HEADS // KV_HEADS
ATTN_WIDTH = Q_HEADS * HEAD_DIM
KV_WIDTH = KV_HEADS * HEAD_DIM
WINDOW = 128
BLOCK = 128
POOL_WIDTH = D_MODEL - ATTN_WIDTH
POOL_WINDOWS = (2, 4, 8, 16)
POOL_GROUPS = len(POOL_WINDOWS)
POOL_GC = POOL_WIDTH // POOL_GROUPS
EVEN_IN = ATTN_WIDTH + 2 * KV_WIDTH + ATTN_WIDTH + POOL_WIDTH + POOL_WIDTH
EVEN_MIX = ATTN_WIDTH + POOL_WIDTH
CONV_WIDTH = D_MODEL
CONV_K = 31
ODD_IN = 3 * CONV_WIDTH
EPS = 1e-6
NEG = -1e30
N_EVEN = (DEPTH + 1) // 2
N_ODD = DEPTH // 2

kernel_name = "hybrid_swa_pool_conformer_sandwich"


def rms_norm(x, g):
    xf = x.astype(jnp.float32)
    y = xf * lax.rsqrt(jnp.mean(xf * xf, axis=-1, keepdims=True) + EPS)
    return (y * g.astype(jnp.float32)).astype(x.dtype)


def alibi_slopes(n):
    return jnp.exp2(-8.0 * jnp.arange(1, n + 1, dtype=jnp.float32) / n)


def sliding_window_attention(q, k, v, sinks):
    B, S, _ = q.shape
    nb = S // BLOCK
    q = q.reshape(B, nb, BLOCK, KV_HEADS, GROUP, HEAD_DIM)
    k = k.reshape(B, nb, BLOCK, KV_HEADS, HEAD_DIM)
    v = v.reshape(B, nb, BLOCK, KV_HEADS, HEAD_DIM)
    kpad = jnp.zeros_like(k[:, :1])
    vpad = jnp.zeros_like(v[:, :1])
    kk = jnp.concatenate([jnp.concatenate([kpad, k[:, :-1]], axis=1), k], axis=2)
    vv = jnp.concatenate([jnp.concatenate([vpad, v[:, :-1]], axis=1), v], axis=2)
    scores = jnp.einsum('bnqkgd,bnskd->bnkgqs', q, kk).astype(jnp.float32) * (HEAD_DIM ** -0.5)
    qi = jnp.arange(BLOCK)[:, None] + BLOCK
    sj = jnp.arange(2 * BLOCK)[None, :]
    dist = qi - sj
    key_pos = jnp.arange(nb)[:, None, None] * BLOCK + sj[None] - BLOCK
    valid = (dist >= 0)[None] & (dist < WINDOW)[None] & (key_pos >= 0)
    slopes = alibi_slopes(Q_HEADS).reshape(KV_HEADS, GROUP)
    bias = -slopes[:, :, None, None] * dist.astype(jnp.float32)
    scores = jnp.where(valid[None, :, None, None], scores + bias, NEG)
    sink = sinks.astype(jnp.float32).reshape(KV_HEADS, GROUP)[None, None, :, :, None, None]
    mx = jnp.maximum(jnp.max(scores, axis=-1, keepdims=True), sink)
    p = jnp.exp(scores - mx)
    p = p / (jnp.sum(p, axis=-1, keepdims=True) + jnp.exp(sink - mx))
    out = jnp.einsum('bnkgqs,bnskd->bnqkgd', p.astype(vv.dtype), vv)
    return out.reshape(B, S, ATTN_WIDTH)


def multiscale_pool(u, pool_w, pool_scale):
    B, S, _ = u.shape
    uf = u.astype(jnp.float32).reshape(B, S, POOL_GROUPS, POOL_GC)
    cs = jnp.concatenate([jnp.zeros_like(uf[:, :1]), jnp.cumsum(uf, axis=1)], axis=1)
    t = jnp.arange(S)[:, None]
    win = jnp.array(POOL_WINDOWS, dtype=jnp.int32)[None, :]
    lo = jnp.maximum(t + 1 - win, 0)
    cnt = (t + 1 - lo).astype(jnp.float32)
    lower = cs[:, lo, jnp.arange(POOL_GROUPS)[None, :]]
    pooled = (cs[:, 1:] - lower) / cnt[None, :, :, None] - uf
    y = jnp.einsum('bsgc,gcd->bsgd', pooled, pool_w.astype(jnp.float32))
    y = y * pool_scale.astype(jnp.float32).reshape(POOL_GROUPS, POOL_GC)
    return y.reshape(B, S, POOL_WIDTH).astype(u.dtype)


def even_mixer(h, w_in, sinks, pool_w, pool_scale, w_out):
    proj = h @ w_in
    splits = np.cumsum([ATTN_WIDTH, KV_WIDTH, KV_WIDTH, ATTN_WIDTH, POOL_WIDTH]).tolist()
    q, k, v, ga, u, gb = jnp.split(proj, splits, axis=-1)
    ya = sliding_window_attention(q, k, v, sinks).astype(h.dtype) * jax.nn.silu(ga)
    yb = multiscale_pool(u, pool_w, pool_scale) * jax.nn.silu(gb)
    return jnp.concatenate([ya, yb], axis=-1) @ w_out


def odd_mixer(h, w_in, dw_w, dw_b, ln_g, ln_b, w_out):
    proj = h @ w_in
    a, b, gate = jnp.split(proj, [CONV_WIDTH, 2 * CONV_WIDTH], axis=-1)
    glu = a * jax.nn.sigmoid(b)
    conv = lax.conv_general_dilated(
        glu, dw_w.astype(glu.dtype), window_strides=(1,), padding=[(CONV_K - 1, 0)],
        dimension_numbers=('NWC', 'WIO', 'NWC'), feature_group_count=CONV_WIDTH)
    cf = conv.astype(jnp.float32) + dw_b.astype(jnp.float32)
    mu = jnp.mean(cf, axis=-1, keepdims=True)
    var = jnp.mean(jnp.square(cf - mu), axis=-1, keepdims=True)
    cn = (cf - mu) * lax.rsqrt(var + EPS) * ln_g.astype(jnp.float32) + ln_b.astype(jnp.float32)
    y = jax.nn.silu(cn).astype(h.dtype) * jax.nn.silu(gate)
    return y @ w_out


def setup_inputs(seed: int = 0) -> dict:
    key = jax.random.key(seed)
    ks = jax.random.split(key, 16)
    f32 = jnp.float32
    nrm = lambda k, shape, s: jax.random.normal(k, shape, f32) * s
    return {
        'x': nrm(ks[0], (BATCH, SEQ, D_MODEL), 1.0),
        'pre_norm': 1.0 + nrm(ks[1], (DEPTH, D_MODEL), 0.05),
        'post_norm': 1.0 + nrm(ks[2], (DEPTH, D_MODEL), 0.05),
        'a_w_in': nrm(ks[3], (N_EVEN, D_MODEL, EVEN_IN), D_MODEL ** -0.5),
        'a_sinks': nrm(ks[4], (N_EVEN, Q_HEADS), 0.5),
        'b_pool_w': nrm(ks[5], (N_EVEN, POOL_GROUPS, POOL_GC, POOL_GC), POOL_GC ** -0.5),
        'b_pool_scale': 1.0 + nrm(ks[6], (N_EVEN, POOL_WIDTH), 0.1),
        'ab_w_out': nrm(ks[7], (N_EVEN, EVEN_MIX, D_MODEL), EVEN_MIX ** -0.5),
        'c_w_in': nrm(ks[8], (N_ODD, D_MODEL, ODD_IN), D_MODEL ** -0.5),
        'c_dw_w': nrm(ks[9], (N_ODD, CONV_K, 1, CONV_WIDTH), CONV_K ** -0.5),
        'c_dw_b': nrm(ks[10], (N_ODD, CONV_WIDTH), 0.02),
        'c_ln_g': 1.0 + nrm(ks[11], (N_ODD, CONV_WIDTH), 0.05),
        'c_ln_b': nrm(ks[12], (N_ODD, CONV_WIDTH), 0.02),
        'c_w_out': nrm(ks[13], (N_ODD, CONV_WIDTH, D_MODEL), CONV_WIDTH ** -0.5),
    }


def reference(x, pre_norm, post_norm, a_w_in, a_sinks, b_pool_w, b_pool_scale, ab_w_out,
              c_w_in, c_dw_w, c_dw_b, c_ln_g, c_ln_b, c_w_out):
    for layer in range(DEPTH):
        h = rms_norm(x, pre_norm[layer])
        if layer % 2 == 0:
            i = layer // 2
            y = even_mixer(h, a_w_in[i], a_sinks[i], b_pool_w[i], b_pool_scale[i], ab_w_out[i])
        else:
            i = layer // 2
            y = odd_mixer(h, c_w_in[i], c_dw_w[i], c_dw_b[i], c_ln_g[i], c_ln_b[i], c_w_out[i])
        x = x + rms_norm(y, post_norm[layer])
    return x
```

```python
import numpy as np
from contextlib import ExitStack
import concourse.bass as bass
import concourse.mybir as mybir
from concourse.bass_utils import run_bass_kernel_spmd

F32 = mybir.dt.float32
BF16 = mybir.dt.bfloat16
AF = mybir.ActivationFunctionType
ALU = mybir.AluOpType

D = 1024
NCORES = 8
TOK_CORE = 4096
EPS = 1e-6
NEG = -1e30
CONV_K = 31


class Prog:
    def __init__(self, nc, stack):
        self.nc = nc
        self.eng = {"pe": nc.tensor, "act": nc.scalar, "dve": nc.vector, "pool": nc.gpsimd, "sp": nc.sync}
        self.sem = {}
        for n in ["pe", "act", "dve", "pool"]:
            self.sem[n] = stack.enter_context(nc.semaphore("s_" + n))
        self.ND = {"sp": 24, "pool": 16, "act": 2}
        for q, n in self.ND.items():
            for i in range(n):
                self.sem[("d", q, i)] = stack.enter_context(nc.semaphore("s_d%s%d" % (q, i)))
        self.dma_cnt = {"sp": 0, "pool": 0, "act": 0}
        self.cnt = {n: 0 for n in ["pe", "act", "dve", "pool"]}
        self.dma_i = 0
        self.waited = {n: {} for n in self.eng}
        self.res = {}
        self.out_events = []

    def _deps(self, reads, writes):
        deps = {}

        def add(ev):
            if ev is None:
                return
            k, v = ev
            if deps.get(k, 0) < v:
                deps[k] = v

        for r in reads:
            st = self.res.get(r)
            if st:
                add(st["w"])
        for w in writes:
            st = self.res.get(w)
            if st:
                add(st["w"])
                for k, v in st["r"].items():
                    add((k, v))
        return deps

    def _wait(self, en, deps):
        e = self.eng[en]
        for k, v in deps.items():
            if k == "pe" and en == "pe":
                continue
            if self.waited[en].get(k, 0) >= v:
                continue
            e.wait_ge(self.sem[k], v)
            self.waited[en][k] = v

    def _record(self, ev, reads, writes):
        k, v = ev
        for r in reads:
            st = self.res.setdefault(r, {"w": None, "r": {}})
            if st["r"].get(k, 0) < v:
                st["r"][k] = v
        for w in writes:
            self.res[w] = {"w": ev, "r": {}}

    def op(self, en, fn, reads=(), writes=(), inc=True):
        self._wait(en, self._deps(reads, writes))
        ins = fn(self.eng[en])
        ev = (en, self.cnt[en] + 1)
        if inc:
            self.cnt[en] += 1
            ins.then_inc(self.sem[en], 1)
        self._record(ev, reads, writes)
        return ins

    def dma(self, out, in_, reads=(), writes=(), queue="sp", is_output=False):
        n = self.dma_cnt[queue]
        self.dma_cnt[queue] += 1
        slot = n % self.ND[queue]
        k = ("d", queue, slot)
        prev = 16 * (n // self.ND[queue])
        deps = self._deps(reads, writes)
        if prev > 0 and deps.get(k, 0) < prev:
            deps[k] = prev
        self._wait(queue, deps)
        ins = self.eng[queue].dma_start(out=out, in_=in_)
        ins.then_inc(self.sem[k], 16)
        ev = (k, prev + 16)
        self.dma_i += 1
        self._record(ev, reads, writes)
        if is_output:
            self.out_events.append(ev)
        return ev

    def finish(self):
        deps = {}
        for k, v in self.out_events:
            deps[k] = max(deps.get(k, 0), v)
        self.waited["sp"] = {}
        self._wait("sp", deps)


NW = 6
NXS = 8
T_Q, T_K, T_GA, T_U, T_GB, T_V = 0, 4, 5, 9, 13, 17
T_A, T_B, T_G = 18, 26, 34
NTILE = 42


def build_fused(n_own_blocks=32, NB=4):
    nc = bass.Bass("TRN2", target_bir_lowering=False)
    NT = 128 * (n_own_blocks + 2)
    dr = lambda n, s: nc.dram_tensor(n, s, F32, kind="ExternalInput").ap()
    xin = dr("xin", [NT, D])
    wt_d = dr("wt", [NTILE, 128, 1024])
    wo_d = dr("wo", [4, 128, 4096])
    pool_w = dr("pool_w", [4, 128, 128])
    gpreT_d = dr("gpreT", [128, 2, 8])
    gpost_d = dr("gpost", [128, 2, D])
    sink_d = dr("sinkT", [1, 1024])
    pscale_d = dr("pscaleT", [128, 4])
    btab_d = dr("btab", [128, 6, 512])
    rc0_d = dr("rc0", [128, 4, 16])
    wsh_d = dr("wsh", [128, 8, 32])
    id4_d = dr("id4", [128, 32])
    cp_d = dr("cparT", [128, 3, 8])
    yout = nc.dram_tensor("yout", [NT - 256, D], F32, kind="ExternalOutput").ap()
    wt_bf = nc.dram_tensor("wt_bf", [NTILE, 128, 1024], BF16, kind="Internal").ap()
    wo_bf = nc.dram_tensor("wo_bf", [4, 128, 4096], BF16, kind="Internal").ap()

    with ExitStack() as stack:
        P = Prog(nc, stack)
        a = nc.alloc_sbuf_tensor
        NTC = 128 * NB
        HL = CONV_K - 1
        ident = a("ident", [128, 128], BF16)
        ones = a("ones", [128, 128], BF16)
        epsc = a("epsc", [128, 1], F32)
        st = a("stt", [128, 16], F32)
        btab = a("btab_s", [128, 6, 512], BF16)
        gpost = a("gpost_s", [128, 2, D], F32)
        gpreT = a("gpreT_s", [128, 2, 8], F32)
        ES = a("ES", [1, 2, 1024], BF16)
        eshi = ES[0:1, 0, :]
        eslo = ES[0:1, 1, :]
        pscale = a("pscale", [128, 4], F32)
        rc0 = a("rc0_s", [128, 4, 16], F32)
        wsh = a("wsh_s", [128, 8, 32], F32)
        id4 = a("id4_s", [128, 32], F32)
        DG2 = a("DG2", [128, 8, 32, 32], BF16)
        GS = a("GS", [128, 2, 4, NTC + 28], BF16)
        cp = a("cp", [128, 3, 8], F32)
        PW = a("PW", [128, 4, 128], BF16)
        X = a("X", [128, NXS, D], F32)
        Hbf = a("Hbf", [128, 2, D], BF16)
        HT = a("HT", [128, 2, 8, NTC], BF16)
        TMP = a("TMP", [128, D], F32)
        Wr = a("Wr", [128, NW, 8, 128], BF16)
        WOr = a("WOr", [128, 2, 8, 512], BF16)
        KT = a("KT", [128, 128 + NTC], BF16)
        VS = a("VS", [128, NB + 1, 128], BF16)
        U = a("U", [128, 4, 16 + NTC], F32)
        PT = [a("PT%d" % i, [128, 512], BF16) for i in range(12)]
        SA = a("SA", [128, 16 + NTC], F32)
        SB = a("SB", [128, 16 + NTC], F32)
        T16 = a("T16", [128, 16], F32)
        DS = a("DS", [128, 512], F32)
        YN = a("YN", [128, 512], F32)
        R1 = a("R1", [128, 8, NTC], BF16)
        R2 = a("R2", [128, 8, NTC], F32)
        R2v = R2[:].bitcast(BF16).rearrange("p a (b c) -> p (a b) c", c=NTC)
        GLU = a("GLU", [128, 8, HL + NTC], BF16)
        SIG = [a("SIG%d" % i, [128, NTC], F32) for i in range(2)]
        CFB = [a("CFB%d" % i, [128, NTC], BF16) for i in range(2)]
        SQB = [a("SQB%d" % i, [128, NTC], BF16) for i in range(2)]
        MU = a("MU", [128, NTC], F32)
        RS = a("RS", [128, NTC], F32)
        MSQ = SA
        SCN = [a("SCN%d" % i, [128, NTC], BF16) for i in range(2)]
        psT = nc.alloc_psum_tensor("psT", [128, 8, 128], BF16)
        psA = [nc.alloc_psum_tensor("psA%d" % i, [128, 512], F32) for i in range(2)]
        psB = [nc.alloc_psum_tensor("psB%d" % i, [128, 512], F32) for i in range(2)]
        psN = nc.alloc_psum_tensor("psN", [128, 512], F32)
        psD = nc.alloc_psum_tensor("psD", [128, 512], F32)
        psV = nc.alloc_psum_tensor("psV", [128, 512], F32)

        QTk = lambda g: ("R2", g)
        GAk = lambda g: ("R2", 4 + g)
        GBk = lambda g: ("R2", 8 + g)
        PLk = lambda g: ("GS", 0, 0, g)
        PLks = lambda g: [("GS", 0, r, g) for r in range(4)]
        YAk = lambda g, i: ("R1", g, i)
        YBk = lambda g: [("R1", 4 + g, i) for i in range(NB)]
        GATEk = lambda ct: [("R1", ct, i) for i in range(NB)]
        CFk = lambda ct: [("R2", 2 * ct), ("R2", 2 * ct + 1)]
        QT = lambda g: R2v[:, g, :]
        GA = lambda g: R2v[:, 4 + g, :]
        GB = lambda g: R2v[:, 8 + g, :]
        PL = lambda g: GS[:, 0, g, 0:NTC]
        YA = lambda g: R1[:, g, :]
        YB = lambda g: R1[:, 4 + g, :]
        GATE = lambda ct: R1[:, ct, :]
        CF = lambda ct: R2[:, ct, :]

        conv_t = lambda t: P.dma(wt_bf[t], wt_d[t], writes=[("wtbf", t)], queue="pool")
        P.op("pool", lambda e: e.memset(ident[:], 0.0), writes=["ident"])
        P.op("pool", lambda e: e.affine_select(out=ident[:], in_=ident[:], pattern=[[-1, 128]], compare_op=ALU.not_equal,
                                                fill=1.0, base=0, channel_multiplier=1), reads=["ident"], writes=["ident"])
        P.op("pool", lambda e: e.memset(ones[:], 1.0), writes=["ones"])
        P.op("pool", lambda e: e.memset(epsc[:], EPS), writes=["epsc"])
        P.dma(gpreT[:], gpreT_d, writes=["gpreT"])
        P.dma(gpost[:], gpost_d, writes=["gpost"])
        P.dma(TMP[0:1, :], sink_d, writes=[("TMP", 0), ("TMP", 1)])
        P.dma(pscale[:], pscale_d, writes=["pscale"])
        P.dma(rc0[:], rc0_d, writes=["rc0"])
        P.dma(wsh[:], wsh_d, writes=["wsh"])
        P.dma(id4[:], id4_d, writes=["id4"])
        P.dma(cp[:], cp_d, writes=["cp"])
        for t in [T_K, T_V] + [T_U + g for g in range(4)]:
            conv_t(t)
        P.op("pool", lambda e: e.memset(U[:], 0.0), writes=[("U", g) for g in range(4)])
        P.op("pool", lambda e: e.memset(KT[:, 0:128], 0.0), writes=["KTh"])
        P.op("pool", lambda e: e.memset(VS[:, 0, :], 0.0), writes=[("VS", 0)])
        for t in [T_Q + g for g in range(4)]:
            conv_t(t)
        for t in range(6):
            P.dma(btab[:, t, :], btab_d[:, t, :], writes=["btab"], queue="pool")
        P.dma(PW[:], pool_w.rearrange("g c d -> c g d"), writes=["PW"], queue="pool")
        for t in range(6):
            P.op("act", lambda e, t=t: e.activation(out=btab[:, t, :], in_=btab[:, t, :], func=AF.Exp), reads=["btab"], writes=["btab"])
        for t in [T_GA + g for g in range(4)] + [T_GB + g for g in range(4)]:
            conv_t(t)
        for h in range(2):
            P.dma(wo_bf[h], wo_d[h], writes=[("wobf", h)], queue="pool")
        P.op("pool", lambda e: e.memset(GLU[:], 0.0), writes=[("GLU", ct) for ct in range(8)])
        for ct in range(2):
            for t in (T_B + ct, T_A + ct, T_G + ct):
                conv_t(t)
        P.op("pool", lambda e: e.memset(GS[:], 0.0), writes=[("GS", gi, r, cg) for gi in range(2) for r in range(4) for cg in range(4)])
        for ct in range(8):
            P.op("pool", lambda e, ct=ct: e.tensor_tensor(out=DG2[:, ct], in0=wsh[:, ct, :].unsqueeze(2).broadcast_to([128, 32, 32]),
                                                         in1=id4[:].unsqueeze(1).broadcast_to([128, 32, 32]), op=ALU.mult),
                 reads=["wsh", "id4"], writes=["DG2"])
        for ct in range(2, 8):
            for t in (T_B + ct, T_A + ct, T_G + ct):
                conv_t(t)
        for h in range(2, 4):
            P.dma(wo_bf[h], wo_d[h], writes=[("wobf", h)], queue="pool")
        tk = [("TMP", 0), ("TMP", 1)]
        P.op("act", lambda e: e.activation(out=TMP[0:1, :], in_=TMP[0:1, :], func=AF.Exp), reads=tk, writes=tk)
        P.op("dve", lambda e: e.tensor_copy(out=eshi, in_=TMP[0:1, :]), reads=tk, writes=["eshi"])
        P.op("dve", lambda e: e.tensor_tensor(out=TMP[0:1, :], in0=TMP[0:1, :], in1=eshi, op=ALU.subtract), reads=tk + ["eshi"], writes=tk)
        P.op("dve", lambda e: e.tensor_copy(out=eslo, in_=TMP[0:1, :]), reads=tk, writes=["eslo"])

        ctr = {"x": 0, "hb": 0, "psA": 0, "w": 0, "sc": 0, "sig": 0, "dg": 0, "pc": 0, "cfb": 0, "scn": 0}

        def nxt(k, m):
            v = ctr[k] % m
            ctr[k] += 1
            return v

        def wtile(t):
            s = nxt("w", NW)
            P.dma(Wr[:, s].rearrange("p a b -> p (a b)"), wt_bf[t], reads=[("wtbf", t)], writes=[("Wr", s)])
            return s

        def wo_load(l):
            for h in range(2):
                P.dma(WOr[:, h].rearrange("p a b -> p (a b)"), wo_bf[l * 2 + h], reads=[("wobf", l * 2 + h)], writes=[("WOr", h)])

        class Chunk:
            pass

        def norm_a(ch, i, l):
            s = ch.slots[i]
            hb = nxt("hb", 2)
            ch.hb[(i, l)] = hb
            P.op("act", lambda e: e.activation(out=Hbf[:, hb, :], in_=X[:, s, :], func=AF.Square, accum_out=st[:, 0:1]),
                 reads=[("X", s)], writes=[("Hbf", hb), "st0"])
            P.op("act", lambda e: e.activation(out=st[:, 1:2], in_=st[:, 0:1], func=AF.Ln, bias=epsc[:], scale=1.0 / D),
                 reads=["st0", "epsc"], writes=["st1"])
            P.op("act", lambda e: e.activation(out=st[:, 2:3], in_=st[:, 1:2], func=AF.Exp, scale=-0.5), reads=["st1"], writes=["st2"])
            P.op("dve", lambda e: e.tensor_scalar(out=Hbf[:, hb, :], in0=X[:, s, :], scalar1=st[:, 2:3], scalar2=None, op0=ALU.mult),
                 reads=[("X", s), "st2"], writes=[("Hbf", hb)])

        def norm_b(ch, i, l):
            hb = ch.hb[(i, l)]
            for kt in range(8):
                P.op("pe", lambda e, kt=kt: e.transpose(psT[:, kt, :], Hbf[:, hb, kt * 128:(kt + 1) * 128], ident[:]),
                     reads=[("Hbf", hb), "ident"], writes=["psT"], inc=(kt == 7))
            gb = gpreT[:, l, :].unsqueeze(2).broadcast_to([128, 8, 128])
            P.op("dve", lambda e: e.tensor_tensor(out=HT[:, l, :, i * 128:(i + 1) * 128], in0=psT[:], in1=gb, op=ALU.mult),
                 reads=["psT", "gpreT"], writes=[("HT", l)])

        pref = {}

        def inproj(t, ntok, l):
            ws = pref.pop(t) if t in pref else wtile(t)
            bi = nxt("psA", 2)
            for kt in range(8):
                P.op("pe", lambda e, kt=kt: e.matmul(psA[bi][:, 0:ntok], lhsT=Wr[:, ws, kt, :], rhs=HT[:, l, kt, 0:ntok],
                                                     start=(kt == 0), stop=(kt == 7)),
                     reads=[("Wr", ws), ("HT", l)], writes=[("psA", bi)], inc=(kt == 7))
            return bi

        def outproj_post(s, lhs_fn, lhs_reads, l, out_ap, bank_list, bank_key):
            for half in range(2):
                for kt in range(8):
                    P.op("pe", lambda e, kt=kt, half=half: e.matmul(bank_list[half][:], lhsT=lhs_fn(kt), rhs=WOr[:, half, kt, :],
                                                                   start=(kt == 0), stop=(kt == 7)),
                         reads=list(lhs_reads) + [("WOr", half)], writes=[(bank_key, half)], inc=(kt == 7))
            for half in range(2):
                P.op("act", lambda e, half=half: e.activation(out=TMP[:, half * 512:(half + 1) * 512], in_=bank_list[half][:], func=AF.Square,
                                                             accum_out=st[:, 4 + half:5 + half]),
                     reads=[(bank_key, half)], writes=[("TMP", half), ("st4", half)])
            P.op("dve", lambda e: e.tensor_tensor(out=st[:, 6:7], in0=st[:, 4:5], in1=st[:, 5:6], op=ALU.add),
                 reads=[("st4", 0), ("st4", 1)], writes=["st6"])
            P.op("act", lambda e: e.activation(out=st[:, 7:8], in_=st[:, 6:7], func=AF.Ln, bias=epsc[:], scale=1.0 / D),
                 reads=["st6", "epsc"], writes=["st7"])
            P.op("act", lambda e: e.activation(out=st[:, 8:9], in_=st[:, 7:8], func=AF.Exp, scale=-0.5), reads=["st7"], writes=["st8"])
            for half in range(2):
                hs = slice(half * 512, (half + 1) * 512)
                P.op("dve", lambda e, half=half, hs=hs: e.tensor_tensor(out=TMP[:, hs], in0=bank_list[half][:], in1=gpost[:, l, hs], op=ALU.mult),
                     reads=[(bank_key, half), "gpost"], writes=[("TMP", half)])
                P.op("dve", lambda e, hs=hs: e.scalar_tensor_tensor(out=X[:, s, hs], in0=TMP[:, hs], scalar=st[:, 8:9], in1=X[:, s, hs],
                                                                    op0=ALU.mult, op1=ALU.add),
                     reads=[("TMP", half), "st8", ("X", s)], writes=[("X", s)])
            if out_ap is not None:
                P.dma(out_ap, X[:, s, :], reads=[("X", s)], queue="pool", is_output=True)

        def x_load(ch):
            ch.slots = []
            for b in ch.blocks:
                s = nxt("x", NXS)
                P.dma(X[:, s, :], xin[b * 128:(b + 1) * 128, :], writes=[("X", s)])
                ch.slots.append(s)

        def l0_tile_q(ch, g):
            bi = inproj(T_Q + g, ch.ntok, 0)
            P.op("act", lambda e: e.activation(out=QT(g)[:, 0:ch.ntok], in_=psA[bi][:, 0:ch.ntok], func=AF.Copy, scale=0.125),
                 reads=[("psA", bi)], writes=[QTk(g)])

        def l0_tile_k(ch):
            bi = inproj(T_K, ch.ntok, 0)
            P.op("act", lambda e: e.activation(out=KT[:, 128:128 + ch.ntok], in_=psA[bi][:, 0:ch.ntok], func=AF.Copy), reads=[("psA", bi)], writes=["KTn"])

        def l0_tile_ga(ch, g):
            bi = inproj(T_GA + g, ch.ntok, 0)
            P.op("act", lambda e: e.activation(out=GA(g)[:, 0:ch.ntok], in_=psA[bi][:, 0:ch.ntok], func=AF.Silu),
                 reads=[("psA", bi)], writes=[GAk(g)])

        def l0_tile_gb(ch, g):
            bi = inproj(T_GB + g, ch.ntok, 0)
            P.op("act", lambda e: e.activation(out=GB(g)[:, 0:ch.ntok], in_=psA[bi][:, 0:ch.ntok], func=AF.Silu),
                 reads=[("psA", bi)], writes=[GBk(g)])

        def l0_tile_u(ch, g):
            bi = inproj(T_U + g, ch.ntok, 0)
            P.op("act", lambda e: e.activation(out=U[:, g, 16:16 + ch.ntok], in_=psA[bi][:, 0:ch.ntok], func=AF.Copy),
                 reads=[("psA", bi)], writes=[("U", g)])

        def l0_tile_v(ch):
            ws = wtile(T_V)
            for i in range(ch.nb):
                for kt in range(8):
                    P.op("pe", lambda e, kt=kt, i=i: e.matmul(psV[:, 0:128], lhsT=HT[:, 0, kt, i * 128:(i + 1) * 128], rhs=Wr[:, ws, kt, :],
                                                              start=(kt == 0), stop=(kt == 7)),
                         reads=[("Wr", ws), ("HT", 0)], writes=["psV"], inc=(kt == 7))
                P.op("act", lambda e, i=i: e.activation(out=VS[:, i + 1, :], in_=psV[:, 0:128], func=AF.Copy), reads=["psV"], writes=[("VS", i + 1)])

        def l0_scores(ch, i):
            tb = slice(i * 128, (i + 1) * 128)
            for c in range(2):
                kcols = slice(i * 128 + c * 128, i * 128 + c * 128 + 128)
                pis = [nxt("sc", 2), nxt("sc", 2)]
                for g in range(4):
                    for kv in range(2):
                        pi = pis[kv]
                        pr = slice(kv * 64, kv * 64 + 64)
                        P.op("pe", lambda e, g=g, pi=pi, pr=pr: e.matmul(psB[pi][:, g * 128:(g + 1) * 128], lhsT=KT[pr, kcols], rhs=QT(g)[pr, tb],
                                                                        start=True, stop=True),
                             reads=["KTn", "KTh", QTk(g)], writes=[("psB", pi)], inc=(g == 3))
                for kv in range(2):
                    pi = pis[kv]
                    tab = kv * 2 + c
                    if c == 0 and ch.first_own and i == 0:
                        tab = 4 + kv
                    u = (i % 3) * 4 + kv * 2 + c
                    P.op("act", lambda e, u=u, pi=pi: e.activation(out=PT[u][:], in_=psB[pi][:], func=AF.Exp), reads=[("psB", pi)], writes=[("PT", u)])
                    P.op("pool", lambda e, u=u, tab=tab: e.tensor_tensor(out=PT[u][:], in0=PT[u][:], in1=btab[:, tab, :], op=ALU.mult),
                         reads=[("PT", u), "btab"], writes=[("PT", u)])

        def l0_pv(ch, i):
            tb = slice(i * 128, (i + 1) * 128)
            for kv in range(2):
                for c in range(2):
                    u = (i % 3) * 4 + kv * 2 + c
                    P.op("pe", lambda e, kv=kv, c=c, u=u: e.matmul(psD[kv * 64:kv * 64 + 64, :], lhsT=ones[:, 0:64], rhs=PT[u][:],
                                                                  start=(c == 0), stop=False, tile_position=(0, kv * 64)),
                         reads=["ones", ("PT", u)], writes=["psD"], inc=False)
                P.op("pe", lambda e, kv=kv: e.matmul(psD[kv * 64:kv * 64 + 64, :], lhsT=ones[0:1, 0:64], rhs=eshi[:, kv * 512:(kv + 1) * 512],
                                                     start=False, stop=False, tile_position=(0, kv * 64)),
                     reads=["ones", "eshi"], writes=["psD"], inc=False)
                P.op("pe", lambda e, kv=kv: e.matmul(psD[kv * 64:kv * 64 + 64, :], lhsT=ones[0:1, 0:64], rhs=eslo[:, kv * 512:(kv + 1) * 512],
                                                     start=False, stop=True, tile_position=(0, kv * 64)),
                     reads=["ones", "eslo"], writes=["psD"], inc=(kv == 1))
            for kv in range(2):
                for c in range(2):
                    u = (i % 3) * 4 + kv * 2 + c
                    P.op("pe", lambda e, kv=kv, c=c, u=u: e.matmul(psN[kv * 64:kv * 64 + 64, :], lhsT=VS[:, i + c, kv * 64:kv * 64 + 64], rhs=PT[u][:],
                                                                  start=(c == 0), stop=(c == 1), tile_position=(0, kv * 64)),
                         reads=[("VS", i + c), ("PT", u)], writes=["psN"], inc=(kv == 1 and c == 1))
            P.op("act", lambda e: e.activation(out=DS[:], in_=psD[:], func=AF.Ln), reads=["psD"], writes=["DS"])
            P.op("act", lambda e: e.activation(out=DS[:], in_=DS[:], func=AF.Exp, scale=-1.0), reads=["DS"], writes=["DS"])
            P.op("dve", lambda e: e.tensor_tensor(out=YN[:], in0=psN[:], in1=DS[:], op=ALU.mult), reads=["psN", "DS"], writes=["YN"])
            for g in range(4):
                P.op("pool", lambda e, g=g: e.tensor_tensor(out=YA(g)[:, tb], in0=YN[:, g * 128:(g + 1) * 128], in1=GA(g)[:, tb], op=ALU.mult),
                     reads=["YN", GAk(g)], writes=[YAk(g, i)])

        def l0_pool_pre(ch, g):
            ntok = ch.ntok
            w = 2 << g
            E = 16 + ntok
            cur = None
            step = 1
            bufs = [SA, SB]
            k = 0
            while step < w:
                lo = 2 * step - 1
                dst = bufs[k % 2]
                srcap = U[:, g, :] if cur is None else cur[:]
                rk = [("U", g)] if cur is None else [("SAB", (k - 1) % 2)]
                P.op("pool", lambda e, dst=dst, srcap=srcap, lo=lo, step=step: e.tensor_tensor(
                    out=dst[:, lo:E], in0=srcap[:, lo:E], in1=srcap[:, lo - step:E - step], op=ALU.add),
                     reads=rk, writes=[("SAB", k % 2)])
                cur = dst
                k += 1
                step *= 2
            ch.pool_cur[g] = (cur, ("SAB", (k - 1) % 2))

        def l0_pool_fin(ch, g):
            ntok = ch.ntok
            w = 2 << g
            E = 16 + ntok
            cur, lastk = ch.pool_cur[g]
            P.op("dve", lambda e: e.scalar_tensor_tensor(out=PL(g)[:, 0:ntok], in0=cur[:, 16:E], scalar=1.0 / w, in1=U[:, g, 16:E],
                                                         op0=ALU.mult, op1=ALU.subtract),
                 reads=[lastk, ("U", g)], writes=PLks(g))
            if ch.first_own:
                P.op("dve", lambda e: e.tensor_tensor(out=T16[:], in0=cur[:, 16:32], in1=rc0[:, g, :], op=ALU.mult),
                     reads=[lastk, "rc0"], writes=["T16"])
                P.op("dve", lambda e: e.tensor_tensor(out=PL(g)[:, 0:16], in0=T16[:], in1=U[:, g, 16:32], op=ALU.subtract),
                     reads=["T16", ("U", g)] + PLks(g), writes=PLks(g))

        def l0_pool_mm(ch, g):
            ntok = ch.ntok
            bk, bkey = [(psV, "psV"), (psA[0], ("psA", 0)), (psA[1], ("psA", 1)), (psV, "psV")][g]
            P.op("pe", lambda e: e.matmul(bk[:, 0:ntok], lhsT=PW[:, g, :], rhs=PL(g)[:, 0:ntok], start=True, stop=True),
                 reads=["PW"] + PLks(g), writes=[bkey])
            P.op("dve", lambda e: e.scalar_tensor_tensor(out=YB(g)[:, 0:ntok], in0=bk[:, 0:ntok], scalar=pscale[:, g:g + 1],
                                                         in1=GB(g)[:, 0:ntok], op0=ALU.mult, op1=ALU.mult),
                 reads=[bkey, "pscale", GBk(g)], writes=YBk(g))

        def l0_out(ch, i):
            tb = slice(i * 128, (i + 1) * 128)
            lhs = lambda kt: (YA(kt)[:, tb] if kt < 4 else YB(kt - 4)[:, tb])
            bl_, bk_ = (psA, "psA") if i % 2 == 0 else (psB, "psB")
            outproj_post(ch.slots[i], lhs, [YAk(g, i) for g in range(4)] + [("R1", 4 + g, i) for g in range(4)], 0, None, bl_, bk_)

        def l0_carry(ch):
            ntok, nb = ch.ntok, ch.nb
            P.op("pool", lambda e: e.tensor_copy(out=KT[:, 0:128], in_=KT[:, ntok:ntok + 128]), reads=["KTn"], writes=["KTh"])
            P.op("pool", lambda e: e.tensor_copy(out=VS[:, 0, :], in_=VS[:, nb, :]), reads=[("VS", nb)], writes=[("VS", 0)])
            for g in range(4):
                P.op("pool", lambda e, g=g: e.tensor_copy(out=U[:, g, 0:16], in_=U[:, g, ntok:ntok + 16]), reads=[("U", g)], writes=[("U", g)])

        def l1_tile_ba(ch, ct):
            ntok = ch.ntok
            si = nxt("sig", 2)
            bi = inproj(T_B + ct, ntok, 1)
            P.op("act", lambda e: e.activation(out=SIG[si][:, 0:ntok], in_=psA[bi][:, 0:ntok], func=AF.Tanh, scale=0.5),
                 reads=[("psA", bi)], writes=[("SIG", si)])
            bi2 = inproj(T_A + ct, ntok, 1)
            P.op("dve", lambda e: e.scalar_tensor_tensor(out=GLU[:, ct, HL:HL + ntok], in0=SIG[si][:, 0:ntok], scalar=1.0,
                                                         in1=psA[bi2][:, 0:ntok], op0=ALU.add, op1=ALU.mult),
                 reads=[("psA", bi2), ("SIG", si)], writes=[("GLU", ct)])

        def l1_tile_g(ch, ct):
            bi3 = inproj(T_G + ct, ch.ntok, 1)
            P.op("act", lambda e: e.activation(out=GATE(ct)[:, 0:ch.ntok], in_=psA[bi3][:, 0:ch.ntok], func=AF.Silu),
                 reads=[("psA", bi3)], writes=GATEk(ct))

        def l1_gs(ch, ct):
            ntok = ch.ntok
            gi = nxt("dg", 2)
            ch.gi[ct] = gi
            for r in range(4):
                Lr = ntok + 28 - (1 if r == 3 else 0)
                for cg in range(4):
                    dst = GS[r * 32:(r + 1) * 32, gi, cg, 0:Lr]
                    src = GLU[cg * 32:(cg + 1) * 32, ct, r:r + Lr]
                    if r < 2:
                        P.op("dve", lambda e, dst=dst, src=src: e.tensor_copy(out=dst, in_=src),
                             reads=[("GLU", ct)], writes=[("GS", gi, r, cg)], inc=(r == 1 and cg == 3))
                    else:
                        P.dma(dst, src, reads=[("GLU", ct)], writes=[("GS", gi, r, cg)])

        def l1_conv(ch, ct):
            ntok = ch.ntok
            gi = ch.gi[ct]
            pi = nxt("pc", 2)
            for tg in range(8):
                for cg in range(4):
                    P.op("pe", lambda e, tg=tg, cg=cg: e.matmul(psB[pi][cg * 32:(cg + 1) * 32, 0:ntok], lhsT=DG2[:, ct, cg * 8 + tg, :],
                                                                rhs=GS[:, gi, cg, 4 * tg:4 * tg + ntok], start=(tg == 0), stop=(tg == 7),
                                                                tile_position=(0, cg * 32)),
                         reads=["DG2"] + [("GS", gi, r, cg) for r in range(4)], writes=[("psB", pi)], inc=(tg == 7 and cg == 3))
            ci = nxt("cfb", 2)
            ch.ci[ct] = ci
            P.op("act", lambda e: e.activation(out=CFB[ci][:, 0:ntok], in_=psB[pi][:, 0:ntok], func=AF.Identity,
                                               bias=cp[:, 0, ct:ct + 1], scale=0.5), reads=[("psB", pi), "cp"], writes=[("CFB", ci)])
            P.op("act", lambda e: e.activation(out=SQB[ci][:, 0:ntok], in_=psB[pi][:, 0:ntok], func=AF.Square,
                                               bias=cp[:, 0, ct:ct + 1], scale=0.5), reads=[("psB", pi), "cp"], writes=[("SQB", ci)])
            P.op("act", lambda e: e.activation(out=CF(ct)[:, 0:ntok], in_=psB[pi][:, 0:ntok], func=AF.Identity,
                                               bias=cp[:, 0, ct:ct + 1], scale=0.5),
                 reads=[("psB", pi), "cp"], writes=CFk(ct))

        def l1_stats(ch, ct):
            ntok = ch.ntok
            ci = ch.ci[ct]
            P.op("pe", lambda e: e.matmul(psN[:, 0:ntok], lhsT=ones[:], rhs=CFB[ci][:, 0:ntok], start=(ct == 0), stop=(ct == 7)),
                 reads=["ones", ("CFB", ci)], writes=["psN"])
            P.op("pe", lambda e: e.matmul(psD[:, 0:ntok], lhsT=ones[:], rhs=SQB[ci][:, 0:ntok], start=(ct == 0), stop=(ct == 7)),
                 reads=["ones", ("SQB", ci)], writes=["psD"])

        def l1_carry(ch):
            for ct in range(8):
                P.op("pool", lambda e, ct=ct: e.tensor_copy(out=GLU[:, ct, 0:HL], in_=GLU[:, ct, ch.ntok:ch.ntok + HL]),
                     reads=[("GLU", ct)], writes=[("GLU", ct)])

        def l1_ln_chain(ch):
            ntok = ch.ntok
            P.op("act", lambda e: e.activation(out=MU[:, 0:ntok], in_=psN[:, 0:ntok], func=AF.Copy, scale=1.0 / D), reads=["psN"], writes=["MU"])
            P.op("dve", lambda e: e.tensor_tensor(out=MSQ[:, 0:ntok], in0=MU[:, 0:ntok], in1=MU[:, 0:ntok], op=ALU.mult), reads=["MU"], writes=[("SAB", 0)])
            P.op("dve", lambda e: e.scalar_tensor_tensor(out=RS[:, 0:ntok], in0=psD[:, 0:ntok], scalar=1.0 / D, in1=MSQ[:, 0:ntok],
                                                         op0=ALU.mult, op1=ALU.subtract), reads=["psD", ("SAB", 0)], writes=["RS"])
            P.op("act", lambda e: e.activation(out=RS[:, 0:ntok], in_=RS[:, 0:ntok], func=AF.Ln, bias=epsc[:], scale=1.0),
                 reads=["RS", "epsc"], writes=["RS"])
            P.op("act", lambda e: e.activation(out=RS[:, 0:ntok], in_=RS[:, 0:ntok], func=AF.Exp, scale=-0.5), reads=["RS"], writes=["RS"])

        def l1_ln_apply(ch, ct):
            ntok = ch.ntok
            P.op("dve", lambda e: e.tensor_tensor(out=CF(ct)[:, 0:ntok], in0=CF(ct)[:, 0:ntok], in1=MU[:, 0:ntok], op=ALU.subtract),
                 reads=CFk(ct) + ["MU"], writes=CFk(ct))
            P.op("dve", lambda e: e.tensor_tensor(out=CF(ct)[:, 0:ntok], in0=CF(ct)[:, 0:ntok], in1=RS[:, 0:ntok], op=ALU.mult),
                 reads=CFk(ct) + ["RS"], writes=CFk(ct))
            si = nxt("scn", 2)
            P.op("act", lambda e: e.activation(out=SCN[si][:, 0:ntok], in_=CF(ct)[:, 0:ntok], func=AF.Silu,
                                               bias=cp[:, 2, ct:ct + 1], scale=cp[:, 1, ct:ct + 1]),
                 reads=CFk(ct) + ["cp"], writes=[("SCN", si)])
            P.op("pool", lambda e: e.tensor_tensor(out=GATE(ct)[:, 0:ntok], in0=GATE(ct)[:, 0:ntok], in1=SCN[si][:, 0:ntok], op=ALU.mult),
                 reads=GATEk(ct) + [("SCN", si)], writes=GATEk(ct))

        def l1_out(ch, i):
            tb = slice(i * 128, (i + 1) * 128)
            ob = ch.blocks[i] - 2
            outproj_post(ch.slots[i], lambda kt: GATE(kt)[:, tb], [("R1", ct, i) for ct in range(8)], 1,
                         yout[ob * 128:(ob + 1) * 128, :], *((psB, "psB") if i % 2 == 0 else (psA, "psA")))

        def weave(A, B):
            na, nb_ = len(A), len(B)
            ia = ib = 0
            while ia < na or ib < nb_:
                if ia < na and (ib >= nb_ or ia * max(nb_, 1) <= ib * max(na, 1)):
                    A[ia]()
                    ia += 1
                else:
                    B[ib]()
                    ib += 1

        def l0_norms(ch):
            L = []
            for i in range(ch.nb + 1):
                if i < ch.nb:
                    L.append(lambda i=i: norm_a(ch, i, 0))
                if i >= 1:
                    L.append(lambda i=i: norm_b(ch, i - 1, 0))
            return L

        def l0_early(ch):
            L = [lambda: l0_tile_k(ch), lambda: l0_tile_v(ch)]
            for g in range(4):
                L.append(lambda g=g: l0_tile_u(ch, g))
                L.append(lambda g=g: (l0_pool_fin(ch, g - 1) if g >= 1 else None, l0_pool_pre(ch, g)))
            return L

        def l0_mid(ch):
            L = [lambda g=g: l0_tile_q(ch, g) for g in range(4)]
            L += [lambda g=g: l0_tile_ga(ch, g) for g in range(4)]
            L += [lambda g=g: l0_tile_gb(ch, g) for g in range(4)]
            return L

        def l0_late(ch):
            nb = ch.nb
            wo_load(0)
            l0_pool_fin(ch, 3)
            l0_scores(ch, 0)
            if nb > 1:
                l0_scores(ch, 1)
            outs = []

            def do_out(j):
                l0_out(ch, j)
                norm_a(ch, j, 1)
                if j >= 1:
                    norm_b(ch, j - 1, 1)

            nout = 0
            for i in range(nb):
                if i + 2 < nb:
                    l0_scores(ch, i + 2)
                l0_pv(ch, i)
                if i == min(1, nb - 1):
                    for g in range(4):
                        l0_pool_mm(ch, g)
                if i >= 1:
                    do_out(nout)
                    nout += 1
            while nout < nb:
                do_out(nout)
                nout += 1
            norm_b(ch, nb - 1, 1)
            l0_carry(ch)

        def l1_front(ch, nx):
            tail = l0_norms(nx) if nx is not None else []
            if ch.halo:
                for ct in range(8):
                    l1_tile_ba(ch, ct)
                for f in tail:
                    f()
                l1_carry(ch)
                return
            for ct in range(8):
                l1_tile_ba(ch, ct)
                if ct - 1 >= 0:
                    l1_gs(ch, ct - 1)
                if ct - 2 >= 0:
                    l1_conv(ch, ct - 2)
                if ct - 3 >= 0:
                    l1_stats(ch, ct - 3)
            per = [1, 1, 1, 1, 1, 1, 1, 1]
            ti = 0
            for k in range(8):
                l1_tile_g(ch, k)
                if k == 0:
                    l1_gs(ch, 7)
                    l1_conv(ch, 6)
                    l1_stats(ch, 5)
                elif k == 1:
                    l1_conv(ch, 7)
                    l1_stats(ch, 6)
                elif k == 2:
                    l1_stats(ch, 7)
                elif k == 3:
                    l1_ln_chain(ch)
                for f in tail[ti:ti + per[k]]:
                    f()
                ti += per[k]
            for f in tail[ti:]:
                f()
            l1_carry(ch)

        def s2_s3(ch, nx):
            E = l0_early(nx) if nx is not None else []
            Q = [lambda g=g: l0_tile_q(nx, g) for g in range(4)] if nx is not None else []
            GAl = [lambda g=g: l0_tile_ga(nx, g) for g in range(4)] if nx is not None else []
            GBl = [lambda g=g: l0_tile_gb(nx, g) for g in range(4)] if nx is not None else []
            if ch.halo:
                for f in E + Q + GAl + GBl:
                    f()
                return
            I = (E + Q + GAl + GBl) if nx is not None else []
            for ct in range(8):
                l1_ln_apply(ch, ct)
                for f in I[3 * ct:3 * ct + 3] if ct < 7 else I[21:]:
                    f()
            wo_load(1)
            for i in range(ch.nb):
                l1_out(ch, i)

        chunks = []
        bl = [[0, 1]]
        b = 2
        while b < n_own_blocks + 2:
            nb = min(NB, n_own_blocks + 2 - b)
            bl.append(list(range(b, b + nb)))
            b += nb
        for idx, blocks in enumerate(bl):
            ch = Chunk()
            ch.blocks, ch.nb, ch.ntok = blocks, len(blocks), 128 * len(blocks)
            ch.halo, ch.first_own = (idx == 0), (idx == 1)
            ch.gi, ch.ci, ch.hb, ch.pool_cur = {}, {}, {}, {}
            chunks.append(ch)

        x_load(chunks[0])
        for f in l0_norms(chunks[0]) + l0_early(chunks[0]) + l0_mid(chunks[0]):
            f()
        l0_late(chunks[0])
        for idx, ch in enumerate(chunks):
            nx = chunks[idx + 1] if idx + 1 < len(chunks) else None
            if nx is not None:
                x_load(nx)
            l1_front(ch, nx)
            s2_s3(ch, nx)
            if nx is not None:
                l0_late(nx)
        P.finish()
    return nc


def _perm_l0():
    q0, k0, v0, ga0, u0, gb0 = 0, 512, 640, 768, 1280, 1792
    cols = []
    for g in range(4):
        cols += list(range(q0 + g * 64, q0 + g * 64 + 64)) + list(range(q0 + (4 + g) * 64, q0 + (4 + g) * 64 + 64))
    cols += list(range(k0, k0 + 128))
    for g in range(4):
        cols += list(range(ga0 + g * 64, ga0 + g * 64 + 64)) + list(range(ga0 + (4 + g) * 64, ga0 + (4 + g) * 64 + 64))
    cols += list(range(u0, u0 + 512))
    cols += list(range(gb0, gb0 + 512))
    cols += list(range(v0, v0 + 128))
    rows = []
    for g in range(4):
        rows += list(range(g * 64, g * 64 + 64)) + list(range((4 + g) * 64, (4 + g) * 64 + 64))
    rows += list(range(512, 1024))
    return np.array(cols), np.array(rows)


def _btab(first_half):
    s = np.arange(128)[:, None]
    q = np.arange(128)[None, :]
    tabs = np.zeros((128, 6, 4, 128), np.float32)
    for kv in range(2):
        for g in range(4):
            h = kv * 4 + g
            slope = 2.0 ** (-(h + 1))
            cur = np.where(q >= s, -slope * (q - s).astype(np.float32), NEG)
            prev = np.where(s > q, -slope * (q + 128 - s).astype(np.float32), NEG)
            tabs[:, kv * 2 + 0, g, :] = prev
            tabs[:, kv * 2 + 1, g, :] = cur
            tabs[:, 4 + kv, g, :] = NEG if first_half else prev
    return np.ascontiguousarray(tabs.reshape(128, 6, 512))


def _rc0(first_half):
    rc = np.zeros((128, 4, 16), np.float32)
    for g in range(4):
        w = 2 << g
        t = np.arange(16)
        cntv = np.minimum(w, t + 1) if first_half else np.full(16, w)
        rc[:, g, :] = (1.0 / cntv.astype(np.float32))[None, :]
    return rc


def _tiles(w):
    n = w.shape[1] // 128
    return np.ascontiguousarray(w.reshape(8, 128, n, 128).transpose(2, 1, 0, 3).reshape(n, 128, 1024))


def _wo_halves(w):
    return np.ascontiguousarray(w.reshape(8, 128, 2, 512).transpose(2, 1, 0, 3).reshape(2, 128, 4096))


_NC_CACHE = {}


def kernel(x, pre_norm, post_norm, a_w_in, a_sinks, b_pool_w, b_pool_scale, ab_w_out,
           c_w_in, c_dw_w, c_dw_b, c_ln_g, c_ln_b, c_w_out):
    f = lambda v: np.asarray(v, dtype=np.float32)
    x, pre_norm, post_norm = f(x), f(pre_norm), f(post_norm)
    a_w_in, a_sinks, b_pool_w, b_pool_scale, ab_w_out = f(a_w_in), f(a_sinks), f(b_pool_w), f(b_pool_scale), f(ab_w_out)
    c_w_in, c_dw_w, c_dw_b, c_ln_g, c_ln_b, c_w_out = f(c_w_in), f(c_dw_w), f(c_dw_b), f(c_ln_g), f(c_ln_b), f(c_w_out)
    cols, rows = _perm_l0()
    wt = np.concatenate([_tiles(a_w_in[0][:, cols]), _tiles(c_w_in[0])], axis=0)
    wo = np.concatenate([_wo_halves(ab_w_out[0][rows, :]), _wo_halves(c_w_out[0])], axis=0)
    gpreT = np.ascontiguousarray(pre_norm.reshape(2, 8, 128).transpose(2, 0, 1))
    gpost = np.ascontiguousarray(np.broadcast_to(post_norm[None, :, :], (128, 2, D)))
    sk = a_sinks[0]
    sinkT = np.ascontiguousarray(np.repeat(sk, 128)[None, :])
    pscaleT = np.ascontiguousarray(b_pool_scale[0].reshape(4, 128).T)
    dwp = np.zeros((32, D), np.float32)
    dwp[:CONV_K] = c_dw_w[0][:, 0, :]
    wsh = np.ascontiguousarray(dwp.reshape(8, 4, 8, 4, 32).transpose(1, 4, 2, 3, 0).reshape(128, 8, 32))
    id4 = np.ascontiguousarray(np.tile(np.eye(32, dtype=np.float32), (4, 1)))
    cparT = np.ascontiguousarray(np.stack([c_dw_b[0], c_ln_g[0], c_ln_b[0]], axis=0).reshape(3, 8, 128).transpose(2, 0, 1))
    pw = np.ascontiguousarray(b_pool_w[0])
    in_maps = []
    for c in range(NCORES):
        b, hlf = c // 2, c % 2
        own = x[b, hlf * TOK_CORE:(hlf + 1) * TOK_CORE]
        halo = x[b, TOK_CORE - 256:TOK_CORE] if hlf == 1 else np.zeros((256, D), np.float32)
        xin = np.ascontiguousarray(np.concatenate([halo, own], axis=0))
        fh = (hlf == 0)
        in_maps.append({"xin": xin, "wt": wt, "wo": wo, "pool_w": pw, "gpreT": gpreT, "gpost": gpost, "sinkT": sinkT,
                        "pscaleT": pscaleT, "btab": _btab(fh), "rc0": _rc0(fh), "wsh": wsh, "id4": id4, "cparT": cparT})
    if "nc" not in _NC_CACHE:
        _NC_CACHE["nc"] = build_fused()
    res = run_bass_kernel_spmd(_NC_CACHE["nc"], in_maps, core_ids=list(range(NCORES)))
    out = np.zeros((4, 8192, D), np.float32)
    for c in range(NCORES):
        b, hlf = c // 2, c % 2
        out[b, hlf * TOK_CORE:(hlf + 1) * TOK_CORE] = res.results[c]["yout"]
    return out
```

```python
import numpy as np
from contextlib import ExitStack
import concourse.bass as bass
import concourse.mybir as mybir
from concourse.bass_utils import run_bass_kernel_spmd

F32 = mybir.dt.float32
BF16 = mybir.dt.bfloat16
AF = mybir.ActivationFunctionType
ALU = mybir.AluOpType

D = 1024
NCORES = 8
TOK_CORE = 4096
EPS = 1e-6
NEG = -1e30
CONV_K = 31


class Prog:
    def __init__(self, nc, stack):
        self.nc = nc
        self.eng = {"pe": nc.tensor, "act": nc.scalar, "dve": nc.vector, "pool": nc.gpsimd, "sp": nc.sync}
        self.sem = {}
        for n in ["pe", "act", "dve", "pool"]:
            self.sem[n] = stack.enter_context(nc.semaphore("s_" + n))
        self.ND = {"sp": 24, "pool": 16, "act": 2}
        for q, n in self.ND.items():
            for i in range(n):
                self.sem[("d", q, i)] = stack.enter_context(nc.semaphore("s_d%s%d" % (q, i)))
        self.dma_cnt = {"sp": 0, "pool": 0, "act": 0}
        self.cnt = {n: 0 for n in ["pe", "act", "dve", "pool"]}
        self.dma_i = 0
        self.waited = {n: {} for n in self.eng}
        self.res = {}
        self.out_events = []

    def _deps(self, reads, writes):
        deps = {}

        def add(ev):
            if ev is None:
                return
            k, v = ev
            if deps.get(k, 0) < v:
                deps[k] = v

        for r in reads:
            st = self.res.get(r)
            if st:
                add(st["w"])
        for w in writes:
            st = self.res.get(w)
            if st:
                add(st["w"])
                for k, v in st["r"].items():
                    add((k, v))
        return deps

    def _wait(self, en, deps):
        e = self.eng[en]
        for k, v in deps.items():
            if k == "pe" and en == "pe":
                continue
            if self.waited[en].get(k, 0) >= v:
                continue
            e.wait_ge(self.sem[k], v)
            self.waited[en][k] = v

    def _record(self, ev, reads, writes):
        k, v = ev
        for r in reads:
            st = self.res.setdefault(r, {"w": None, "r": {}})
            if st["r"].get(k, 0) < v:
                st["r"][k] = v
        for w in writes:
            self.res[w] = {"w": ev, "r": {}}

    def op(self, en, fn, reads=(), writes=(), inc=True):
        self._wait(en, self._deps(reads, writes))
        ins = fn(self.eng[en])
        ev = (en, self.cnt[en] + 1)
        if inc:
            self.cnt[en] += 1
            ins.then_inc(self.sem[en], 1)
        self._record(ev, reads, writes)
        return ins

    def dma(self, out, in_, reads=(), writes=(), queue="sp", is_output=False):
        n = self.dma_cnt[queue]
        self.dma_cnt[queue] += 1
        slot = n % self.ND[queue]
        k = ("d", queue, slot)
        prev = 16 * (n // self.ND[queue])
        deps = self._deps(reads, writes)
        if prev > 0 and deps.get(k, 0) < prev:
            deps[k] = prev
        self._wait(queue, deps)
        ins = self.eng[queue].dma_start(out=out, in_=in_)
        ins.then_inc(self.sem[k], 16)
        ev = (k, prev + 16)
        self.dma_i += 1
        self._record(ev, reads, writes)
        if is_output:
            self.out_events.append(ev)
        return ev

    def finish(self):
        deps = {}
        for k, v in self.out_events:
            deps[k] = max(deps.get(k, 0), v)
        self.waited["sp"] = {}
        self._wait("sp", deps)


NW = 6
NXS = 8
T_Q, T_K, T_GA, T_U, T_GB, T_V = 0, 4, 5, 9, 13, 17
T_A, T_B, T_G = 18, 26, 34
NTILE = 42


def build_fused(n_own_blocks=32, NB=4):
    nc = bass.Bass("TRN2", target_bir_lowering=False)
    NT = 128 * (n_own_blocks + 2)
    dr = lambda n, s: nc.dram_tensor(n, s, F32, kind="ExternalInput").ap()
    xin = dr("xin", [NT, D])
    wt_d = dr("wt", [NTILE, 128, 1024])
    wo_d = dr("wo", [4, 128, 4096])
    pool_w = dr("pool_w", [4, 128, 128])
    gpreT_d = dr("gpreT", [128, 2, 8])
    gpost_d = dr("gpost", [128, 2, D])
    sink_d = dr("sinkT", [1, 1024])
    pscale_d = dr("pscaleT", [128, 4])
    btab_d = dr("btab", [128, 6, 512])
    rc0_d = dr("rc0", [128, 4, 16])
    wsh_d = dr("wsh", [128, 8, 32])
    id4_d = dr("id4", [128, 32])
    cp_d = dr("cparT", [128, 3, 8])
    yout = nc.dram_tensor("yout", [NT - 256, D], F32, kind="ExternalOutput").ap()
    wt_bf = nc.dram_tensor("wt_bf", [NTILE, 128, 1024], BF16, kind="Internal").ap()
    wo_bf = nc.dram_tensor("wo_bf", [4, 128, 4096], BF16, kind="Internal").ap()

    with ExitStack() as stack:
        P = Prog(nc, stack)
        a = nc.alloc_sbuf_tensor
        NTC = 128 * NB
        HL = CONV_K - 1
        ident = a("ident", [128, 128], BF16)
        ones = a("ones", [128, 128], BF16)
        epsc = a("epsc", [128, 1], F32)
        st = a("stt", [128, 16], F32)
        btab = a("btab_s", [128, 6, 512], BF16)
        gpost = a("gpost_s", [128, 2, D], F32)
        gpreT = a("gpreT_s", [128, 2, 8], F32)
        ES = a("ES", [1, 2, 1024], BF16)
        eshi = ES[0:1, 0, :]
        eslo = ES[0:1, 1, :]
        pscale = a("pscale", [128, 4], F32)
        rc0 = a("rc0_s", [128, 4, 16], F32)
        wsh = a("wsh_s", [128, 8, 32], F32)
        id4 = a("id4_s", [128, 32], F32)
        DG2 = a("DG2", [128, 8, 32, 32], BF16)
        GS = a("GS", [128, 2, 4, NTC + 28], BF16)
        cp = a("cp", [128, 3, 8], F32)
        PW = a("PW", [128, 4, 128], BF16)
        X = a("X", [128, NXS, D], F32)
        Hbf = a("Hbf", [128, 2, D], BF16)
        HT = a("HT", [128, 2, 8, NTC], BF16)
        TMP = a("TMP", [128, D], F32)
        Wr = a("Wr", [128, NW, 8, 128], BF16)
        WOr = a("WOr", [128, 2, 8, 512], BF16)
        KT = a("KT", [128, 128 + NTC], BF16)
        VS = a("VS", [128, NB + 1, 128], BF16)
        U = a("U", [128, 4, 16 + NTC], F32)
        PT = [a("PT%d" % i, [128, 512], BF16) for i in range(12)]
        SA = a("SA", [128, 16 + NTC], F32)
        SB = a("SB", [128, 16 + NTC], F32)
        T16 = a("T16", [128, 16], F32)
        DS = a("DS", [128, 512], F32)
        YN = a("YN", [128, 512], F32)
        R1 = a("R1", [128, 8, NTC], BF16)
        R2 = a("R2", [128, 8, NTC], F32)
        R2v = R2[:].bitcast(BF16).rearrange("p a (b c) -> p (a b) c", c=NTC)
        GLU = a("GLU", [128, 8, HL + NTC], BF16)
        SIG = [a("SIG%d" % i, [128, NTC], F32) for i in range(2)]
        CFB = [a("CFB%d" % i, [128, NTC], BF16) for i in range(2)]
        SQB = [a("SQB%d" % i, [128, NTC], BF16) for i in range(2)]
        MU = a("MU", [128, NTC], F32)
        RS = a("RS", [128, NTC], F32)
        MSQ = SA
        SCN = [a("SCN%d" % i, [128, NTC], BF16) for i in range(2)]
        psT = nc.alloc_psum_tensor("psT", [128, 8, 128], BF16)
        psA = [nc.alloc_psum_tensor("psA%d" % i, [128, 512], F32) for i in range(2)]
        psB = [nc.alloc_psum_tensor("psB%d" % i, [128, 512], F32) for i in range(2)]
        psN = nc.alloc_psum_tensor("psN", [128, 512], F32)
        psD = nc.alloc_psum_tensor("psD", [128, 512], F32)
        psV = nc.alloc_psum_tensor("psV", [128, 512], F32)

        QTk = lambda g: ("R2", g)
        GAk = lambda g: ("R2", 4 + g)
        GBk = lambda g: ("R2", 8 + g)
        PLk = lambda g: ("GS", 0, 0, g)
        PLks = lambda g: [("GS", 0, r, g) for r in range(4)]
        YAk = lambda g, i: ("R1", g, i)
        YBk = lambda g: [("R1", 4 + g, i) for i in range(NB)]
        GATEk = lambda ct: [("R1", ct, i) for i in range(NB)]
        CFk = lambda ct: [("R2", 2 * ct), ("R2", 2 * ct + 1)]
        QT = lambda g: R2v[:, g, :]
        GA = lambda g: R2v[:, 4 + g, :]
        GB = lambda g: R2v[:, 8 + g, :]
        PL = lambda g: GS[:, 0, g, 0:NTC]
        YA = lambda g: R1[:, g, :]
        YB = lambda g: R1[:, 4 + g, :]
        GATE = lambda ct: R1[:, ct, :]
        CF = lambda ct: R2[:, ct, :]

        conv_t = lambda t: P.dma(wt_bf[t], wt_d[t], writes=[("wtbf", t)], queue="pool")
        P.op("pool", lambda e: e.memset(ident[:], 0.0), writes=["ident"])
        P.op("pool", lambda e: e.affine_select(out=ident[:], in_=ident[:], pattern=[[-1, 128]], compare_op=ALU.not_equal,
                                                fill=1.0, base=0, channel_multiplier=1), reads=["ident"], writes=["ident"])
        P.op("pool", lambda e: e.memset(ones[:], 1.0), writes=["ones"])
        P.op("pool", lambda e: e.memset(epsc[:], EPS), writes=["epsc"])
        P.dma(gpreT[:], gpreT_d, writes=["gpreT"])
        P.dma(gpost[:], gpost_d, writes=["gpost"])
        P.dma(TMP[0:1, :], sink_d, writes=[("TMP", 0), ("TMP", 1)])
        P.dma(pscale[:], pscale_d, writes=["pscale"])
        P.dma(rc0[:], rc0_d, writes=["rc0"])
        P.dma(wsh[:], wsh_d, writes=["wsh"])
        P.dma(id4[:], id4_d, writes=["id4"])
        P.dma(cp[:], cp_d, writes=["cp"])
        for t in [T_K, T_V] + [T_U + g for g in range(4)]:
            conv_t(t)
        P.op("pool", lambda e: e.memset(U[:], 0.0), writes=[("U", g) for g in range(4)])
        P.op("pool", lambda e: e.memset(KT[:, 0:128], 0.0), writes=["KTh"])
        P.op("pool", lambda e: e.memset(VS[:, 0, :], 0.0), writes=[("VS", 0)])
        for t in [T_Q + g for g in range(4)]:
            conv_t(t)
        for t in range(6):
            P.dma(btab[:, t, :], btab_d[:, t, :], writes=["btab"], queue="pool")
        P.dma(PW[:], pool_w.rearrange("g c d -> c g d"), writes=["PW"], queue="pool")
        for t in range(6):
            P.op("act", lambda e, t=t: e.activation(out=btab[:, t, :], in_=btab[:, t, :], func=AF.Exp), reads=["btab"], writes=["btab"])
        for t in [T_GA + g for g in range(4)] + [T_GB + g for g in range(4)]:
            conv_t(t)
        for h in range(2):
            P.dma(wo_bf[h], wo_d[h], writes=[("wobf", h)], queue="pool")
        for ct in range(2):
            for t in (T_B + ct, T_A + ct, T_G + ct):
                conv_t(t)
        for ct in range(2, 8):
            for t in (T_B + ct, T_A + ct, T_G + ct):
                conv_t(t)
        for h in range(2, 4):
            P.dma(wo_bf[h], wo_d[h], writes=[("wobf", h)], queue="pool")
        tk = [("TMP", 0), ("TMP", 1)]
        P.op("act", lambda e: e.activation(out=TMP[0:1, :], in_=TMP[0:1, :], func=AF.Exp), reads=tk, writes=tk)
        P.op("dve", lambda e: e.tensor_copy(out=eshi, in_=TMP[0:1, :]), reads=tk, writes=["eshi"])
        P.op("dve", lambda e: e.tensor_tensor(out=TMP[0:1, :], in0=TMP[0:1, :], in1=eshi, op=ALU.subtract), reads=tk + ["eshi"], writes=tk)
        P.op("dve", lambda e: e.tensor_copy(out=eslo, in_=TMP[0:1, :]), reads=tk, writes=["eslo"])

        ctr = {"x": 0, "hb": 0, "psA": 0, "w": 0, "sc": 0, "sig": 0, "dg": 0, "pc": 0, "cfb": 0, "scn": 0}

        def nxt(k, m):
            v = ctr[k] % m
            ctr[k] += 1
            return v

        def wtile(t):
            s = nxt("w", NW)
            P.dma(Wr[:, s].rearrange("p a b -> p (a b)"), wt_bf[t], reads=[("wtbf", t)], writes=[("Wr", s)])
            return s

        def wo_load(l):
            for h in range(2):
                P.dma(WOr[:, h].rearrange("p a b -> p (a b)"), wo_bf[l * 2 + h], reads=[("wobf", l * 2 + h)], writes=[("WOr", h)])

        class Chunk:
            pass

        def norm_a(ch, i, l):
            s = ch.slots[i]
            hb = nxt("hb", 2)
            ch.hb[(i, l)] = hb
            P.op("act", lambda e: e.activation(out=Hbf[:, hb, :], in_=X[:, s, :], func=AF.Square, accum_out=st[:, 0:1]),
                 reads=[("X", s)], writes=[("Hbf", hb), "st0"])
            P.op("act", lambda e: e.activation(out=st[:, 1:2], in_=st[:, 0:1], func=AF.Ln, bias=epsc[:], scale=1.0 / D),
                 reads=["st0", "epsc"], writes=["st1"])
            P.op("act", lambda e: e.activation(out=st[:, 2:3], in_=st[:, 1:2], func=AF.Exp, scale=-0.5), reads=["st1"], writes=["st2"])
            P.op("dve", lambda e: e.tensor_scalar(out=Hbf[:, hb, :], in0=X[:, s, :], scalar1=st[:, 2:3], scalar2=None, op0=ALU.mult),
                 reads=[("X", s), "st2"], writes=[("Hbf", hb)])

        def norm_b(ch, i, l):
            hb = ch.hb[(i, l)]
            for kt in range(8):
                P.op("pe", lambda e, kt=kt: e.transpose(psT[:, kt, :], Hbf[:, hb, kt * 128:(kt + 1) * 128], ident[:]),
                     reads=[("Hbf", hb), "ident"], writes=["psT"], inc=(kt == 7))
            gb = gpreT[:, l, :].unsqueeze(2).broadcast_to([128, 8, 128])
            P.op("dve", lambda e: e.tensor_tensor(out=HT[:, l, :, i * 128:(i + 1) * 128], in0=psT[:], in1=gb, op=ALU.mult),
                 reads=["psT", "gpreT"], writes=[("HT", l)])

        pref = {}

        def inproj(t, ntok, l):
            ws = pref.pop(t) if t in pref else wtile(t)
            bi = nxt("psA", 2)
            for kt in range(8):
                P.op("pe", lambda e, kt=kt: e.matmul(psA[bi][:, 0:ntok], lhsT=Wr[:, ws, kt, :], rhs=HT[:, l, kt, 0:ntok],
                                                     start=(kt == 0), stop=(kt == 7)),
                     reads=[("Wr", ws), ("HT", l)], writes=[("psA", bi)], inc=(kt == 7))
            return bi

        def outproj_post(s, lhs_fn, lhs_reads, l, out_ap, bank_list, bank_key):
            for half in range(2):
                for kt in range(8):
                    P.op("pe", lambda e, kt=kt, half=half: e.matmul(bank_list[half][:], lhsT=lhs_fn(kt), rhs=WOr[:, half, kt, :],
                                                                   start=(kt == 0), stop=(kt == 7)),
                         reads=list(lhs_reads) + [("WOr", half)], writes=[(bank_key, half)], inc=(kt == 7))
            for half in range(2):
                P.op("act", lambda e, half=half: e.activation(out=TMP[:, half * 512:(half + 1) * 512], in_=bank_list[half][:], func=AF.Square,
                                                             accum_out=st[:, 4 + half:5 + half]),
                     reads=[(bank_key, half)], writes=[("TMP", half), ("st4", half)])
            P.op("dve", lambda e: e.tensor_tensor(out=st[:, 6:7], in0=st[:, 4:5], in1=st[:, 5:6], op=ALU.add),
                 reads=[("st4", 0), ("st4", 1)], writes=["st6"])
            P.op("act", lambda e: e.activation(out=st[:, 7:8], in_=st[:, 6:7], func=AF.Ln, bias=epsc[:], scale=1.0 / D),
                 reads=["st6", "epsc"], writes=["st7"])
            P.op("act", lambda e: e.activation(out=st[:, 8:9], in_=st[:, 7:8], func=AF.Exp, scale=-0.5), reads=["st7"], writes=["st8"])
            for half in range(2):
                hs = slice(half * 512, (half + 1) * 512)
                P.op("dve", lambda e, half=half, hs=hs: e.tensor_tensor(out=TMP[:, hs], in0=bank_list[half][:], in1=gpost[:, l, hs], op=ALU.mult),
                     reads=[(bank_key, half), "gpost"], writes=[("TMP", half)])
                P.op("dve", lambda e, hs=hs: e.scalar_tensor_tensor(out=X[:, s, hs], in0=TMP[:, hs], scalar=st[:, 8:9], in1=X[:, s, hs],
                                                                    op0=ALU.mult, op1=ALU.add),
                     reads=[("TMP", half), "st8", ("X", s)], writes=[("X", s)])
            if out_ap is not None:
                P.dma(out_ap, X[:, s, :], reads=[("X", s)], queue="pool", is_output=True)

        def x_load(ch):
            ch.slots = []
            for b in ch.blocks:
                s = nxt("x", NXS)
                P.dma(X[:, s, :], xin[b * 128:(b + 1) * 128, :], writes=[("X", s)])
                ch.slots.append(s)

        def l0_tile_q(ch, g):
            bi = inproj(T_Q + g, ch.ntok, 0)
            P.op("act", lambda e: e.activation(out=QT(g)[:, 0:ch.ntok], in_=psA[bi][:, 0:ch.ntok], func=AF.Copy, scale=0.125),
                 reads=[("psA", bi)], writes=[QTk(g)])

        def l0_tile_k(ch):
            bi = inproj(T_K, ch.ntok, 0)
            P.op("act", lambda e: e.activation(out=KT[:, 128:128 + ch.ntok], in_=psA[bi][:, 0:ch.ntok], func=AF.Copy), reads=[("psA", bi)], writes=["KTn"])

        def l0_tile_ga(ch, g):
            bi = inproj(T_GA + g, ch.ntok, 0)
            P.op("act", lambda e: e.activation(out=GA(g)[:, 0:ch.ntok], in_=psA[bi][:, 0:ch.ntok], func=AF.Silu),
                 reads=[("psA", bi)], writes=[GAk(g)])

        def l0_tile_gb(ch, g):
            bi = inproj(T_GB + g, ch.ntok, 0)
            P.op("act", lambda e: e.activation(out=GB(g)[:, 0:ch.ntok], in_=psA[bi][:, 0:ch.ntok], func=AF.Silu),
                 reads=[("psA", bi)], writes=[GBk(g)])

        def l0_tile_u(ch, g):
            bi = inproj(T_U + g, ch.ntok, 0)
            P.op("act", lambda e: e.activation(out=U[:, g, 16:16 + ch.ntok], in_=psA[bi][:, 0:ch.ntok], func=AF.Copy),
                 reads=[("psA", bi)], writes=[("U", g)])

        def l0_tile_v(ch):
            ws = wtile(T_V)
            for i in range(ch.nb):
                for kt in range(8):
                    P.op("pe", lambda e, kt=kt, i=i: e.matmul(psV[:, 0:128], lhsT=HT[:, 0, kt, i * 128:(i + 1) * 128], rhs=Wr[:, ws, kt, :],
                                                              start=(kt == 0), stop=(kt == 7)),
                         reads=[("Wr", ws), ("HT", 0)], writes=["psV"], inc=(kt == 7))
                P.op("act", lambda e, i=i: e.activation(out=VS[:, i + 1, :], in_=psV[:, 0:128], func=AF.Copy), reads=["psV"], writes=[("VS", i + 1)])

        def l0_scores(ch, i):
            tb = slice(i * 128, (i + 1) * 128)
            for c in range(2):
                kcols = slice(i * 128 + c * 128, i * 128 + c * 128 + 128)
                pis = [nxt("sc", 2), nxt("sc", 2)]
                for g in range(4):
                    for kv in range(2):
                        pi = pis[kv]
                        pr = slice(kv * 64, kv * 64 + 64)
                        P.op("pe", lambda e, g=g, pi=pi, pr=pr: e.matmul(psB[pi][:, g * 128:(g + 1) * 128], lhsT=KT[pr, kcols], rhs=QT(g)[pr, tb],
                                                                        start=True, stop=True),
                             reads=["KTn", "KTh", QTk(g)], writes=[("psB", pi)], inc=(g == 3))
                for kv in range(2):
                    pi = pis[kv]
                    tab = kv * 2 + c
                    if c == 0 and ch.first_own and i == 0:
                        tab = 4 + kv
                    u = (i % 3) * 4 + kv * 2 + c
                    P.op("act", lambda e, u=u, pi=pi: e.activation(out=PT[u][:], in_=psB[pi][:], func=AF.Exp), reads=[("psB", pi)], writes=[("PT", u)])
                    P.op("pool", lambda e, u=u, tab=tab: e.tensor_tensor(out=PT[u][:], in0=PT[u][:], in1=btab[:, tab, :], op=ALU.mult),
                         reads=[("PT", u), "btab"], writes=[("PT", u)])

        def l0_pv(ch, i):
            tb = slice(i * 128, (i + 1) * 128)
            for kv in range(2):
                for c in range(2):
                    u = (i % 3) * 4 + kv * 2 + c
                    P.op("pe", lambda e, kv=kv, c=c, u=u: e.matmul(psD[kv * 64:kv * 64 + 64, :], lhsT=ones[:, 0:64], rhs=PT[u][:],
                                                                  start=(c == 0), stop=False, tile_position=(0, kv * 64)),
                         reads=["ones", ("PT", u)], writes=["psD"], inc=False)
                P.op("pe", lambda e, kv=kv: e.matmul(psD[kv * 64:kv * 64 + 64, :], lhsT=ones[0:1, 0:64], rhs=eshi[:, kv * 512:(kv + 1) * 512],
                                                     start=False, stop=False, tile_position=(0, kv * 64)),
                     reads=["ones", "eshi"], writes=["psD"], inc=False)
                P.op("pe", lambda e, kv=kv: e.matmul(psD[kv * 64:kv * 64 + 64, :], lhsT=ones[0:1, 0:64], rhs=eslo[:, kv * 512:(kv + 1) * 512],
                                                     start=False, stop=True, tile_position=(0, kv * 64)),
                     reads=["ones", "eslo"], writes=["psD"], inc=(kv == 1))
            for kv in range(2):
                for c in range(2):
                    u = (i % 3) * 4 + kv * 2 + c
                    P.op("pe", lambda e, kv=kv, c=c, u=u: e.matmul(psN[kv * 64:kv * 64 + 64, :], lhsT=VS[:, i + c, kv * 64:kv * 64 + 64], rhs=PT[u][:],
                                                                  start=(c == 0), stop=(c == 1), tile_position=(0, kv * 64)),
                         reads=[("VS", i + c), ("PT", u)], writes=["psN"], inc=(kv == 1 and c == 1))
            P.op("act", lambda e: e.activation(out=DS[:], in_=psD[:], func=AF.Ln), reads=["psD"], writes=["DS"])
            P.op("act", lambda e: e.activation(out=DS[:], in_=DS[:], func=AF.Exp, scale=-1.0), reads=["DS"], writes=["DS"])
            P.op("dve", lambda e: e.tensor_tensor(out=YN[:], in0=psN[:], in1=DS[:], op=ALU.mult), reads=["psN", "DS"], writes=["YN"])
            for g in range(4):
                P.op("pool", lambda e, g=g: e.tensor_tensor(out=YA(g)[:, tb], in0=YN[:, g * 128:(g + 1) * 128], in1=GA(g)[:, tb], op=ALU.mult),
                     reads=["YN", GAk(g)], writes=[YAk(g, i)])

        def l0_pool_pre(ch, g):
            ntok = ch.ntok
            w = 2 << g
            E = 16 + ntok
            cur = None
            step = 1
            bufs = [SA, SB]
            k = 0
            while step < w:
                lo = 2 * step - 1
                dst = bufs[k % 2]
                srcap = U[:, g, :] if cur is None else cur[:]
                rk = [("U", g)] if cur is None else [("SAB", (k - 1) % 2)]
                P.op("pool", lambda e, dst=dst, srcap=srcap, lo=lo, step=step: e.tensor_tensor(
                    out=dst[:, lo:E], in0=srcap[:, lo:E], in1=srcap[:, lo - step:E - step], op=ALU.add),
                     reads=rk, writes=[("SAB", k % 2)])
                cur = dst
                k += 1
                step *= 2
            ch.pool_cur[g] = (cur, ("SAB", (k - 1) % 2))

        def l0_pool_fin(ch, g):
            ntok = ch.ntok
            w = 2 << g
            E = 16 + ntok
            cur, lastk = ch.pool_cur[g]
            P.op("dve", lambda e: e.scalar_tensor_tensor(out=PL(g)[:, 0:ntok], in0=cur[:, 16:E], scalar=1.0 / w, in1=U[:, g, 16:E],
                                                         op0=ALU.mult, op1=ALU.subtract),
                 reads=[lastk, ("U", g)], writes=PLks(g))
            if ch.first_own:
                P.op("dve", lambda e: e.tensor_tensor(out=T16[:], in0=cur[:, 16:32], in1=rc0[:, g, :], op=ALU.mult),
                     reads=[lastk, "rc0"], writes=["T16"])
                P.op("dve", lambda e: e.tensor_tensor(out=PL(g)[:, 0:16], in0=T16[:], in1=U[:, g, 16:32], op=ALU.subtract),
                     reads=["T16", ("U", g)] + PLks(g), writes=PLks(g))

        def l0_pool_mm(ch, g):
            ntok = ch.ntok
            bk, bkey = [(psV, "psV"), (psA[0], ("psA", 0)), (psA[1], ("psA", 1)), (psV, "psV")][g]
            P.op("pe", lambda e: e.matmul(bk[:, 0:ntok], lhsT=PW[:, g, :], rhs=PL(g)[:, 0:ntok], start=True, stop=True),
                 reads=["PW"] + PLks(g), writes=[bkey])
            P.op("dve", lambda e: e.scalar_tensor_tensor(out=YB(g)[:, 0:ntok], in0=bk[:, 0:ntok], scalar=pscale[:, g:g + 1],
                                                         in1=GB(g)[:, 0:ntok], op0=ALU.mult, op1=ALU.mult),
                 reads=[bkey, "pscale", GBk(g)], writes=YBk(g))

        def l0_out(ch, i):
            tb = slice(i * 128, (i + 1) * 128)
            lhs = lambda kt: (YA(kt)[:, tb] if kt < 4 else YB(kt - 4)[:, tb])
            bl_, bk_ = (psA, "psA") if i % 2 == 0 else (psB, "psB")
            outproj_post(ch.slots[i], lhs, [YAk(g, i) for g in range(4)] + [("R1", 4 + g, i) for g in range(4)], 0, None, bl_, bk_)

        def l0_carry(ch):
            ntok, nb = ch.ntok, ch.nb
            P.op("pool", lambda e: e.tensor_copy(out=KT[:, 0:128], in_=KT[:, ntok:ntok + 128]), reads=["KTn"], writes=["KTh"])
            P.op("pool", lambda e: e.tensor_copy(out=VS[:, 0, :], in_=VS[:, nb, :]), reads=[("VS", nb)], writes=[("VS", 0)])
            for g in range(4):
                P.op("pool", lambda e, g=g: e.tensor_copy(out=U[:, g, 0:16], in_=U[:, g, ntok:ntok + 16]), reads=[("U", g)], writes=[("U", g)])

        def l1_tile_ba(ch, ct):
            ntok = ch.ntok
            si = nxt("sig", 2)
            bi = inproj(T_B + ct, ntok, 1)
            P.op("act", lambda e: e.activation(out=SIG[si][:, 0:ntok], in_=psA[bi][:, 0:ntok], func=AF.Tanh, scale=0.5),
                 reads=[("psA", bi)], writes=[("SIG", si)])
            bi2 = inproj(T_A + ct, ntok, 1)
            P.op("dve", lambda e: e.scalar_tensor_tensor(out=GLU[:, ct, HL:HL + ntok], in0=SIG[si][:, 0:ntok], scalar=1.0,
                                                         in1=psA[bi2][:, 0:ntok], op0=ALU.add, op1=ALU.mult),
                 reads=[("psA", bi2), ("SIG", si)], writes=[("GLU", ct)])

        def l1_tile_g(ch, ct):
            bi3 = inproj(T_G + ct, ch.ntok, 1)
            P.op("act", lambda e: e.activation(out=GATE(ct)[:, 0:ch.ntok], in_=psA[bi3][:, 0:ch.ntok], func=AF.Silu),
                 reads=[("psA", bi3)], writes=GATEk(ct))

        def l1_gs(ch, ct):
            ntok = ch.ntok
            gi = nxt("dg", 2)
            ch.gi[ct] = gi
            for r in range(4):
                Lr = ntok + 28 - (1 if r == 3 else 0)
                for cg in range(4):
                    P.op("dve", lambda e, r=r, cg=cg, Lr=Lr: e.tensor_copy(out=GS[r * 32:(r + 1) * 32, gi, cg, 0:Lr],
                                                                          in_=GLU[cg * 32:(cg + 1) * 32, ct, r:r + Lr]),
                         reads=[("GLU", ct)], writes=[("GS", gi, r, cg)], inc=(r == 3 and cg == 3))

        def l1_conv(ch, ct):
            ntok = ch.ntok
            gi = ch.gi[ct]
            pi = nxt("pc", 2)
            for tg in range(8):
                for cg in range(4):
                    P.op("pe", lambda e, tg=tg, cg=cg: e.matmul(psB[pi][cg * 32:(cg + 1) * 32, 0:ntok], lhsT=DG2[:, ct, cg * 8 + tg, :],
                                                                rhs=GS[:, gi, cg, 4 * tg:4 * tg + ntok], start=(tg == 0), stop=(tg == 7),
                                                                tile_position=(0, cg * 32)),
                         reads=["DG2"] + [("GS", gi, r, cg) for r in range(4)], writes=[("psB", pi)], inc=(tg == 7 and cg == 3))
            ci = nxt("cfb", 2)
            ch.ci[ct] = ci
            P.op("act", lambda e: e.activation(out=CFB[ci][:, 0:ntok], in_=psB[pi][:, 0:ntok], func=AF.Identity,
                                               bias=cp[:, 0, ct:ct + 1], scale=0.5), reads=[("psB", pi), "cp"], writes=[("CFB", ci)])
            P.op("act", lambda e: e.activation(out=SQB[ci][:, 0:ntok], in_=psB[pi][:, 0:ntok], func=AF.Square,
                                               bias=cp[:, 0, ct:ct + 1], scale=0.5), reads=[("psB", pi), "cp"], writes=[("SQB", ci)])
            P.op("act", lambda e: e.activation(out=CF(ct)[:, 0:ntok], in_=psB[pi][:, 0:ntok], func=AF.Identity,
                                               bias=cp[:, 0, ct:ct + 1], scale=0.5),
                 reads=[("psB", pi), "cp"], writes=CFk(ct))

        def l1_stats(ch, ct):
            ntok = ch.ntok
            ci = ch.ci[ct]
            P.op("pe", lambda e: e.matmul(psN[:, 0:ntok], lhsT=ones[:], rhs=CFB[ci][:, 0:ntok], start=(ct == 0), stop=(ct == 7)),
                 reads=["ones", ("CFB", ci)], writes=["psN"])
            P.op("pe", lambda e: e.matmul(psD[:, 0:ntok], lhsT=ones[:], rhs=SQB[ci][:, 0:ntok], start=(ct == 0), stop=(ct == 7)),
                 reads=["ones", ("SQB", ci)], writes=["psD"])

        def l1_carry(ch):
            for ct in range(8):
                P.op("pool", lambda e, ct=ct: e.tensor_copy(out=GLU[:, ct, 0:HL], in_=GLU[:, ct, ch.ntok:ch.ntok + HL]),
                     reads=[("GLU", ct)], writes=[("GLU", ct)])

        def l1_ln_chain(ch):
            ntok = ch.ntok
            P.op("act", lambda e: e.activation(out=MU[:, 0:ntok], in_=psN[:, 0:ntok], func=AF.Copy, scale=1.0 / D), reads=["psN"], writes=["MU"])
            P.op("dve", lambda e: e.tensor_tensor(out=MSQ[:, 0:ntok], in0=MU[:, 0:ntok], in1=MU[:, 0:ntok], op=ALU.mult), reads=["MU"], writes=[("SAB", 0)])
            P.op("dve", lambda e: e.scalar_tensor_tensor(out=RS[:, 0:ntok], in0=psD[:, 0:ntok], scalar=1.0 / D, in1=MSQ[:, 0:ntok],
                                                         op0=ALU.mult, op1=ALU.subtract), reads=["psD", ("SAB", 0)], writes=["RS"])
            P.op("act", lambda e: e.activation(out=RS[:, 0:ntok], in_=RS[:, 0:ntok], func=AF.Ln, bias=epsc[:], scale=1.0),
                 reads=["RS", "epsc"], writes=["RS"])
            P.op("act", lambda e: e.activation(out=RS[:, 0:ntok], in_=RS[:, 0:ntok], func=AF.Exp, scale=-0.5), reads=["RS"], writes=["RS"])

        def l1_ln_apply(ch, ct):
            ntok = ch.ntok
            P.op("dve", lambda e: e.tensor_tensor(out=CF(ct)[:, 0:ntok], in0=CF(ct)[:, 0:ntok], in1=MU[:, 0:ntok], op=ALU.subtract),
                 reads=CFk(ct) + ["MU"], writes=CFk(ct))
            P.op("dve", lambda e: e.tensor_tensor(out=CF(ct)[:, 0:ntok], in0=CF(ct)[:, 0:ntok], in1=RS[:, 0:ntok], op=ALU.mult),
                 reads=CFk(ct) + ["RS"], writes=CFk(ct))
            si = nxt("scn", 2)
            P.op("act", lambda e: e.activation(out=SCN[si][:, 0:ntok], in_=CF(ct)[:, 0:ntok], func=AF.Silu,
                                               bias=cp[:, 2, ct:ct + 1], scale=cp[:, 1, ct:ct + 1]),
                 reads=CFk(ct) + ["cp"], writes=[("SCN", si)])
            P.op("pool", lambda e: e.tensor_tensor(out=GATE(ct)[:, 0:ntok], in0=GATE(ct)[:, 0:ntok], in1=SCN[si][:, 0:ntok], op=ALU.mult),
                 reads=GATEk(ct) + [("SCN", si)], writes=GATEk(ct))

        def l1_out(ch, i):
            tb = slice(i * 128, (i + 1) * 128)
            ob = ch.blocks[i] - 2
            outproj_post(ch.slots[i], lambda kt: GATE(kt)[:, tb], [("R1", ct, i) for ct in range(8)], 1,
                         yout[ob * 128:(ob + 1) * 128, :], *((psB, "psB") if i % 2 == 0 else (psA, "psA")))

        def weave(A, B):
            na, nb_ = len(A), len(B)
            ia = ib = 0
            while ia < na or ib < nb_:
                if ia < na and (ib >= nb_ or ia * max(nb_, 1) <= ib * max(na, 1)):
                    A[ia]()
                    ia += 1
                else:
                    B[ib]()
                    ib += 1

        def l0_norms(ch):
            L = []
            for i in range(ch.nb + 1):
                if i < ch.nb:
                    L.append(lambda i=i: norm_a(ch, i, 0))
                if i >= 1:
                    L.append(lambda i=i: norm_b(ch, i - 1, 0))
            return L

        def l0_early(ch):
            L = [lambda: l0_tile_k(ch), lambda: l0_tile_v(ch)]
            for g in range(4):
                L.append(lambda g=g: l0_tile_u(ch, g))
                L.append(lambda g=g: (l0_pool_fin(ch, g - 1) if g >= 1 else None, l0_pool_pre(ch, g)))
            return L

        def l0_mid(ch):
            L = [lambda g=g: l0_tile_q(ch, g) for g in range(4)]
            L += [lambda g=g: l0_tile_ga(ch, g) for g in range(4)]
            L += [lambda g=g: l0_tile_gb(ch, g) for g in range(4)]
            return L

        def l0_late(ch):
            nb = ch.nb
            wo_load(0)
            l0_pool_fin(ch, 3)
            l0_scores(ch, 0)
            if nb > 1:
                l0_scores(ch, 1)
            outs = []

            def do_out(j):
                l0_out(ch, j)
                norm_a(ch, j, 1)
                if j >= 1:
                    norm_b(ch, j - 1, 1)

            nout = 0
            for i in range(nb):
                if i + 2 < nb:
                    l0_scores(ch, i + 2)
                l0_pv(ch, i)
                if i == min(1, nb - 1):
                    for g in range(4):
                        l0_pool_mm(ch, g)
                if i >= 1:
                    do_out(nout)
                    nout += 1
            while nout < nb:
                do_out(nout)
                nout += 1
            norm_b(ch, nb - 1, 1)
            l0_carry(ch)

        def l1_front(ch, nx):
            tail = l0_norms(nx) if nx is not None else []
            if ch.halo:
                for ct in range(8):
                    l1_tile_ba(ch, ct)
                for f in tail:
                    f()
                l1_carry(ch)
                return
            for ct in range(8):
                l1_tile_ba(ch, ct)
                if ct - 1 >= 0:
                    l1_gs(ch, ct - 1)
                if ct - 2 >= 0:
                    l1_conv(ch, ct - 2)
                if ct - 3 >= 0:
                    l1_stats(ch, ct - 3)
            per = [1, 1, 1, 1, 1, 1, 1, 1]
            ti = 0
            for k in range(8):
                l1_tile_g(ch, k)
                if k == 0:
                    l1_gs(ch, 7)
                    l1_conv(ch, 6)
                    l1_stats(ch, 5)
                elif k == 1:
                    l1_conv(ch, 7)
                    l1_stats(ch, 6)
                elif k == 2:
                    l1_stats(ch, 7)
                elif k == 3:
                    l1_ln_chain(ch)
                for f in tail[ti:ti + per[k]]:
                    f()
                ti += per[k]
            for f in tail[ti:]:
                f()
            l1_carry(ch)

        def s2_s3(ch, nx):
            E = l0_early(nx) if nx is not None else []
            Q = [lambda g=g: l0_tile_q(nx, g) for g in range(4)] if nx is not None else []
            GAl = [lambda g=g: l0_tile_ga(nx, g) for g in range(4)] if nx is not None else []
            GBl = [lambda g=g: l0_tile_gb(nx, g) for g in range(4)] if nx is not None else []
            if ch.halo:
                for f in E + Q + GAl + GBl:
                    f()
                return
            I = (E + Q + GAl + GBl) if nx is not None else []
            for ct in range(8):
                l1_ln_apply(ch, ct)
                for f in I[3 * ct:3 * ct + 3] if ct < 7 else I[21:]:
                    f()
            wo_load(1)
            for i in range(ch.nb):
                l1_out(ch, i)

        chunks = []
        bl = [[0, 1]]
        b = 2
        while b < n_own_blocks + 2:
            nb = min(NB, n_own_blocks + 2 - b)
            bl.append(list(range(b, b + nb)))
            b += nb
        for idx, blocks in enumerate(bl):
            ch = Chunk()
            ch.blocks, ch.nb, ch.ntok = blocks, len(blocks), 128 * len(blocks)
            ch.halo, ch.first_own = (idx == 0), (idx == 1)
            ch.gi, ch.ci, ch.hb, ch.pool_cur = {}, {}, {}, {}
            chunks.append(ch)

        def late_consts():
            P.op("pool", lambda e: e.memset(GLU[:], 0.0), writes=[("GLU", ct) for ct in range(8)])
            P.op("pool", lambda e: e.memset(GS[:], 0.0), writes=[("GS", gi, r, cg) for gi in range(2) for r in range(4) for cg in range(4)])
            for ct in range(8):
                P.op("pool", lambda e, ct=ct: e.tensor_tensor(out=DG2[:, ct], in0=wsh[:, ct, :].unsqueeze(2).broadcast_to([128, 32, 32]),
                                                             in1=id4[:].unsqueeze(1).broadcast_to([128, 32, 32]), op=ALU.mult),
                     reads=["wsh", "id4"], writes=["DG2"])

        x_load(chunks[0])
        for f in l0_norms(chunks[0]) + l0_early(chunks[0]) + l0_mid(chunks[0]):
            f()
        l0_late(chunks[0])
        late_consts()
        for idx, ch in enumerate(chunks):
            nx = chunks[idx + 1] if idx + 1 < len(chunks) else None
            if nx is not None:
                x_load(nx)
            l1_front(ch, nx)
            s2_s3(ch, nx)
            if nx is not None:
                l0_late(nx)
        P.finish()
    return nc


def _perm_l0():
    q0, k0, v0, ga0, u0, gb0 = 0, 512, 640, 768, 1280, 1792
    cols = []
    for g in range(4):
        cols += list(range(q0 + g * 64, q0 + g * 64 + 64)) + list(range(q0 + (4 + g) * 64, q0 + (4 + g) * 64 + 64))
    cols += list(range(k0, k0 + 128))
    for g in range(4):
        cols += list(range(ga0 + g * 64, ga0 + g * 64 + 64)) + list(range(ga0 + (4 + g) * 64, ga0 + (4 + g) * 64 + 64))
    cols += list(range(u0, u0 + 512))
    cols += list(range(gb0, gb0 + 512))
    cols += list(range(v0, v0 + 128))
    rows = []
    for g in range(4):
        rows += list(range(g * 64, g * 64 + 64)) + list(range((4 + g) * 64, (4 + g) * 64 + 64))
    rows += list(range(512, 1024))
    return np.array(cols), np.array(rows)


def _btab(first_half):
    s = np.arange(128)[:, None]
    q = np.arange(128)[None, :]
    tabs = np.zeros((128, 6, 4, 128), np.float32)
    for kv in range(2):
        for g in range(4):
            h = kv * 4 + g
            slope = 2.0 ** (-(h + 1))
            cur = np.where(q >= s, -slope * (q - s).astype(np.float32), NEG)
            prev = np.where(s > q, -slope * (q + 128 - s).astype(np.float32), NEG)
            tabs[:, kv * 2 + 0, g, :] = prev
            tabs[:, kv * 2 + 1, g, :] = cur
            tabs[:, 4 + kv, g, :] = NEG if first_half else prev
    return np.ascontiguousarray(tabs.reshape(128, 6, 512))


def _rc0(first_half):
    rc = np.zeros((128, 4, 16), np.float32)
    for g in range(4):
        w = 2 << g
        t = np.arange(16)
        cntv = np.minimum(w, t + 1) if first_half else np.full(16, w)
        rc[:, g, :] = (1.0 / cntv.astype(np.float32))[None, :]
    return rc


def _tiles(w):
    n = w.shape[1] // 128
    return np.ascontiguousarray(w.reshape(8, 128, n, 128).transpose(2, 1, 0, 3).reshape(n, 128, 1024))


def _wo_halves(w):
    return np.ascontiguousarray(w.reshape(8, 128, 2, 512).transpose(2, 1, 0, 3).reshape(2, 128, 4096))


_NC_CACHE = {}


def kernel(x, pre_norm, post_norm, a_w_in, a_sinks, b_pool_w, b_pool_scale, ab_w_out,
           c_w_in, c_dw_w, c_dw_b, c_ln_g, c_ln_b, c_w_out):
    f = lambda v: np.asarray(v, dtype=np.float32)
    x, pre_norm, post_norm = f(x), f(pre_norm), f(post_norm)
    a_w_in, a_sinks, b_pool_w, b_pool_scale, ab_w_out = f(a_w_in), f(a_sinks), f(b_pool_w), f(b_pool_scale), f(ab_w_out)
    c_w_in, c_dw_w, c_dw_b, c_ln_g, c_ln_b, c_w_out = f(c_w_in), f(c_dw_w), f(c_dw_b), f(c_ln_g), f(c_ln_b), f(c_w_out)
    cols, rows = _perm_l0()
    wt = np.concatenate([_tiles(a_w_in[0][:, cols]), _tiles(c_w_in[0])], axis=0)
    wo = np.concatenate([_wo_halves(ab_w_out[0][rows, :]), _wo_halves(c_w_out[0])], axis=0)
    gpreT = np.ascontiguousarray(pre_norm.reshape(2, 8, 128).transpose(2, 0, 1))
    gpost = np.ascontiguousarray(np.broadcast_to(post_norm[None, :, :], (128, 2, D)))
    sk = a_sinks[0]
    sinkT = np.ascontiguousarray(np.repeat(sk, 128)[None, :])
    pscaleT = np.ascontiguousarray(b_pool_scale[0].reshape(4, 128).T)
    dwp = np.zeros((32, D), np.float32)
    dwp[:CONV_K] = c_dw_w[0][:, 0, :]
    wsh = np.ascontiguousarray(dwp.reshape(8, 4, 8, 4, 32).transpose(1, 4, 2, 3, 0).reshape(128, 8, 32))
    id4 = np.ascontiguousarray(np.tile(np.eye(32, dtype=np.float32), (4, 1)))
    cparT = np.ascontiguousarray(np.stack([c_dw_b[0], c_ln_g[0], c_ln_b[0]], axis=0).reshape(3, 8, 128).transpose(2, 0, 1))
    pw = np.ascontiguousarray(b_pool_w[0])
    in_maps = []
    for c in range(NCORES):
        b, hlf = c // 2, c % 2
        own = x[b, hlf * TOK_CORE:(hlf + 1) * TOK_CORE]
        halo = x[b, TOK_CORE - 256:TOK_CORE] if hlf == 1 else np.zeros((256, D), np.float32)
        xin = np.ascontiguousarray(np.concatenate([halo, own], axis=0))
        fh = (hlf == 0)
        in_maps.append({"xin": xin, "wt": wt, "wo": wo, "pool_w": pw, "gpreT": gpreT, "gpost": gpost, "sinkT": sinkT,
                        "pscaleT": pscaleT, "btab": _btab(fh), "rc0": _rc0(fh), "wsh": wsh, "id4": id4, "cparT": cparT})
    if "nc" not in _NC_CACHE:
        _NC_CACHE["nc"] = build_fused()
    res = run_bass_kernel_spmd(_NC_CACHE["nc"], in_maps, core_ids=list(range(NCORES)))
    out = np.zeros((4, 8192, D), np.float32)
    for c in range(NCORES):
        b, hlf = c // 2, c % 2
        out[b, hlf * TOK_CORE:(hlf + 1) * TOK_CORE] = res.results[c]["yout"]
    return out
```

```python
import numpy as np
from contextlib import ExitStack
import concourse.bass as bass
import concourse.mybir as mybir
from concourse.bass_utils import run_bass_kernel_spmd

F32 = mybir.dt.float32
BF16 = mybir.dt.bfloat16
AF = mybir.ActivationFunctionType
ALU = mybir.AluOpType

D = 1024
NCORES = 8
TOK_CORE = 4096
EPS = 1e-6
NEG = -1e30
CONV_K = 31


class Prog:
    def __init__(self, nc, stack):
        self.nc = nc
        self.eng = {"pe": nc.tensor, "act": nc.scalar, "dve": nc.vector, "pool": nc.gpsimd, "sp": nc.sync}
        self.sem = {}
        for n in ["pe", "act", "dve", "pool"]:
            self.sem[n] = stack.enter_context(nc.semaphore("s_" + n))
        self.ND = {"sp": 24, "pool": 16, "act": 6}
        for q, n in self.ND.items():
            for i in range(n):
                self.sem[("d", q, i)] = stack.enter_context(nc.semaphore("s_d%s%d" % (q, i)))
        self.dma_cnt = {"sp": 0, "pool": 0, "act": 0}
        self.cnt = {n: 0 for n in ["pe", "act", "dve", "pool"]}
        self.dma_i = 0
        self.waited = {n: {} for n in self.eng}
        self.res = {}
        self.out_events = []

    def _deps(self, reads, writes):
        deps = {}

        def add(ev):
            if ev is None:
                return
            k, v = ev
            if deps.get(k, 0) < v:
                deps[k] = v

        for r in reads:
            st = self.res.get(r)
            if st:
                add(st["w"])
        for w in writes:
            st = self.res.get(w)
            if st:
                add(st["w"])
                for k, v in st["r"].items():
                    add((k, v))
        return deps

    def _wait(self, en, deps):
        e = self.eng[en]
        for k, v in deps.items():
            if k == "pe" and en == "pe":
                continue
            if self.waited[en].get(k, 0) >= v:
                continue
            e.wait_ge(self.sem[k], v)
            self.waited[en][k] = v

    def _record(self, ev, reads, writes):
        k, v = ev
        for r in reads:
            st = self.res.setdefault(r, {"w": None, "r": {}})
            if st["r"].get(k, 0) < v:
                st["r"][k] = v
        for w in writes:
            self.res[w] = {"w": ev, "r": {}}

    def op(self, en, fn, reads=(), writes=(), inc=True):
        self._wait(en, self._deps(reads, writes))
        ins = fn(self.eng[en])
        ev = (en, self.cnt[en] + 1)
        if inc:
            self.cnt[en] += 1
            ins.then_inc(self.sem[en], 1)
        self._record(ev, reads, writes)
        return ins

    def dma(self, out, in_, reads=(), writes=(), queue="sp", is_output=False):
        n = self.dma_cnt[queue]
        self.dma_cnt[queue] += 1
        slot = n % self.ND[queue]
        k = ("d", queue, slot)
        prev = 16 * (n // self.ND[queue])
        deps = self._deps(reads, writes)
        if prev > 0 and deps.get(k, 0) < prev:
            deps[k] = prev
        self._wait(queue, deps)
        ins = self.eng[queue].dma_start(out=out, in_=in_)
        ins.then_inc(self.sem[k], 16)
        ev = (k, prev + 16)
        self.dma_i += 1
        self._record(ev, reads, writes)
        if is_output:
            self.out_events.append(ev)
        return ev

    def finish(self):
        deps = {}
        for k, v in self.out_events:
            deps[k] = max(deps.get(k, 0), v)
        self.waited["sp"] = {}
        self._wait("sp", deps)


NW = 6
NXS = 8
T_Q, T_K, T_GA, T_U, T_GB, T_V = 0, 4, 5, 9, 13, 17
T_A, T_B, T_G = 18, 26, 34
NTILE = 42


def build_fused(n_own_blocks=32, NB=4):
    nc = bass.Bass("TRN2", target_bir_lowering=False)
    NT = 128 * (n_own_blocks + 2)
    dr = lambda n, s: nc.dram_tensor(n, s, F32, kind="ExternalInput").ap()
    xin = dr("xin", [NT, D])
    wt_d = dr("wt", [NTILE, 128, 1024])
    wo_d = dr("wo", [4, 128, 4096])
    pool_w = dr("pool_w", [4, 128, 128])
    gpreT_d = dr("gpreT", [128, 2, 8])
    gpost_d = dr("gpost", [128, 2, D])
    sink_d = dr("sinkT", [1, 1024])
    pscale_d = dr("pscaleT", [128, 4])
    btab_d = dr("btab", [128, 6, 512])
    rc0_d = dr("rc0", [128, 4, 16])
    wsh_d = dr("wsh", [128, 8, 32])
    id4_d = dr("id4", [128, 32])
    cp_d = dr("cparT", [128, 3, 8])
    yout = nc.dram_tensor("yout", [NT - 256, D], F32, kind="ExternalOutput").ap()
    wt_bf = nc.dram_tensor("wt_bf", [NTILE, 128, 1024], BF16, kind="Internal").ap()
    wo_bf = nc.dram_tensor("wo_bf", [4, 128, 4096], BF16, kind="Internal").ap()

    with ExitStack() as stack:
        P = Prog(nc, stack)
        a = nc.alloc_sbuf_tensor
        NTC = 128 * NB
        HL = CONV_K - 1
        ident = a("ident", [128, 128], BF16)
        ones = a("ones", [128, 128], BF16)
        epsc = a("epsc", [128, 1], F32)
        st = a("stt", [128, 16], F32)
        btab = a("btab_s", [128, 6, 512], BF16)
        gpost = a("gpost_s", [128, 2, D], F32)
        gpreT = a("gpreT_s", [128, 2, 8], F32)
        ES = a("ES", [1, 2, 1024], BF16)
        eshi = ES[0:1, 0, :]
        eslo = ES[0:1, 1, :]
        pscale = a("pscale", [128, 4], F32)
        rc0 = a("rc0_s", [128, 4, 16], F32)
        wsh = a("wsh_s", [128, 8, 32], F32)
        id4 = a("id4_s", [128, 32], F32)
        DG2 = a("DG2", [128, 8, 32, 32], BF16)
        GS = a("GS", [128, 2, 4, NTC + 28], BF16)
        cp = a("cp", [128, 3, 8], F32)
        PW = a("PW", [128, 4, 128], BF16)
        X = a("X", [128, NXS, D], F32)
        Hbf = a("Hbf", [128, 2, D], BF16)
        HT = a("HT", [128, 2, 8, NTC], BF16)
        TMP = a("TMP", [128, D], F32)
        Wr = a("Wr", [128, NW, 8, 128], BF16)
        WOr = a("WOr", [128, 2, 8, 512], BF16)
        KT = a("KT", [128, 128 + NTC], BF16)
        VS = a("VS", [128, NB + 1, 128], BF16)
        U = a("U", [128, 4, 16 + NTC], F32)
        PT = [a("PT%d" % i, [128, 512], BF16) for i in range(12)]
        SA = a("SA", [128, 16 + NTC], F32)
        SB = a("SB", [128, 16 + NTC], F32)
        T16 = a("T16", [128, 16], F32)
        DS = a("DS", [128, 512], F32)
        YN = a("YN", [128, 512], F32)
        R1 = a("R1", [128, 8, NTC], BF16)
        R2 = a("R2", [128, 8, NTC], F32)
        R2v = R2[:].bitcast(BF16).rearrange("p a (b c) -> p (a b) c", c=NTC)
        GLU = a("GLU", [128, 8, HL + NTC], BF16)
        SIG = [a("SIG%d" % i, [128, NTC], F32) for i in range(2)]
        CFB = [a("CFB%d" % i, [128, NTC], BF16) for i in range(2)]
        SQB = [a("SQB%d" % i, [128, NTC], BF16) for i in range(2)]
        MU = a("MU", [128, NTC], F32)
        RS = a("RS", [128, NTC], F32)
        MSQ = SA
        SCN = [a("SCN%d" % i, [128, NTC], BF16) for i in range(2)]
        psT = nc.alloc_psum_tensor("psT", [128, 8, 128], BF16)
        psA = [nc.alloc_psum_tensor("psA%d" % i, [128, 512], F32) for i in range(2)]
        psB = [nc.alloc_psum_tensor("psB%d" % i, [128, 512], F32) for i in range(2)]
        psN = nc.alloc_psum_tensor("psN", [128, 512], F32)
        psD = nc.alloc_psum_tensor("psD", [128, 512], F32)
        psV = nc.alloc_psum_tensor("psV", [128, 512], F32)

        QTk = lambda g: ("R2", g)
        GAk = lambda g: ("R2", 4 + g)
        GBk = lambda g: ("R2", 8 + g)
        PLk = lambda g: ("GS", 0, 0, g)
        PLks = lambda g: [("GS", 0, r, g) for r in range(4)]
        YAk = lambda g, i: ("R1", g, i)
        YBk = lambda g: [("R1", 4 + g, i) for i in range(NB)]
        GATEk = lambda ct: [("R1", ct, i) for i in range(NB)]
        CFk = lambda ct: [("R2", 2 * ct), ("R2", 2 * ct + 1)]
        QT = lambda g: R2v[:, g, :]
        GA = lambda g: R2v[:, 4 + g, :]
        GB = lambda g: R2v[:, 8 + g, :]
        PL = lambda g: GS[:, 0, g, 0:NTC]
        YA = lambda g: R1[:, g, :]
        YB = lambda g: R1[:, 4 + g, :]
        GATE = lambda ct: R1[:, ct, :]
        CF = lambda ct: R2[:, ct, :]

        conv_t = lambda t: P.dma(wt_bf[t], wt_d[t], writes=[("wtbf", t)], queue="pool")
        P.op("pool", lambda e: e.memset(ident[:], 0.0), writes=["ident"])
        P.op("pool", lambda e: e.affine_select(out=ident[:], in_=ident[:], pattern=[[-1, 128]], compare_op=ALU.not_equal,
                                                fill=1.0, base=0, channel_multiplier=1), reads=["ident"], writes=["ident"])
        P.op("pool", lambda e: e.memset(ones[:], 1.0), writes=["ones"])
        P.op("pool", lambda e: e.memset(epsc[:], EPS), writes=["epsc"])
        P.dma(gpreT[:], gpreT_d, writes=["gpreT"])
        P.dma(gpost[:], gpost_d, writes=["gpost"])
        P.dma(TMP[0:1, :], sink_d, writes=[("TMP", 0), ("TMP", 1)])
        P.dma(pscale[:], pscale_d, writes=["pscale"])
        P.dma(rc0[:], rc0_d, writes=["rc0"])
        P.dma(wsh[:], wsh_d, writes=["wsh"])
        P.dma(id4[:], id4_d, writes=["id4"])
        P.dma(cp[:], cp_d, writes=["cp"])
        P.op("pool", lambda e: e.memset(U[:], 0.0), writes=[("U", g) for g in range(4)])
        P.op("pool", lambda e: e.memset(KT[:, 0:128], 0.0), writes=["KTh"])
        P.op("pool", lambda e: e.memset(VS[:, 0, :], 0.0), writes=[("VS", 0)])
        for t in range(6):
            P.dma(btab[:, t, :], btab_d[:, t, :], writes=["btab"], queue="pool")
        P.dma(PW[:], pool_w.rearrange("g c d -> c g d"), writes=["PW"], queue="pool")
        for t in range(6):
            P.op("act", lambda e, t=t: e.activation(out=btab[:, t, :], in_=btab[:, t, :], func=AF.Exp), reads=["btab"], writes=["btab"])
        for h in range(2):
            P.dma(wo_bf[h], wo_d[h], writes=[("wobf", h)], queue="pool")
        P.op("pool", lambda e: e.memset(GLU[:], 0.0), writes=[("GLU", ct) for ct in range(8)])
        for ct in range(2):
            for t in (T_B + ct, T_A + ct, T_G + ct):
                conv_t(t)
        P.op("pool", lambda e: e.memset(GS[:], 0.0), writes=[("GS", gi, r, cg) for gi in range(2) for r in range(4) for cg in range(4)])
        for ct in range(8):
            P.op("pool", lambda e, ct=ct: e.tensor_tensor(out=DG2[:, ct], in0=wsh[:, ct, :].unsqueeze(2).broadcast_to([128, 32, 32]),
                                                         in1=id4[:].unsqueeze(1).broadcast_to([128, 32, 32]), op=ALU.mult),
                 reads=["wsh", "id4"], writes=["DG2"])
        for ct in range(2, 8):
            for t in (T_B + ct, T_A + ct, T_G + ct):
                conv_t(t)
        for h in range(2, 4):
            P.dma(wo_bf[h], wo_d[h], writes=[("wobf", h)], queue="pool")
        tk = [("TMP", 0), ("TMP", 1)]
        P.op("act", lambda e: e.activation(out=TMP[0:1, :], in_=TMP[0:1, :], func=AF.Exp), reads=tk, writes=tk)
        P.op("dve", lambda e: e.tensor_copy(out=eshi, in_=TMP[0:1, :]), reads=tk, writes=["eshi"])
        P.op("dve", lambda e: e.tensor_tensor(out=TMP[0:1, :], in0=TMP[0:1, :], in1=eshi, op=ALU.subtract), reads=tk + ["eshi"], writes=tk)
        P.op("dve", lambda e: e.tensor_copy(out=eslo, in_=TMP[0:1, :]), reads=tk, writes=["eslo"])

        ctr = {"x": 0, "hb": 0, "psA": 0, "w": 0, "sc": 0, "sig": 0, "dg": 0, "pc": 0, "cfb": 0, "scn": 0}

        def nxt(k, m):
            v = ctr[k] % m
            ctr[k] += 1
            return v

        def wtile(t):
            s = nxt("w", NW)
            P.dma(Wr[:, s].rearrange("p a b -> p (a b)"), wt_bf[t], reads=[("wtbf", t)], writes=[("Wr", s)])
            return s

        def wo_load(l):
            for h in range(2):
                P.dma(WOr[:, h].rearrange("p a b -> p (a b)"), wo_bf[l * 2 + h], reads=[("wobf", l * 2 + h)], writes=[("WOr", h)])

        class Chunk:
            pass

        def norm_a(ch, i, l):
            s = ch.slots[i]
            hb = nxt("hb", 2)
            ch.hb[(i, l)] = hb
            P.op("act", lambda e: e.activation(out=Hbf[:, hb, :], in_=X[:, s, :], func=AF.Square, accum_out=st[:, 0:1]),
                 reads=[("X", s)], writes=[("Hbf", hb), "st0"])
            P.op("act", lambda e: e.activation(out=st[:, 1:2], in_=st[:, 0:1], func=AF.Ln, bias=epsc[:], scale=1.0 / D),
                 reads=["st0", "epsc"], writes=["st1"])
            P.op("act", lambda e: e.activation(out=st[:, 2:3], in_=st[:, 1:2], func=AF.Exp, scale=-0.5), reads=["st1"], writes=["st2"])
            P.op("dve", lambda e: e.tensor_scalar(out=Hbf[:, hb, :], in0=X[:, s, :], scalar1=st[:, 2:3], scalar2=None, op0=ALU.mult),
                 reads=[("X", s), "st2"], writes=[("Hbf", hb)])

        def norm_b(ch, i, l):
            hb = ch.hb[(i, l)]
            for kt in range(8):
                P.op("pe", lambda e, kt=kt: e.transpose(psT[:, kt, :], Hbf[:, hb, kt * 128:(kt + 1) * 128], ident[:]),
                     reads=[("Hbf", hb), "ident"], writes=["psT"], inc=(kt == 7))
            gb = gpreT[:, l, :].unsqueeze(2).broadcast_to([128, 8, 128])
            P.op("dve", lambda e: e.tensor_tensor(out=HT[:, l, :, i * 128:(i + 1) * 128], in0=psT[:], in1=gb, op=ALU.mult),
                 reads=["psT", "gpreT"], writes=[("HT", l)])

        pref = {}

        def inproj(t, ntok, l):
            ws = pref.pop(t) if t in pref else wtile(t)
            bi = nxt("psA", 2)
            for kt in range(8):
                P.op("pe", lambda e, kt=kt: e.matmul(psA[bi][:, 0:ntok], lhsT=Wr[:, ws, kt, :], rhs=HT[:, l, kt, 0:ntok],
                                                     start=(kt == 0), stop=(kt == 7)),
                     reads=[("Wr", ws), ("HT", l)], writes=[("psA", bi)], inc=(kt == 7))
            return bi

        def outproj_post(s, lhs_fn, lhs_reads, l, out_ap, bank_list, bank_key):
            for half in range(2):
                for kt in range(8):
                    P.op("pe", lambda e, kt=kt, half=half: e.matmul(bank_list[half][:], lhsT=lhs_fn(kt), rhs=WOr[:, half, kt, :],
                                                                   start=(kt == 0), stop=(kt == 7)),
                         reads=list(lhs_reads) + [("WOr", half)], writes=[(bank_key, half)], inc=(kt == 7))
            for half in range(2):
                P.op("act", lambda e, half=half: e.activation(out=TMP[:, half * 512:(half + 1) * 512], in_=bank_list[half][:], func=AF.Square,
                                                             accum_out=st[:, 4 + half:5 + half]),
                     reads=[(bank_key, half)], writes=[("TMP", half), ("st4", half)])
            P.op("dve", lambda e: e.tensor_tensor(out=st[:, 6:7], in0=st[:, 4:5], in1=st[:, 5:6], op=ALU.add),
                 reads=[("st4", 0), ("st4", 1)], writes=["st6"])
            P.op("act", lambda e: e.activation(out=st[:, 7:8], in_=st[:, 6:7], func=AF.Ln, bias=epsc[:], scale=1.0 / D),
                 reads=["st6", "epsc"], writes=["st7"])
            P.op("act", lambda e: e.activation(out=st[:, 8:9], in_=st[:, 7:8], func=AF.Exp, scale=-0.5), reads=["st7"], writes=["st8"])
            for half in range(2):
                hs = slice(half * 512, (half + 1) * 512)
                P.op("dve", lambda e, half=half, hs=hs: e.tensor_tensor(out=TMP[:, hs], in0=bank_list[half][:], in1=gpost[:, l, hs], op=ALU.mult),
                     reads=[(bank_key, half), "gpost"], writes=[("TMP", half)])
                P.op("dve", lambda e, hs=hs: e.scalar_tensor_tensor(out=X[:, s, hs], in0=TMP[:, hs], scalar=st[:, 8:9], in1=X[:, s, hs],
                                                                    op0=ALU.mult, op1=ALU.add),
                     reads=[("TMP", half), "st8", ("X", s)], writes=[("X", s)])
            if out_ap is not None:
                P.dma(out_ap, X[:, s, :], reads=[("X", s)], queue="pool", is_output=True)

        def x_load(ch):
            ch.slots = []
            for b in ch.blocks:
                s = nxt("x", NXS)
                P.dma(X[:, s, :], xin[b * 128:(b + 1) * 128, :], writes=[("X", s)])
                ch.slots.append(s)

        def l0_tile_q(ch, g):
            bi = inproj(T_Q + g, ch.ntok, 0)
            P.op("act", lambda e: e.activation(out=QT(g)[:, 0:ch.ntok], in_=psA[bi][:, 0:ch.ntok], func=AF.Copy, scale=0.125),
                 reads=[("psA", bi)], writes=[QTk(g)])

        def l0_tile_k(ch):
            bi = inproj(T_K, ch.ntok, 0)
            P.op("act", lambda e: e.activation(out=KT[:, 128:128 + ch.ntok], in_=psA[bi][:, 0:ch.ntok], func=AF.Copy), reads=[("psA", bi)], writes=["KTn"])

        def l0_tile_ga(ch, g):
            bi = inproj(T_GA + g, ch.ntok, 0)
            P.op("act", lambda e: e.activation(out=GA(g)[:, 0:ch.ntok], in_=psA[bi][:, 0:ch.ntok], func=AF.Silu),
                 reads=[("psA", bi)], writes=[GAk(g)])

        def l0_tile_gb(ch, g):
            bi = inproj(T_GB + g, ch.ntok, 0)
            P.op("act", lambda e: e.activation(out=GB(g)[:, 0:ch.ntok], in_=psA[bi][:, 0:ch.ntok], func=AF.Silu),
                 reads=[("psA", bi)], writes=[GBk(g)])

        def l0_tile_u(ch, g):
            bi = inproj(T_U + g, ch.ntok, 0)
            P.op("act", lambda e: e.activation(out=U[:, g, 16:16 + ch.ntok], in_=psA[bi][:, 0:ch.ntok], func=AF.Copy),
                 reads=[("psA", bi)], writes=[("U", g)])

        def l0_tile_v(ch):
            ws = wtile(T_V)
            for i in range(ch.nb):
                for kt in range(8):
                    P.op("pe", lambda e, kt=kt, i=i: e.matmul(psV[:, 0:128], lhsT=HT[:, 0, kt, i * 128:(i + 1) * 128], rhs=Wr[:, ws, kt, :],
                                                              start=(kt == 0), stop=(kt == 7)),
                         reads=[("Wr", ws), ("HT", 0)], writes=["psV"], inc=(kt == 7))
                P.op("act", lambda e, i=i: e.activation(out=VS[:, i + 1, :], in_=psV[:, 0:128], func=AF.Copy), reads=["psV"], writes=[("VS", i + 1)])

        def l0_scores(ch, i):
            tb = slice(i * 128, (i + 1) * 128)
            for c in range(2):
                kcols = slice(i * 128 + c * 128, i * 128 + c * 128 + 128)
                pis = [nxt("sc", 2), nxt("sc", 2)]
                for g in range(4):
                    for kv in range(2):
                        pi = pis[kv]
                        pr = slice(kv * 64, kv * 64 + 64)
                        P.op("pe", lambda e, g=g, pi=pi, pr=pr: e.matmul(psB[pi][:, g * 128:(g + 1) * 128], lhsT=KT[pr, kcols], rhs=QT(g)[pr, tb],
                                                                        start=True, stop=True),
                             reads=["KTn", "KTh", QTk(g)], writes=[("psB", pi)], inc=(g == 3))
                for kv in range(2):
                    pi = pis[kv]
                    tab = kv * 2 + c
                    if c == 0 and ch.first_own and i == 0:
                        tab = 4 + kv
                    u = (i % 3) * 4 + kv * 2 + c
                    P.op("act", lambda e, u=u, pi=pi: e.activation(out=PT[u][:], in_=psB[pi][:], func=AF.Exp), reads=[("psB", pi)], writes=[("PT", u)])
                    P.op("pool", lambda e, u=u, tab=tab: e.tensor_tensor(out=PT[u][:], in0=PT[u][:], in1=btab[:, tab, :], op=ALU.mult),
                         reads=[("PT", u), "btab"], writes=[("PT", u)])

        def l0_pv(ch, i):
            tb = slice(i * 128, (i + 1) * 128)
            for kv in range(2):
                for c in range(2):
                    u = (i % 3) * 4 + kv * 2 + c
                    P.op("pe", lambda e, kv=kv, c=c, u=u: e.matmul(psD[kv * 64:kv * 64 + 64, :], lhsT=ones[:, 0:64], rhs=PT[u][:],
                                                                  start=(c == 0), stop=False, tile_position=(0, kv * 64)),
                         reads=["ones", ("PT", u)], writes=["psD"], inc=False)
                P.op("pe", lambda e, kv=kv: e.matmul(psD[kv * 64:kv * 64 + 64, :], lhsT=ones[0:1, 0:64], rhs=eshi[:, kv * 512:(kv + 1) * 512],
                                                     start=False, stop=False, tile_position=(0, kv * 64)),
                     reads=["ones", "eshi"], writes=["psD"], inc=False)
                P.op("pe", lambda e, kv=kv: e.matmul(psD[kv * 64:kv * 64 + 64, :], lhsT=ones[0:1, 0:64], rhs=eslo[:, kv * 512:(kv + 1) * 512],
                                                     start=False, stop=True, tile_position=(0, kv * 64)),
                     reads=["ones", "eslo"], writes=["psD"], inc=(kv == 1))
            for kv in range(2):
                for c in range(2):
                    u = (i % 3) * 4 + kv * 2 + c
                    P.op("pe", lambda e, kv=kv, c=c, u=u: e.matmul(psN[kv * 64:kv * 64 + 64, :], lhsT=VS[:, i + c, kv * 64:kv * 64 + 64], rhs=PT[u][:],
                                                                  start=(c == 0), stop=(c == 1), tile_position=(0, kv * 64)),
                         reads=[("VS", i + c), ("PT", u)], writes=["psN"], inc=(kv == 1 and c == 1))
            P.op("act", lambda e: e.activation(out=DS[:], in_=psD[:], func=AF.Ln), reads=["psD"], writes=["DS"])
            P.op("act", lambda e: e.activation(out=DS[:], in_=DS[:], func=AF.Exp, scale=-1.0), reads=["DS"], writes=["DS"])
            P.op("dve", lambda e: e.tensor_tensor(out=YN[:], in0=psN[:], in1=DS[:], op=ALU.mult), reads=["psN", "DS"], writes=["YN"])
            for g in range(4):
                P.op("pool", lambda e, g=g: e.tensor_tensor(out=YA(g)[:, tb], in0=YN[:, g * 128:(g + 1) * 128], in1=GA(g)[:, tb], op=ALU.mult),
                     reads=["YN", GAk(g)], writes=[YAk(g, i)])

        def l0_pool_pre(ch, g):
            ntok = ch.ntok
            w = 2 << g
            E = 16 + ntok
            cur = None
            step = 1
            bufs = [SA, SB]
            k = 0
            while step < w:
                lo = 2 * step - 1
                dst = bufs[k % 2]
                srcap = U[:, g, :] if cur is None else cur[:]
                rk = [("U", g)] if cur is None else [("SAB", (k - 1) % 2)]
                P.op("pool", lambda e, dst=dst, srcap=srcap, lo=lo, step=step: e.tensor_tensor(
                    out=dst[:, lo:E], in0=srcap[:, lo:E], in1=srcap[:, lo - step:E - step], op=ALU.add),
                     reads=rk, writes=[("SAB", k % 2)])
                cur = dst
                k += 1
                step *= 2
            ch.pool_cur[g] = (cur, ("SAB", (k - 1) % 2))

        def l0_pool_fin(ch, g):
            ntok = ch.ntok
            w = 2 << g
            E = 16 + ntok
            cur, lastk = ch.pool_cur[g]
            P.op("dve", lambda e: e.scalar_tensor_tensor(out=PL(g)[:, 0:ntok], in0=cur[:, 16:E], scalar=1.0 / w, in1=U[:, g, 16:E],
                                                         op0=ALU.mult, op1=ALU.subtract),
                 reads=[lastk, ("U", g)], writes=PLks(g))
            if ch.first_own:
                P.op("dve", lambda e: e.tensor_tensor(out=T16[:], in0=cur[:, 16:32], in1=rc0[:, g, :], op=ALU.mult),
                     reads=[lastk, "rc0"], writes=["T16"])
                P.op("dve", lambda e: e.tensor_tensor(out=PL(g)[:, 0:16], in0=T16[:], in1=U[:, g, 16:32], op=ALU.subtract),
                     reads=["T16", ("U", g)] + PLks(g), writes=PLks(g))

        def l0_pool_mm(ch, g):
            ntok = ch.ntok
            bk, bkey = [(psV, "psV"), (psA[0], ("psA", 0)), (psA[1], ("psA", 1)), (psV, "psV")][g]
            P.op("pe", lambda e: e.matmul(bk[:, 0:ntok], lhsT=PW[:, g, :], rhs=PL(g)[:, 0:ntok], start=True, stop=True),
                 reads=["PW"] + PLks(g), writes=[bkey])
            P.op("dve", lambda e: e.scalar_tensor_tensor(out=YB(g)[:, 0:ntok], in0=bk[:, 0:ntok], scalar=pscale[:, g:g + 1],
                                                         in1=GB(g)[:, 0:ntok], op0=ALU.mult, op1=ALU.mult),
                 reads=[bkey, "pscale", GBk(g)], writes=YBk(g))

        def l0_out(ch, i):
            tb = slice(i * 128, (i + 1) * 128)
            lhs = lambda kt: (YA(kt)[:, tb] if kt < 4 else YB(kt - 4)[:, tb])
            bl_, bk_ = (psA, "psA") if i % 2 == 0 else (psB, "psB")
            outproj_post(ch.slots[i], lhs, [YAk(g, i) for g in range(4)] + [("R1", 4 + g, i) for g in range(4)], 0, None, bl_, bk_)

        def l0_carry(ch):
            ntok, nb = ch.ntok, ch.nb
            P.op("pool", lambda e: e.tensor_copy(out=KT[:, 0:128], in_=KT[:, ntok:ntok + 128]), reads=["KTn"], writes=["KTh"])
            P.op("pool", lambda e: e.tensor_copy(out=VS[:, 0, :], in_=VS[:, nb, :]), reads=[("VS", nb)], writes=[("VS", 0)])
            for g in range(4):
                P.op("pool", lambda e, g=g: e.tensor_copy(out=U[:, g, 0:16], in_=U[:, g, ntok:ntok + 16]), reads=[("U", g)], writes=[("U", g)])

        def l1_tile_ba(ch, ct):
            ntok = ch.ntok
            si = nxt("sig", 2)
            bi = inproj(T_B + ct, ntok, 1)
            P.op("act", lambda e: e.activation(out=SIG[si][:, 0:ntok], in_=psA[bi][:, 0:ntok], func=AF.Tanh, scale=0.5),
                 reads=[("psA", bi)], writes=[("SIG", si)])
            bi2 = inproj(T_A + ct, ntok, 1)
            P.op("dve", lambda e: e.scalar_tensor_tensor(out=GLU[:, ct, HL:HL + ntok], in0=SIG[si][:, 0:ntok], scalar=1.0,
                                                         in1=psA[bi2][:, 0:ntok], op0=ALU.add, op1=ALU.mult),
                 reads=[("psA", bi2), ("SIG", si)], writes=[("GLU", ct)])

        def l1_tile_g(ch, ct):
            bi3 = inproj(T_G + ct, ch.ntok, 1)
            P.op("act", lambda e: e.activation(out=GATE(ct)[:, 0:ch.ntok], in_=psA[bi3][:, 0:ch.ntok], func=AF.Silu),
                 reads=[("psA", bi3)], writes=GATEk(ct))

        def l1_gs(ch, ct):
            ntok = ch.ntok
            gi = nxt("dg", 2)
            ch.gi[ct] = gi
            for r in range(4):
                Lr = ntok + 28 - (1 if r == 3 else 0)
                for cg in range(4):
                    P.op("dve", lambda e, r=r, cg=cg, Lr=Lr: e.tensor_copy(out=GS[r * 32:(r + 1) * 32, gi, cg, 0:Lr],
                                                                          in_=GLU[cg * 32:(cg + 1) * 32, ct, r:r + Lr]),
                         reads=[("GLU", ct)], writes=[("GS", gi, r, cg)], inc=(r == 3 and cg == 3))

        def l1_conv(ch, ct):
            ntok = ch.ntok
            gi = ch.gi[ct]
            pi = nxt("pc", 2)
            for tg in range(8):
                for cg in range(4):
                    P.op("pe", lambda e, tg=tg, cg=cg: e.matmul(psB[pi][cg * 32:(cg + 1) * 32, 0:ntok], lhsT=DG2[:, ct, cg * 8 + tg, :],
                                                                rhs=GS[:, gi, cg, 4 * tg:4 * tg + ntok], start=(tg == 0), stop=(tg == 7),
                                                                tile_position=(0, cg * 32)),
                         reads=["DG2"] + [("GS", gi, r, cg) for r in range(4)], writes=[("psB", pi)], inc=(tg == 7 and cg == 3))
            ci = nxt("cfb", 2)
            ch.ci[ct] = ci
            P.op("act", lambda e: e.activation(out=CFB[ci][:, 0:ntok], in_=psB[pi][:, 0:ntok], func=AF.Identity,
                                               bias=cp[:, 0, ct:ct + 1], scale=0.5), reads=[("psB", pi), "cp"], writes=[("CFB", ci)])
            P.op("act", lambda e: e.activation(out=SQB[ci][:, 0:ntok], in_=psB[pi][:, 0:ntok], func=AF.Square,
                                               bias=cp[:, 0, ct:ct + 1], scale=0.5), reads=[("psB", pi), "cp"], writes=[("SQB", ci)])
            P.op("act", lambda e: e.activation(out=CF(ct)[:, 0:ntok], in_=psB[pi][:, 0:ntok], func=AF.Identity,
                                               bias=cp[:, 0, ct:ct + 1], scale=0.5),
                 reads=[("psB", pi), "cp"], writes=CFk(ct))

        def l1_stats(ch, ct):
            ntok = ch.ntok
            ci = ch.ci[ct]
            P.op("pe", lambda e: e.matmul(psN[:, 0:ntok], lhsT=ones[:], rhs=CFB[ci][:, 0:ntok], start=(ct == 0), stop=(ct == 7)),
                 reads=["ones", ("CFB", ci)], writes=["psN"])
            P.op("pe", lambda e: e.matmul(psD[:, 0:ntok], lhsT=ones[:], rhs=SQB[ci][:, 0:ntok], start=(ct == 0), stop=(ct == 7)),
                 reads=["ones", ("SQB", ci)], writes=["psD"])

        def l1_carry(ch):
            for ct in range(8):
                P.op("pool", lambda e, ct=ct: e.tensor_copy(out=GLU[:, ct, 0:HL], in_=GLU[:, ct, ch.ntok:ch.ntok + HL]),
                     reads=[("GLU", ct)], writes=[("GLU", ct)])

        def l1_ln_chain(ch):
            ntok = ch.ntok
            P.op("act", lambda e: e.activation(out=MU[:, 0:ntok], in_=psN[:, 0:ntok], func=AF.Copy, scale=1.0 / D), reads=["psN"], writes=["MU"])
            P.op("dve", lambda e: e.tensor_tensor(out=MSQ[:, 0:ntok], in0=MU[:, 0:ntok], in1=MU[:, 0:ntok], op=ALU.mult), reads=["MU"], writes=[("SAB", 0)])
            P.op("dve", lambda e: e.scalar_tensor_tensor(out=RS[:, 0:ntok], in0=psD[:, 0:ntok], scalar=1.0 / D, in1=MSQ[:, 0:ntok],
                                                         op0=ALU.mult, op1=ALU.subtract), reads=["psD", ("SAB", 0)], writes=["RS"])
            P.op("act", lambda e: e.activation(out=RS[:, 0:ntok], in_=RS[:, 0:ntok], func=AF.Ln, bias=epsc[:], scale=1.0),
                 reads=["RS", "epsc"], writes=["RS"])
            P.op("act", lambda e: e.activation(out=RS[:, 0:ntok], in_=RS[:, 0:ntok], func=AF.Exp, scale=-0.5), reads=["RS"], writes=["RS"])

        def l1_ln_apply(ch, ct):
            ntok = ch.ntok
            P.op("dve", lambda e: e.tensor_tensor(out=CF(ct)[:, 0:ntok], in0=CF(ct)[:, 0:ntok], in1=MU[:, 0:ntok], op=ALU.subtract),
                 reads=CFk(ct) + ["MU"], writes=CFk(ct))
            P.op("dve", lambda e: e.tensor_tensor(out=CF(ct)[:, 0:ntok], in0=CF(ct)[:, 0:ntok], in1=RS[:, 0:ntok], op=ALU.mult),
                 reads=CFk(ct) + ["RS"], writes=CFk(ct))
            si = nxt("scn", 2)
            P.op("act", lambda e: e.activation(out=SCN[si][:, 0:ntok], in_=CF(ct)[:, 0:ntok], func=AF.Silu,
                                               bias=cp[:, 2, ct:ct + 1], scale=cp[:, 1, ct:ct + 1]),
                 reads=CFk(ct) + ["cp"], writes=[("SCN", si)])
            P.op("pool", lambda e: e.tensor_tensor(out=GATE(ct)[:, 0:ntok], in0=GATE(ct)[:, 0:ntok], in1=SCN[si][:, 0:ntok], op=ALU.mult),
                 reads=GATEk(ct) + [("SCN", si)], writes=GATEk(ct))

        def l1_out(ch, i):
            tb = slice(i * 128, (i + 1) * 128)
            ob = ch.blocks[i] - 2
            outproj_post(ch.slots[i], lambda kt: GATE(kt)[:, tb], [("R1", ct, i) for ct in range(8)], 1,
                         yout[ob * 128:(ob + 1) * 128, :], *((psB, "psB") if i % 2 == 0 else (psA, "psA")))

        def weave(A, B):
            na, nb_ = len(A), len(B)
            ia = ib = 0
            while ia < na or ib < nb_:
                if ia < na and (ib >= nb_ or ia * max(nb_, 1) <= ib * max(na, 1)):
                    A[ia]()
                    ia += 1
                else:
                    B[ib]()
                    ib += 1

        def l0_norms(ch):
            L = []
            for i in range(ch.nb + 1):
                if i < ch.nb:
                    L.append(lambda i=i: norm_a(ch, i, 0))
                if i >= 1:
                    L.append(lambda i=i: norm_b(ch, i - 1, 0))
            return L

        def l0_early(ch):
            L = [lambda: l0_tile_k(ch), lambda: l0_tile_v(ch)]
            for g in range(4):
                L.append(lambda g=g: l0_tile_u(ch, g))
                L.append(lambda g=g: (l0_pool_fin(ch, g - 1) if g >= 1 else None, l0_pool_pre(ch, g)))
            return L

        def l0_mid(ch):
            L = [lambda g=g: l0_tile_q(ch, g) for g in range(4)]
            L += [lambda g=g: l0_tile_ga(ch, g) for g in range(4)]
            L += [lambda g=g: l0_tile_gb(ch, g) for g in range(4)]
            return L

        def l0_late(ch):
            nb = ch.nb
            wo_load(0)
            l0_pool_fin(ch, 3)
            l0_scores(ch, 0)
            if nb > 1:
                l0_scores(ch, 1)
            outs = []

            def do_out(j):
                l0_out(ch, j)
                norm_a(ch, j, 1)
                if j >= 1:
                    norm_b(ch, j - 1, 1)

            nout = 0
            for i in range(nb):
                if i + 2 < nb:
                    l0_scores(ch, i + 2)
                l0_pv(ch, i)
                if i == min(1, nb - 1):
                    for g in range(4):
                        l0_pool_mm(ch, g)
                if i >= 1:
                    do_out(nout)
                    nout += 1
            while nout < nb:
                do_out(nout)
                nout += 1
            norm_b(ch, nb - 1, 1)
            l0_carry(ch)

        def l1_front(ch, nx):
            tail = l0_norms(nx) if nx is not None else []
            if ch.halo:
                for ct in range(8):
                    l1_tile_ba(ch, ct)
                for f in tail:
                    f()
                l1_carry(ch)
                return
            for ct in range(8):
                l1_tile_ba(ch, ct)
                if ct - 1 >= 0:
                    l1_gs(ch, ct - 1)
                if ct - 2 >= 0:
                    l1_conv(ch, ct - 2)
                if ct - 3 >= 0:
                    l1_stats(ch, ct - 3)
            per = [1, 1, 1, 1, 1, 1, 1, 1]
            ti = 0
            for k in range(8):
                l1_tile_g(ch, k)
                if k == 0:
                    l1_gs(ch, 7)
                    l1_conv(ch, 6)
                    l1_stats(ch, 5)
                elif k == 1:
                    l1_conv(ch, 7)
                    l1_stats(ch, 6)
                elif k == 2:
                    l1_stats(ch, 7)
                elif k == 3:
                    l1_ln_chain(ch)
                for f in tail[ti:ti + per[k]]:
                    f()
                ti += per[k]
            for f in tail[ti:]:
                f()
            l1_carry(ch)

        def s2_s3(ch, nx):
            E = l0_early(nx) if nx is not None else []
            Q = [lambda g=g: l0_tile_q(nx, g) for g in range(4)] if nx is not None else []
            GAl = [lambda g=g: l0_tile_ga(nx, g) for g in range(4)] if nx is not None else []
            GBl = [lambda g=g: l0_tile_gb(nx, g) for g in range(4)] if nx is not None else []
            if ch.halo:
                for f in E + Q + GAl + GBl:
                    f()
                return
            I = (E + Q + GAl + GBl) if nx is not None else []
            for ct in range(8):
                l1_ln_apply(ch, ct)
                for f in I[3 * ct:3 * ct + 3] if ct < 7 else I[21:]:
                    f()
            wo_load(1)
            for i in range(ch.nb):
                l1_out(ch, i)

        chunks = []
        bl = [[0, 1]]
        b = 2
        while b < n_own_blocks + 2:
            nb = min(NB, n_own_blocks + 2 - b)
            bl.append(list(range(b, b + nb)))
            b += nb
        for idx, blocks in enumerate(bl):
            ch = Chunk()
            ch.blocks, ch.nb, ch.ntok = blocks, len(blocks), 128 * len(blocks)
            ch.halo, ch.first_own = (idx == 0), (idx == 1)
            ch.gi, ch.ci, ch.hb, ch.pool_cur = {}, {}, {}, {}
            chunks.append(ch)

        def l0_weight_prepass():
            order = [T_K, T_V] + [T_U + g for g in range(4)] + [T_Q + g for g in range(4)] \
                + [T_GA + g for g in range(4)] + [T_GB + g for g in range(4)]
            for j, t in enumerate(order):
                stg = 2 + j % 6
                P.dma(X[:, stg, :], wt_d[t], writes=[("X", stg)])
                ws = j % NW
                wflat = Wr[:, ws].rearrange("p a b -> p (a b)")
                if j % 2 == 0:
                    P.op("dve", lambda e, wflat=wflat, stg=stg: e.tensor_copy(out=wflat, in_=X[:, stg, :]), reads=[("X", stg)], writes=[("Wr", ws)])
                else:
                    P.op("act", lambda e, wflat=wflat, stg=stg: e.activation(out=wflat, in_=X[:, stg, :], func=AF.Copy), reads=[("X", stg)], writes=[("Wr", ws)])
                P.dma(wt_bf[t], wflat, reads=[("Wr", ws)], writes=[("wtbf", t)], queue="act")

        x_load(chunks[0])
        l0_weight_prepass()
        for f in l0_norms(chunks[0]) + l0_early(chunks[0]) + l0_mid(chunks[0]):
            f()
        l0_late(chunks[0])
        for idx, ch in enumerate(chunks):
            nx = chunks[idx + 1] if idx + 1 < len(chunks) else None
            if nx is not None:
                x_load(nx)
            l1_front(ch, nx)
            s2_s3(ch, nx)
            if nx is not None:
                l0_late(nx)
        P.finish()
    return nc


def _perm_l0():
    q0, k0, v0, ga0, u0, gb0 = 0, 512, 640, 768, 1280, 1792
    cols = []
    for g in range(4):
        cols += list(range(q0 + g * 64, q0 + g * 64 + 64)) + list(range(q0 + (4 + g) * 64, q0 + (4 + g) * 64 + 64))
    cols += list(range(k0, k0 + 128))
    for g in range(4):
        cols += list(range(ga0 + g * 64, ga0 + g * 64 + 64)) + list(range(ga0 + (4 + g) * 64, ga0 + (4 + g) * 64 + 64))
    cols += list(range(u0, u0 + 512))
    cols += list(range(gb0, gb0 + 512))
    cols += list(range(v0, v0 + 128))
    rows = []
    for g in range(4):
        rows += list(range(g * 64, g * 64 + 64)) + list(range((4 + g) * 64, (4 + g) * 64 + 64))
    rows += list(range(512, 1024))
    return np.array(cols), np.array(rows)


def _btab(first_half):
    s = np.arange(128)[:, None]
    q = np.arange(128)[None, :]
    tabs = np.zeros((128, 6, 4, 128), np.float32)
    for kv in range(2):
        for g in range(4):
            h = kv * 4 + g
            slope = 2.0 ** (-(h + 1))
            cur = np.where(q >= s, -slope * (q - s).astype(np.float32), NEG)
            prev = np.where(s > q, -slope * (q + 128 - s).astype(np.float32), NEG)
            tabs[:, kv * 2 + 0, g, :] = prev
            tabs[:, kv * 2 + 1, g, :] = cur
            tabs[:, 4 + kv, g, :] = NEG if first_half else prev
    return np.ascontiguousarray(tabs.reshape(128, 6, 512))


def _rc0(first_half):
    rc = np.zeros((128, 4, 16), np.float32)
    for g in range(4):
        w = 2 << g
        t = np.arange(16)
        cntv = np.minimum(w, t + 1) if first_half else np.full(16, w)
        rc[:, g, :] = (1.0 / cntv.astype(np.float32))[None, :]
    return rc


def _tiles(w):
    n = w.shape[1] // 128
    return np.ascontiguousarray(w.reshape(8, 128, n, 128).transpose(2, 1, 0, 3).reshape(n, 128, 1024))


def _wo_halves(w):
    return np.ascontiguousarray(w.reshape(8, 128, 2, 512).transpose(2, 1, 0, 3).reshape(2, 128, 4096))


_NC_CACHE = {}


def kernel(x, pre_norm, post_norm, a_w_in, a_sinks, b_pool_w, b_pool_scale, ab_w_out,
           c_w_in, c_dw_w, c_dw_b, c_ln_g, c_ln_b, c_w_out):
    f = lambda v: np.asarray(v, dtype=np.float32)
    x, pre_norm, post_norm = f(x), f(pre_norm), f(post_norm)
    a_w_in, a_sinks, b_pool_w, b_pool_scale, ab_w_out = f(a_w_in), f(a_sinks), f(b_pool_w), f(b_pool_scale), f(ab_w_out)
    c_w_in, c_dw_w, c_dw_b, c_ln_g, c_ln_b, c_w_out = f(c_w_in), f(c_dw_w), f(c_dw_b), f(c_ln_g), f(c_ln_b), f(c_w_out)
    cols, rows = _perm_l0()
    wt = np.concatenate([_tiles(a_w_in[0][:, cols]), _tiles(c_w_in[0])], axis=0)
    wo = np.concatenate([_wo_halves(ab_w_out[0][rows, :]), _wo_halves(c_w_out[0])], axis=0)
    gpreT = np.ascontiguousarray(pre_norm.reshape(2, 8, 128).transpose(2, 0, 1))
    gpost = np.ascontiguousarray(np.broadcast_to(post_norm[None, :, :], (128, 2, D)))
    sk = a_sinks[0]
    sinkT = np.ascontiguousarray(np.repeat(sk, 128)[None, :])
    pscaleT = np.ascontiguousarray(b_pool_scale[0].reshape(4, 128).T)
    dwp = np.zeros((32, D), np.float32)
    dwp[:CONV_K] = c_dw_w[0][:, 0, :]
    wsh = np.ascontiguousarray(dwp.reshape(8, 4, 8, 4, 32).transpose(1, 4, 2, 3, 0).reshape(128, 8, 32))
    id4 = np.ascontiguousarray(np.tile(np.eye(32, dtype=np.float32), (4, 1)))
    cparT = np.ascontiguousarray(np.stack([c_dw_b[0], c_ln_g[0], c_ln_b[0]], axis=0).reshape(3, 8, 128).transpose(2, 0, 1))
    pw = np.ascontiguousarray(b_pool_w[0])
    in_maps = []
    for c in range(NCORES):
        b, hlf = c // 2, c % 2
        own = x[b, hlf * TOK_CORE:(hlf + 1) * TOK_CORE]
        halo = x[b, TOK_CORE - 256:TOK_CORE] if hlf == 1 else np.zeros((256, D), np.float32)
        xin = np.ascontiguousarray(np.concatenate([halo, own], axis=0))
        fh = (hlf == 0)
        in_maps.append({"xin": xin, "wt": wt, "wo": wo, "pool_w": pw, "gpreT": gpreT, "gpost": gpost, "sinkT": sinkT,
                        "pscaleT": pscaleT, "btab": _btab(fh), "rc0": _rc0(fh), "wsh": wsh, "id4": id4, "cparT": cparT})
    if "nc" not in _NC_CACHE:
        _NC_CACHE["nc"] = build_fused()
    res = run_bass_kernel_spmd(_NC_CACHE["nc"], in_maps, core_ids=list(range(NCORES)))
    out = np.zeros((4, 8192, D), np.float32)
    for c in range(NCORES):
        b, hlf = c // 2, c % 2
        out[b, hlf * TOK_CORE:(hlf + 1) * TOK_CORE] = res.results[c]["yout"]
    return out
```

```python
import numpy as np
from contextlib import ExitStack
import concourse.bass as bass
import concourse.mybir as mybir
from concourse.bass_utils import run_bass_kernel_spmd

F32 = mybir.dt.float32
BF16 = mybir.dt.bfloat16
AF = mybir.ActivationFunctionType
ALU = mybir.AluOpType

D = 1024
NCORES = 8
TOK_CORE = 4096
EPS = 1e-6
NEG = -1e30
CONV_K = 31


class Prog:
    def __init__(self, nc, stack):
        self.nc = nc
        self.eng = {"pe": nc.tensor, "act": nc.scalar, "dve": nc.vector, "pool": nc.gpsimd, "sp": nc.sync}
        self.sem = {}
        for n in ["pe", "act", "dve", "pool"]:
            self.sem[n] = stack.enter_context(nc.semaphore("s_" + n))
        self.ND = {"sp": 30, "pool": 16, "act": 2}
        for q, n in self.ND.items():
            for i in range(n):
                self.sem[("d", q, i)] = stack.enter_context(nc.semaphore("s_d%s%d" % (q, i)))
        self.dma_cnt = {"sp": 0, "pool": 0, "act": 0}
        self.cnt = {n: 0 for n in ["pe", "act", "dve", "pool"]}
        self.dma_i = 0
        self.waited = {n: {} for n in self.eng}
        self.res = {}
        self.out_events = []

    def _deps(self, reads, writes):
        deps = {}

        def add(ev):
            if ev is None:
                return
            k, v = ev
            if deps.get(k, 0) < v:
                deps[k] = v

        for r in reads:
            st = self.res.get(r)
            if st:
                add(st["w"])
        for w in writes:
            st = self.res.get(w)
            if st:
                add(st["w"])
                for k, v in st["r"].items():
                    add((k, v))
        return deps

    def _wait(self, en, deps):
        e = self.eng[en]
        for k, v in deps.items():
            if k == "pe" and en == "pe":
                continue
            if self.waited[en].get(k, 0) >= v:
                continue
            e.wait_ge(self.sem[k], v)
            self.waited[en][k] = v

    def _record(self, ev, reads, writes):
        k, v = ev
        for r in reads:
            st = self.res.setdefault(r, {"w": None, "r": {}})
            if st["r"].get(k, 0) < v:
                st["r"][k] = v
        for w in writes:
            self.res[w] = {"w": ev, "r": {}}

    def op(self, en, fn, reads=(), writes=(), inc=True):
        self._wait(en, self._deps(reads, writes))
        ins = fn(self.eng[en])
        ev = (en, self.cnt[en] + 1)
        if inc:
            self.cnt[en] += 1
            ins.then_inc(self.sem[en], 1)
        self._record(ev, reads, writes)
        return ins

    def dma(self, out, in_, reads=(), writes=(), queue="sp", is_output=False):
        n = self.dma_cnt[queue]
        self.dma_cnt[queue] += 1
        slot = n % self.ND[queue]
        k = ("d", queue, slot)
        prev = 16 * (n // self.ND[queue])
        deps = self._deps(reads, writes)
        if prev > 0 and deps.get(k, 0) < prev:
            deps[k] = prev
        self._wait(queue, deps)
        ins = self.eng[queue].dma_start(out=out, in_=in_)
        ins.then_inc(self.sem[k], 16)
        ev = (k, prev + 16)
        self.dma_i += 1
        self._record(ev, reads, writes)
        if is_output:
            self.out_events.append(ev)
        return ev

    def finish(self):
        deps = {}
        for k, v in self.out_events:
            deps[k] = max(deps.get(k, 0), v)
        self.waited["sp"] = {}
        self._wait("sp", deps)


NW = 6
NXS = 8
T_Q, T_K, T_GA, T_U, T_GB, T_V = 0, 4, 5, 9, 13, 17
T_A, T_B, T_G = 18, 26, 34
NTILE = 42


def build_fused(n_own_blocks=32, NB=4):
    nc = bass.Bass("TRN2", target_bir_lowering=False)
    NT = 128 * (n_own_blocks + 2)
    dr = lambda n, s: nc.dram_tensor(n, s, F32, kind="ExternalInput").ap()
    xin = dr("xin", [NT, D])
    wt_d = dr("wt", [NTILE, 128, 1024])
    wo_d = dr("wo", [4, 128, 4096])
    pool_w = dr("pool_w", [4, 128, 128])
    gpreT_d = dr("gpreT", [128, 2, 8])
    gpost_d = dr("gpost", [128, 2, D])
    sink_d = dr("sinkT", [1, 1024])
    pscale_d = dr("pscaleT", [128, 4])
    btab_d = dr("btab", [128, 6, 512])
    rc0_d = dr("rc0", [128, 4, 16])
    wsh_d = dr("wsh", [128, 8, 32])
    id4_d = dr("id4", [128, 32])
    cp_d = dr("cparT", [128, 3, 8])
    yout = nc.dram_tensor("yout", [NT - 256, D], F32, kind="ExternalOutput").ap()
    wt_bf = nc.dram_tensor("wt_bf", [NTILE, 128, 1024], BF16, kind="Internal").ap()
    wo_bf = nc.dram_tensor("wo_bf", [4, 128, 4096], BF16, kind="Internal").ap()

    with ExitStack() as stack:
        P = Prog(nc, stack)
        a = nc.alloc_sbuf_tensor
        NTC = 128 * NB
        HL = CONV_K - 1
        ident = a("ident", [128, 128], BF16)
        ones = a("ones", [128, 128], BF16)
        epsc = a("epsc", [128, 1], F32)
        st = a("stt", [128, 16], F32)
        btab = a("btab_s", [128, 6, 512], BF16)
        gpost = a("gpost_s", [128, 2, D], F32)
        gpreT = a("gpreT_s", [128, 2, 8], F32)
        ES = a("ES", [1, 2, 1024], BF16)
        eshi = ES[0:1, 0, :]
        eslo = ES[0:1, 1, :]
        pscale = a("pscale", [128, 4], F32)
        rc0 = a("rc0_s", [128, 4, 16], F32)
        wsh = a("wsh_s", [128, 8, 32], F32)
        id4 = a("id4_s", [128, 32], F32)
        DG2 = a("DG2", [128, 8, 32, 32], BF16)
        GS = a("GS", [128, 2, 4, NTC + 28], BF16)
        cp = a("cp", [128, 3, 8], F32)
        PW = a("PW", [128, 4, 128], BF16)
        X = a("X", [128, NXS, D], F32)
        Hbf = a("Hbf", [128, 2, D], BF16)
        HT = a("HT", [128, 2, 8, NTC], BF16)
        TMP = a("TMP", [128, D], F32)
        Wr = a("Wr", [128, NW, 8, 128], BF16)
        WOr = a("WOr", [128, 2, 8, 512], BF16)
        KT = a("KT", [128, 128 + NTC], BF16)
        VS = a("VS", [128, NB + 1, 128], BF16)
        U = a("U", [128, 4, 16 + NTC], F32)
        PT = [a("PT%d" % i, [128, 512], BF16) for i in range(12)]
        SA = a("SA", [128, 16 + NTC], F32)
        SB = a("SB", [128, 16 + NTC], F32)
        T16 = a("T16", [128, 16], F32)
        DS = a("DS", [128, 512], F32)
        YN = a("YN", [128, 512], F32)
        R1 = a("R1", [128, 8, NTC], BF16)
        R2 = a("R2", [128, 8, NTC], F32)
        R2v = R2[:].bitcast(BF16).rearrange("p a (b c) -> p (a b) c", c=NTC)
        GLU = a("GLU", [128, 8, HL + NTC], BF16)
        SIG = [a("SIG%d" % i, [128, NTC], F32) for i in range(2)]
        CFB = [a("CFB%d" % i, [128, NTC], BF16) for i in range(2)]
        SQB = [a("SQB%d" % i, [128, NTC], BF16) for i in range(2)]
        MU = a("MU", [128, NTC], F32)
        RS = a("RS", [128, NTC], F32)
        MSQ = SA
        SCN = [a("SCN%d" % i, [128, NTC], BF16) for i in range(2)]
        psT = nc.alloc_psum_tensor("psT", [128, 8, 128], BF16)
        psA = [nc.alloc_psum_tensor("psA%d" % i, [128, 512], F32) for i in range(2)]
        psB = [nc.alloc_psum_tensor("psB%d" % i, [128, 512], F32) for i in range(2)]
        psN = nc.alloc_psum_tensor("psN", [128, 512], F32)
        psD = nc.alloc_psum_tensor("psD", [128, 512], F32)
        psV = nc.alloc_psum_tensor("psV", [128, 512], F32)

        QTk = lambda g: ("R2", g)
        GAk = lambda g: ("R2", 4 + g)
        GBk = lambda g: ("R2", 8 + g)
        PLk = lambda g: ("GS", 0, 0, g)
        PLks = lambda g: [("GS", 0, r, g) for r in range(4)]
        YAk = lambda g, i: ("R1", g, i)
        YBk = lambda g: [("R1", 4 + g, i) for i in range(NB)]
        GATEk = lambda ct: [("R1", ct, i) for i in range(NB)]
        CFk = lambda ct: [("R2", 2 * ct), ("R2", 2 * ct + 1)]
        QT = lambda g: R2v[:, g, :]
        GA = lambda g: R2v[:, 4 + g, :]
        GB = lambda g: R2v[:, 8 + g, :]
        PL = lambda g: GS[:, 0, g, 0:NTC]
        YA = lambda g: R1[:, g, :]
        YB = lambda g: R1[:, 4 + g, :]
        GATE = lambda ct: R1[:, ct, :]
        CF = lambda ct: R2[:, ct, :]

        conv_t = lambda t: P.dma(wt_bf[t], wt_d[t], writes=[("wtbf", t)], queue="pool")
        P.op("pool", lambda e: e.memset(ident[:], 0.0), writes=["ident"])
        P.op("pool", lambda e: e.affine_select(out=ident[:], in_=ident[:], pattern=[[-1, 128]], compare_op=ALU.not_equal,
                                                fill=1.0, base=0, channel_multiplier=1), reads=["ident"], writes=["ident"])
        P.op("pool", lambda e: e.memset(ones[:], 1.0), writes=["ones"])
        P.op("pool", lambda e: e.memset(epsc[:], EPS), writes=["epsc"])
        P.dma(gpreT[:], gpreT_d, writes=["gpreT"])
        P.dma(gpost[:], gpost_d, writes=["gpost"])
        P.dma(TMP[0:1, :], sink_d, writes=[("TMP", 0), ("TMP", 1)])
        P.dma(pscale[:], pscale_d, writes=["pscale"])
        P.dma(rc0[:], rc0_d, writes=["rc0"])
        P.dma(wsh[:], wsh_d, writes=["wsh"])
        P.dma(id4[:], id4_d, writes=["id4"])
        P.dma(cp[:], cp_d, writes=["cp"])
        P.op("pool", lambda e: e.memset(U[:], 0.0), writes=[("U", g) for g in range(4)])
        P.op("pool", lambda e: e.memset(KT[:, 0:128], 0.0), writes=["KTh"])
        P.op("pool", lambda e: e.memset(VS[:, 0, :], 0.0), writes=[("VS", 0)])
        for t in range(6):
            P.dma(btab[:, t, :], btab_d[:, t, :], writes=["btab"], queue="pool")
        P.dma(PW[:], pool_w.rearrange("g c d -> c g d"), writes=["PW"], queue="pool")
        for t in range(6):
            P.op("act", lambda e, t=t: e.activation(out=btab[:, t, :], in_=btab[:, t, :], func=AF.Exp), reads=["btab"], writes=["btab"])
        for h in range(2):
            P.dma(wo_bf[h], wo_d[h], writes=[("wobf", h)], queue="pool")
        tk = [("TMP", 0), ("TMP", 1)]
        P.op("act", lambda e: e.activation(out=TMP[0:1, :], in_=TMP[0:1, :], func=AF.Exp), reads=tk, writes=tk)
        P.op("dve", lambda e: e.tensor_copy(out=eshi, in_=TMP[0:1, :]), reads=tk, writes=["eshi"])
        P.op("dve", lambda e: e.tensor_tensor(out=TMP[0:1, :], in0=TMP[0:1, :], in1=eshi, op=ALU.subtract), reads=tk + ["eshi"], writes=tk)
        P.op("dve", lambda e: e.tensor_copy(out=eslo, in_=TMP[0:1, :]), reads=tk, writes=["eslo"])

        ctr = {"x": 0, "hb": 0, "psA": 0, "w": 0, "sc": 0, "sig": 0, "dg": 0, "pc": 0, "cfb": 0, "scn": 0}

        def nxt(k, m):
            v = ctr[k] % m
            ctr[k] += 1
            return v

        def wtile(t):
            s = nxt("w", NW)
            P.dma(Wr[:, s].rearrange("p a b -> p (a b)"), wt_bf[t], reads=[("wtbf", t)], writes=[("Wr", s)])
            return s

        def wo_load(l):
            for h in range(2):
                P.dma(WOr[:, h].rearrange("p a b -> p (a b)"), wo_bf[l * 2 + h], reads=[("wobf", l * 2 + h)], writes=[("WOr", h)])

        class Chunk:
            pass

        def norm_a(ch, i, l):
            s = ch.slots[i]
            hb = nxt("hb", 2)
            ch.hb[(i, l)] = hb
            P.op("act", lambda e: e.activation(out=Hbf[:, hb, :], in_=X[:, s, :], func=AF.Square, accum_out=st[:, 0:1]),
                 reads=[("X", s)], writes=[("Hbf", hb), "st0"])
            P.op("act", lambda e: e.activation(out=st[:, 1:2], in_=st[:, 0:1], func=AF.Ln, bias=epsc[:], scale=1.0 / D),
                 reads=["st0", "epsc"], writes=["st1"])
            P.op("act", lambda e: e.activation(out=st[:, 2:3], in_=st[:, 1:2], func=AF.Exp, scale=-0.5), reads=["st1"], writes=["st2"])
            P.op("dve", lambda e: e.tensor_scalar(out=Hbf[:, hb, :], in0=X[:, s, :], scalar1=st[:, 2:3], scalar2=None, op0=ALU.mult),
                 reads=[("X", s), "st2"], writes=[("Hbf", hb)])

        def norm_b(ch, i, l):
            hb = ch.hb[(i, l)]
            for kt in range(8):
                P.op("pe", lambda e, kt=kt: e.transpose(psT[:, kt, :], Hbf[:, hb, kt * 128:(kt + 1) * 128], ident[:]),
                     reads=[("Hbf", hb), "ident"], writes=["psT"], inc=(kt == 7))
            gb = gpreT[:, l, :].unsqueeze(2).broadcast_to([128, 8, 128])
            P.op("dve", lambda e: e.tensor_tensor(out=HT[:, l, :, i * 128:(i + 1) * 128], in0=psT[:], in1=gb, op=ALU.mult),
                 reads=["psT", "gpreT"], writes=[("HT", l)])

        pref = {}

        def inproj(t, ntok, l):
            ws = pref.pop(t) if t in pref else wtile(t)
            bi = nxt("psA", 2)
            for kt in range(8):
                P.op("pe", lambda e, kt=kt: e.matmul(psA[bi][:, 0:ntok], lhsT=Wr[:, ws, kt, :], rhs=HT[:, l, kt, 0:ntok],
                                                     start=(kt == 0), stop=(kt == 7)),
                     reads=[("Wr", ws), ("HT", l)], writes=[("psA", bi)], inc=(kt == 7))
            return bi

        def outproj_post(s, lhs_fn, lhs_reads, l, out_ap, bank_list, bank_key):
            for half in range(2):
                for kt in range(8):
                    P.op("pe", lambda e, kt=kt, half=half: e.matmul(bank_list[half][:], lhsT=lhs_fn(kt), rhs=WOr[:, half, kt, :],
                                                                   start=(kt == 0), stop=(kt == 7)),
                         reads=list(lhs_reads) + [("WOr", half)], writes=[(bank_key, half)], inc=(kt == 7))
            for half in range(2):
                P.op("act", lambda e, half=half: e.activation(out=TMP[:, half * 512:(half + 1) * 512], in_=bank_list[half][:], func=AF.Square,
                                                             accum_out=st[:, 4 + half:5 + half]),
                     reads=[(bank_key, half)], writes=[("TMP", half), ("st4", half)])
            P.op("dve", lambda e: e.tensor_tensor(out=st[:, 6:7], in0=st[:, 4:5], in1=st[:, 5:6], op=ALU.add),
                 reads=[("st4", 0), ("st4", 1)], writes=["st6"])
            P.op("act", lambda e: e.activation(out=st[:, 7:8], in_=st[:, 6:7], func=AF.Ln, bias=epsc[:], scale=1.0 / D),
                 reads=["st6", "epsc"], writes=["st7"])
            P.op("act", lambda e: e.activation(out=st[:, 8:9], in_=st[:, 7:8], func=AF.Exp, scale=-0.5), reads=["st7"], writes=["st8"])
            for half in range(2):
                hs = slice(half * 512, (half + 1) * 512)
                P.op("dve", lambda e, half=half, hs=hs: e.tensor_tensor(out=TMP[:, hs], in0=bank_list[half][:], in1=gpost[:, l, hs], op=ALU.mult),
                     reads=[(bank_key, half), "gpost"], writes=[("TMP", half)])
                P.op("dve", lambda e, hs=hs: e.scalar_tensor_tensor(out=X[:, s, hs], in0=TMP[:, hs], scalar=st[:, 8:9], in1=X[:, s, hs],
                                                                    op0=ALU.mult, op1=ALU.add),
                     reads=[("TMP", half), "st8", ("X", s)], writes=[("X", s)])
            if out_ap is not None:
                P.dma(out_ap, X[:, s, :], reads=[("X", s)], queue="pool", is_output=True)

        def x_load(ch):
            ch.slots = []
            for b in ch.blocks:
                s = nxt("x", NXS)
                P.dma(X[:, s, :], xin[b * 128:(b + 1) * 128, :], writes=[("X", s)])
                ch.slots.append(s)

        def l0_tile_q(ch, g):
            bi = inproj(T_Q + g, ch.ntok, 0)
            P.op("act", lambda e: e.activation(out=QT(g)[:, 0:ch.ntok], in_=psA[bi][:, 0:ch.ntok], func=AF.Copy, scale=0.125),
                 reads=[("psA", bi)], writes=[QTk(g)])

        def l0_tile_k(ch):
            bi = inproj(T_K, ch.ntok, 0)
            P.op("act", lambda e: e.activation(out=KT[:, 128:128 + ch.ntok], in_=psA[bi][:, 0:ch.ntok], func=AF.Copy), reads=[("psA", bi)], writes=["KTn"])

        def l0_tile_ga(ch, g):
            bi = inproj(T_GA + g, ch.ntok, 0)
            P.op("act", lambda e: e.activation(out=GA(g)[:, 0:ch.ntok], in_=psA[bi][:, 0:ch.ntok], func=AF.Silu),
                 reads=[("psA", bi)], writes=[GAk(g)])

        def l0_tile_gb(ch, g):
            bi = inproj(T_GB + g, ch.ntok, 0)
            P.op("act", lambda e: e.activation(out=GB(g)[:, 0:ch.ntok], in_=psA[bi][:, 0:ch.ntok], func=AF.Silu),
                 reads=[("psA", bi)], writes=[GBk(g)])

        def l0_tile_u(ch, g):
            bi = inproj(T_U + g, ch.ntok, 0)
            P.op("act", lambda e: e.activation(out=U[:, g, 16:16 + ch.ntok], in_=psA[bi][:, 0:ch.ntok], func=AF.Copy),
                 reads=[("psA", bi)], writes=[("U", g)])

        def l0_tile_v(ch):
            ws = wtile(T_V)
            for i in range(ch.nb):
                for kt in range(8):
                    P.op("pe", lambda e, kt=kt, i=i: e.matmul(psV[:, 0:128], lhsT=HT[:, 0, kt, i * 128:(i + 1) * 128], rhs=Wr[:, ws, kt, :],
                                                              start=(kt == 0), stop=(kt == 7)),
                         reads=[("Wr", ws), ("HT", 0)], writes=["psV"], inc=(kt == 7))
                P.op("act", lambda e, i=i: e.activation(out=VS[:, i + 1, :], in_=psV[:, 0:128], func=AF.Copy), reads=["psV"], writes=[("VS", i + 1)])

        def l0_scores(ch, i):
            tb = slice(i * 128, (i + 1) * 128)
            for c in range(2):
                kcols = slice(i * 128 + c * 128, i * 128 + c * 128 + 128)
                pis = [nxt("sc", 2), nxt("sc", 2)]
                for g in range(4):
                    for kv in range(2):
                        pi = pis[kv]
                        pr = slice(kv * 64, kv * 64 + 64)
                        P.op("pe", lambda e, g=g, pi=pi, pr=pr: e.matmul(psB[pi][:, g * 128:(g + 1) * 128], lhsT=KT[pr, kcols], rhs=QT(g)[pr, tb],
                                                                        start=True, stop=True),
                             reads=["KTn", "KTh", QTk(g)], writes=[("psB", pi)], inc=(g == 3))
                for kv in range(2):
                    pi = pis[kv]
                    tab = kv * 2 + c
                    if c == 0 and ch.first_own and i == 0:
                        tab = 4 + kv
                    u = (i % 3) * 4 + kv * 2 + c
                    P.op("act", lambda e, u=u, pi=pi: e.activation(out=PT[u][:], in_=psB[pi][:], func=AF.Exp), reads=[("psB", pi)], writes=[("PT", u)])
                    P.op("pool", lambda e, u=u, tab=tab: e.tensor_tensor(out=PT[u][:], in0=PT[u][:], in1=btab[:, tab, :], op=ALU.mult),
                         reads=[("PT", u), "btab"], writes=[("PT", u)])

        def l0_pv(ch, i):
            tb = slice(i * 128, (i + 1) * 128)
            for kv in range(2):
                for c in range(2):
                    u = (i % 3) * 4 + kv * 2 + c
                    P.op("pe", lambda e, kv=kv, c=c, u=u: e.matmul(psD[kv * 64:kv * 64 + 64, :], lhsT=ones[:, 0:64], rhs=PT[u][:],
                                                                  start=(c == 0), stop=False, tile_position=(0, kv * 64)),
                         reads=["ones", ("PT", u)], writes=["psD"], inc=False)
                P.op("pe", lambda e, kv=kv: e.matmul(psD[kv * 64:kv * 64 + 64, :], lhsT=ones[0:1, 0:64], rhs=eshi[:, kv * 512:(kv + 1) * 512],
                                                     start=False, stop=False, tile_position=(0, kv * 64)),
                     reads=["ones", "eshi"], writes=["psD"], inc=False)
                P.op("pe", lambda e, kv=kv: e.matmul(psD[kv * 64:kv * 64 + 64, :], lhsT=ones[0:1, 0:64], rhs=eslo[:, kv * 512:(kv + 1) * 512],
                                                     start=False, stop=True, tile_position=(0, kv * 64)),
                     reads=["ones", "eslo"], writes=["psD"], inc=(kv == 1))
            for kv in range(2):
                for c in range(2):
                    u = (i % 3) * 4 + kv * 2 + c
                    P.op("pe", lambda e, kv=kv, c=c, u=u: e.matmul(psN[kv * 64:kv * 64 + 64, :], lhsT=VS[:, i + c, kv * 64:kv * 64 + 64], rhs=PT[u][:],
                                                                  start=(c == 0), stop=(c == 1), tile_position=(0, kv * 64)),
                         reads=[("VS", i + c), ("PT", u)], writes=["psN"], inc=(kv == 1 and c == 1))
            P.op("act", lambda e: e.activation(out=DS[:], in_=psD[:], func=AF.Ln), reads=["psD"], writes=["DS"])
            P.op("act", lambda e: e.activation(out=DS[:], in_=DS[:], func=AF.Exp, scale=-1.0), reads=["DS"], writes=["DS"])
            P.op("dve", lambda e: e.tensor_tensor(out=YN[:], in0=psN[:], in1=DS[:], op=ALU.mult), reads=["psN", "DS"], writes=["YN"])
            for g in range(4):
                P.op("pool", lambda e, g=g: e.tensor_tensor(out=YA(g)[:, tb], in0=YN[:, g * 128:(g + 1) * 128], in1=GA(g)[:, tb], op=ALU.mult),
                     reads=["YN", GAk(g)], writes=[YAk(g, i)])

        def l0_pool_pre(ch, g):
            ntok = ch.ntok
            w = 2 << g
            E = 16 + ntok
            cur = None
            step = 1
            bufs = [SA, SB]
            k = 0
            while step < w:
                lo = 2 * step - 1
                dst = bufs[k % 2]
                srcap = U[:, g, :] if cur is None else cur[:]
                rk = [("U", g)] if cur is None else [("SAB", (k - 1) % 2)]
                P.op("pool", lambda e, dst=dst, srcap=srcap, lo=lo, step=step: e.tensor_tensor(
                    out=dst[:, lo:E], in0=srcap[:, lo:E], in1=srcap[:, lo - step:E - step], op=ALU.add),
                     reads=rk, writes=[("SAB", k % 2)])
                cur = dst
                k += 1
                step *= 2
            ch.pool_cur[g] = (cur, ("SAB", (k - 1) % 2))

        def l0_pool_fin(ch, g):
            ntok = ch.ntok
            w = 2 << g
            E = 16 + ntok
            cur, lastk = ch.pool_cur[g]
            P.op("dve", lambda e: e.scalar_tensor_tensor(out=PL(g)[:, 0:ntok], in0=cur[:, 16:E], scalar=1.0 / w, in1=U[:, g, 16:E],
                                                         op0=ALU.mult, op1=ALU.subtract),
                 reads=[lastk, ("U", g)], writes=PLks(g))
            if ch.first_own:
                P.op("dve", lambda e: e.tensor_tensor(out=T16[:], in0=cur[:, 16:32], in1=rc0[:, g, :], op=ALU.mult),
                     reads=[lastk, "rc0"], writes=["T16"])
                P.op("dve", lambda e: e.tensor_tensor(out=PL(g)[:, 0:16], in0=T16[:], in1=U[:, g, 16:32], op=ALU.subtract),
                     reads=["T16", ("U", g)] + PLks(g), writes=PLks(g))

        def l0_pool_mm(ch, g):
            ntok = ch.ntok
            bk, bkey = [(psV, "psV"), (psA[0], ("psA", 0)), (psA[1], ("psA", 1)), (psV, "psV")][g]
            P.op("pe", lambda e: e.matmul(bk[:, 0:ntok], lhsT=PW[:, g, :], rhs=PL(g)[:, 0:ntok], start=True, stop=True),
                 reads=["PW"] + PLks(g), writes=[bkey])
            P.op("dve", lambda e: e.scalar_tensor_tensor(out=YB(g)[:, 0:ntok], in0=bk[:, 0:ntok], scalar=pscale[:, g:g + 1],
                                                         in1=GB(g)[:, 0:ntok], op0=ALU.mult, op1=ALU.mult),
                 reads=[bkey, "pscale", GBk(g)], writes=YBk(g))

        def l0_out(ch, i):
            tb = slice(i * 128, (i + 1) * 128)
            lhs = lambda kt: (YA(kt)[:, tb] if kt < 4 else YB(kt - 4)[:, tb])
            bl_, bk_ = (psA, "psA") if i % 2 == 0 else (psB, "psB")
            outproj_post(ch.slots[i], lhs, [YAk(g, i) for g in range(4)] + [("R1", 4 + g, i) for g in range(4)], 0, None, bl_, bk_)

        def l0_carry(ch):
            ntok, nb = ch.ntok, ch.nb
            P.op("pool", lambda e: e.tensor_copy(out=KT[:, 0:128], in_=KT[:, ntok:ntok + 128]), reads=["KTn"], writes=["KTh"])
            P.op("pool", lambda e: e.tensor_copy(out=VS[:, 0, :], in_=VS[:, nb, :]), reads=[("VS", nb)], writes=[("VS", 0)])
            for g in range(4):
                P.op("pool", lambda e, g=g: e.tensor_copy(out=U[:, g, 0:16], in_=U[:, g, ntok:ntok + 16]), reads=[("U", g)], writes=[("U", g)])

        def l1_tile_ba(ch, ct):
            ntok = ch.ntok
            si = nxt("sig", 2)
            bi = inproj(T_B + ct, ntok, 1)
            P.op("act", lambda e: e.activation(out=SIG[si][:, 0:ntok], in_=psA[bi][:, 0:ntok], func=AF.Tanh, scale=0.5),
                 reads=[("psA", bi)], writes=[("SIG", si)])
            bi2 = inproj(T_A + ct, ntok, 1)
            P.op("dve", lambda e: e.scalar_tensor_tensor(out=GLU[:, ct, HL:HL + ntok], in0=SIG[si][:, 0:ntok], scalar=1.0,
                                                         in1=psA[bi2][:, 0:ntok], op0=ALU.add, op1=ALU.mult),
                 reads=[("psA", bi2), ("SIG", si)], writes=[("GLU", ct)])

        def l1_tile_g(ch, ct):
            bi3 = inproj(T_G + ct, ch.ntok, 1)
            P.op("act", lambda e: e.activation(out=GATE(ct)[:, 0:ch.ntok], in_=psA[bi3][:, 0:ch.ntok], func=AF.Silu),
                 reads=[("psA", bi3)], writes=GATEk(ct))

        def l1_gs(ch, ct):
            ntok = ch.ntok
            gi = nxt("dg", 2)
            ch.gi[ct] = gi
            for r in range(4):
                Lr = ntok + 28 - (1 if r == 3 else 0)
                for cg in range(4):
                    P.op("dve", lambda e, r=r, cg=cg, Lr=Lr: e.tensor_copy(out=GS[r * 32:(r + 1) * 32, gi, cg, 0:Lr],
                                                                          in_=GLU[cg * 32:(cg + 1) * 32, ct, r:r + Lr]),
                         reads=[("GLU", ct)], writes=[("GS", gi, r, cg)], inc=(r == 3 and cg == 3))

        def l1_conv(ch, ct):
            ntok = ch.ntok
            gi = ch.gi[ct]
            pi = nxt("pc", 2)
            for tg in range(8):
                for cg in range(4):
                    P.op("pe", lambda e, tg=tg, cg=cg: e.matmul(psB[pi][cg * 32:(cg + 1) * 32, 0:ntok], lhsT=DG2[:, ct, cg * 8 + tg, :],
                                                                rhs=GS[:, gi, cg, 4 * tg:4 * tg + ntok], start=(tg == 0), stop=(tg == 7),
                                                                tile_position=(0, cg * 32)),
                         reads=["DG2"] + [("GS", gi, r, cg) for r in range(4)], writes=[("psB", pi)], inc=(tg == 7 and cg == 3))
            ci = nxt("cfb", 2)
            ch.ci[ct] = ci
            P.op("act", lambda e: e.activation(out=CFB[ci][:, 0:ntok], in_=psB[pi][:, 0:ntok], func=AF.Identity,
                                               bias=cp[:, 0, ct:ct + 1], scale=0.5), reads=[("psB", pi), "cp"], writes=[("CFB", ci)])
            P.op("act", lambda e: e.activation(out=SQB[ci][:, 0:ntok], in_=psB[pi][:, 0:ntok], func=AF.Square,
                                               bias=cp[:, 0, ct:ct + 1], scale=0.5), reads=[("psB", pi), "cp"], writes=[("SQB", ci)])
            P.op("act", lambda e: e.activation(out=CF(ct)[:, 0:ntok], in_=psB[pi][:, 0:ntok], func=AF.Identity,
                                               bias=cp[:, 0, ct:ct + 1], scale=0.5),
                 reads=[("psB", pi), "cp"], writes=CFk(ct))

        def l1_stats(ch, ct):
            ntok = ch.ntok
            ci = ch.ci[ct]
            P.op("pe", lambda e: e.matmul(psN[:, 0:ntok], lhsT=ones[:], rhs=CFB[ci][:, 0:ntok], start=(ct == 0), stop=(ct == 7)),
                 reads=["ones", ("CFB", ci)], writes=["psN"])
            P.op("pe", lambda e: e.matmul(psD[:, 0:ntok], lhsT=ones[:], rhs=SQB[ci][:, 0:ntok], start=(ct == 0), stop=(ct == 7)),
                 reads=["ones", ("SQB", ci)], writes=["psD"])

        def l1_carry(ch):
            for ct in range(8):
                P.op("pool", lambda e, ct=ct: e.tensor_copy(out=GLU[:, ct, 0:HL], in_=GLU[:, ct, ch.ntok:ch.ntok + HL]),
                     reads=[("GLU", ct)], writes=[("GLU", ct)])

        def l1_ln_chain(ch):
            ntok = ch.ntok
            P.op("act", lambda e: e.activation(out=MU[:, 0:ntok], in_=psN[:, 0:ntok], func=AF.Copy, scale=1.0 / D), reads=["psN"], writes=["MU"])
            P.op("dve", lambda e: e.tensor_tensor(out=MSQ[:, 0:ntok], in0=MU[:, 0:ntok], in1=MU[:, 0:ntok], op=ALU.mult), reads=["MU"], writes=[("SAB", 0)])
            P.op("dve", lambda e: e.scalar_tensor_tensor(out=RS[:, 0:ntok], in0=psD[:, 0:ntok], scalar=1.0 / D, in1=MSQ[:, 0:ntok],
                                                         op0=ALU.mult, op1=ALU.subtract), reads=["psD", ("SAB", 0)], writes=["RS"])
            P.op("act", lambda e: e.activation(out=RS[:, 0:ntok], in_=RS[:, 0:ntok], func=AF.Ln, bias=epsc[:], scale=1.0),
                 reads=["RS", "epsc"], writes=["RS"])
            P.op("act", lambda e: e.activation(out=RS[:, 0:ntok], in_=RS[:, 0:ntok], func=AF.Exp, scale=-0.5), reads=["RS"], writes=["RS"])

        def l1_ln_apply(ch, ct):
            ntok = ch.ntok
            P.op("dve", lambda e: e.tensor_tensor(out=CF(ct)[:, 0:ntok], in0=CF(ct)[:, 0:ntok], in1=MU[:, 0:ntok], op=ALU.subtract),
                 reads=CFk(ct) + ["MU"], writes=CFk(ct))
            P.op("dve", lambda e: e.tensor_tensor(out=CF(ct)[:, 0:ntok], in0=CF(ct)[:, 0:ntok], in1=RS[:, 0:ntok], op=ALU.mult),
                 reads=CFk(ct) + ["RS"], writes=CFk(ct))
            si = nxt("scn", 2)
            P.op("act", lambda e: e.activation(out=SCN[si][:, 0:ntok], in_=CF(ct)[:, 0:ntok], func=AF.Silu,
                                               bias=cp[:, 2, ct:ct + 1], scale=cp[:, 1, ct:ct + 1]),
                 reads=CFk(ct) + ["cp"], writes=[("SCN", si)])
            P.op("pool", lambda e: e.tensor_tensor(out=GATE(ct)[:, 0:ntok], in0=GATE(ct)[:, 0:ntok], in1=SCN[si][:, 0:ntok], op=ALU.mult),
                 reads=GATEk(ct) + [("SCN", si)], writes=GATEk(ct))

        def l1_out(ch, i):
            tb = slice(i * 128, (i + 1) * 128)
            ob = ch.blocks[i] - 2
            outproj_post(ch.slots[i], lambda kt: GATE(kt)[:, tb], [("R1", ct, i) for ct in range(8)], 1,
                         yout[ob * 128:(ob + 1) * 128, :], *((psB, "psB") if i % 2 == 0 else (psA, "psA")))

        def weave(A, B):
            na, nb_ = len(A), len(B)
            ia = ib = 0
            while ia < na or ib < nb_:
                if ia < na and (ib >= nb_ or ia * max(nb_, 1) <= ib * max(na, 1)):
                    A[ia]()
                    ia += 1
                else:
                    B[ib]()
                    ib += 1

        def l0_norms(ch):
            L = []
            for i in range(ch.nb + 1):
                if i < ch.nb:
                    L.append(lambda i=i: norm_a(ch, i, 0))
                if i >= 1:
                    L.append(lambda i=i: norm_b(ch, i - 1, 0))
            return L

        def l0_early(ch):
            L = [lambda: l0_tile_k(ch), lambda: l0_tile_v(ch)]
            for g in range(4):
                L.append(lambda g=g: l0_tile_u(ch, g))
                L.append(lambda g=g: (l0_pool_fin(ch, g - 1) if g >= 1 else None, l0_pool_pre(ch, g)))
            return L

        def l0_mid(ch):
            L = [lambda g=g: l0_tile_q(ch, g) for g in range(4)]
            L += [lambda g=g: l0_tile_ga(ch, g) for g in range(4)]
            L += [lambda g=g: l0_tile_gb(ch, g) for g in range(4)]
            return L

        def l0_late(ch):
            nb = ch.nb
            wo_load(0)
            l0_pool_fin(ch, 3)
            l0_scores(ch, 0)
            if nb > 1:
                l0_scores(ch, 1)
            outs = []

            def do_out(j):
                l0_out(ch, j)
                norm_a(ch, j, 1)
                if j >= 1:
                    norm_b(ch, j - 1, 1)

            nout = 0
            for i in range(nb):
                if i + 2 < nb:
                    l0_scores(ch, i + 2)
                l0_pv(ch, i)
                if i == min(1, nb - 1):
                    for g in range(4):
                        l0_pool_mm(ch, g)
                if i >= 1:
                    do_out(nout)
                    nout += 1
            while nout < nb:
                do_out(nout)
                nout += 1
            norm_b(ch, nb - 1, 1)
            l0_carry(ch)

        def l1_front(ch, nx):
            tail = l0_norms(nx) if nx is not None else []
            if ch.halo:
                for ct in range(8):
                    l1_tile_ba(ch, ct)
                for f in tail:
                    f()
                l1_carry(ch)
                return
            for ct in range(8):
                l1_tile_ba(ch, ct)
                if ct - 1 >= 0:
                    l1_gs(ch, ct - 1)
                if ct - 2 >= 0:
                    l1_conv(ch, ct - 2)
                if ct - 3 >= 0:
                    l1_stats(ch, ct - 3)
            per = [1, 1, 1, 1, 1, 1, 1, 1]
            ti = 0
            for k in range(8):
                l1_tile_g(ch, k)
                if k == 0:
                    l1_gs(ch, 7)
                    l1_conv(ch, 6)
                    l1_stats(ch, 5)
                elif k == 1:
                    l1_conv(ch, 7)
                    l1_stats(ch, 6)
                elif k == 2:
                    l1_stats(ch, 7)
                elif k == 3:
                    l1_ln_chain(ch)
                for f in tail[ti:ti + per[k]]:
                    f()
                ti += per[k]
            for f in tail[ti:]:
                f()
            l1_carry(ch)

        def s2_s3(ch, nx):
            E = l0_early(nx) if nx is not None else []
            Q = [lambda g=g: l0_tile_q(nx, g) for g in range(4)] if nx is not None else []
            GAl = [lambda g=g: l0_tile_ga(nx, g) for g in range(4)] if nx is not None else []
            GBl = [lambda g=g: l0_tile_gb(nx, g) for g in range(4)] if nx is not None else []
            if ch.halo:
                for f in E + Q + GAl + GBl:
                    f()
                return
            I = (E + Q + GAl + GBl) if nx is not None else []
            for ct in range(8):
                l1_ln_apply(ch, ct)
                for f in I[3 * ct:3 * ct + 3] if ct < 7 else I[21:]:
                    f()
            wo_load(1)
            for i in range(ch.nb):
                l1_out(ch, i)

        chunks = []
        bl = [[0, 1]]
        b = 2
        while b < n_own_blocks + 2:
            nb = min(NB, n_own_blocks + 2 - b)
            bl.append(list(range(b, b + nb)))
            b += nb
        for idx, blocks in enumerate(bl):
            ch = Chunk()
            ch.blocks, ch.nb, ch.ntok = blocks, len(blocks), 128 * len(blocks)
            ch.halo, ch.first_own = (idx == 0), (idx == 1)
            ch.gi, ch.ci, ch.hb, ch.pool_cur = {}, {}, {}, {}
            chunks.append(ch)

        def late_consts():
            P.op("pool", lambda e: e.memset(GLU[:], 0.0), writes=[("GLU", ct) for ct in range(8)])
            P.op("pool", lambda e: e.memset(GS[:], 0.0), writes=[("GS", gi, r, cg) for gi in range(2) for r in range(4) for cg in range(4)])
            for ct in range(8):
                P.op("pool", lambda e, ct=ct: e.tensor_tensor(out=DG2[:, ct], in0=wsh[:, ct, :].unsqueeze(2).broadcast_to([128, 32, 32]),
                                                             in1=id4[:].unsqueeze(1).broadcast_to([128, 32, 32]), op=ALU.mult),
                     reads=["wsh", "id4"], writes=["DG2"])

        def l0_weight_prepass():
            order = [T_K, T_V] + [T_U + g for g in range(4)] + [T_Q + g for g in range(4)] \
                + [T_GA + g for g in range(4)] + [T_GB + g for g in range(4)]
            pend = []
            for j, t in enumerate(order):
                stg = 2 + j % 6
                P.dma(X[:, stg, :], wt_d[t], writes=[("X", stg)])
                ws = j % NW
                wflat = Wr[:, ws].rearrange("p a b -> p (a b)")
                if j % 2 == 0:
                    P.op("dve", lambda e, wflat=wflat, stg=stg: e.tensor_copy(out=wflat, in_=X[:, stg, :]), reads=[("X", stg)], writes=[("Wr", ws)])
                else:
                    P.op("act", lambda e, wflat=wflat, stg=stg: e.activation(out=wflat, in_=X[:, stg, :], func=AF.Copy), reads=[("X", stg)], writes=[("Wr", ws)])
                pend.append((t, ws, wflat))
                if len(pend) > 4:
                    t0, ws0, wf0 = pend.pop(0)
                    P.dma(wt_bf[t0], wf0, reads=[("Wr", ws0)], writes=[("wtbf", t0)])
            for t0, ws0, wf0 in pend:
                P.dma(wt_bf[t0], wf0, reads=[("Wr", ws0)], writes=[("wtbf", t0)])

        def l1_weight_conversions():
            for ct in range(8):
                for t in (T_B + ct, T_A + ct, T_G + ct):
                    conv_t(t)
            for h in range(2, 4):
                P.dma(wo_bf[h], wo_d[h], writes=[("wobf", h)], queue="pool")

        x_load(chunks[0])
        for f in l0_norms(chunks[0]):
            f()
        l0_weight_prepass()
        for f in l0_early(chunks[0]):
            f()
        l1_weight_conversions()
        for f in l0_mid(chunks[0]):
            f()
        l0_late(chunks[0])
        late_consts()
        for idx, ch in enumerate(chunks):
            nx = chunks[idx + 1] if idx + 1 < len(chunks) else None
            if nx is not None:
                x_load(nx)
            l1_front(ch, nx)
            s2_s3(ch, nx)
            if nx is not None:
                l0_late(nx)
        P.finish()
    return nc


def _perm_l0():
    q0, k0, v0, ga0, u0, gb0 = 0, 512, 640, 768, 1280, 1792
    cols = []
    for g in range(4):
        cols += list(range(q0 + g * 64, q0 + g * 64 + 64)) + list(range(q0 + (4 + g) * 64, q0 + (4 + g) * 64 + 64))
    cols += list(range(k0, k0 + 128))
    for g in range(4):
        cols += list(range(ga0 + g * 64, ga0 + g * 64 + 64)) + list(range(ga0 + (4 + g) * 64, ga0 + (4 + g) * 64 + 64))
    cols += list(range(u0, u0 + 512))
    cols += list(range(gb0, gb0 + 512))
    cols += list(range(v0, v0 + 128))
    rows = []
    for g in range(4):
        rows += list(range(g * 64, g * 64 + 64)) + list(range((4 + g) * 64, (4 + g) * 64 + 64))
    rows += list(range(512, 1024))
    return np.array(cols), np.array(rows)


def _btab(first_half):
    s = np.arange(128)[:, None]
    q = np.arange(128)[None, :]
    tabs = np.zeros((128, 6, 4, 128), np.float32)
    for kv in range(2):
        for g in range(4):
            h = kv * 4 + g
            slope = 2.0 ** (-(h + 1))
            cur = np.where(q >= s, -slope * (q - s).astype(np.float32), NEG)
            prev = np.where(s > q, -slope * (q + 128 - s).astype(np.float32), NEG)
            tabs[:, kv * 2 + 0, g, :] = prev
            tabs[:, kv * 2 + 1, g, :] = cur
            tabs[:, 4 + kv, g, :] = NEG if first_half else prev
    return np.ascontiguousarray(tabs.reshape(128, 6, 512))


def _rc0(first_half):
    rc = np.zeros((128, 4, 16), np.float32)
    for g in range(4):
        w = 2 << g
        t = np.arange(16)
        cntv = np.minimum(w, t + 1) if first_half else np.full(16, w)
        rc[:, g, :] = (1.0 / cntv.astype(np.float32))[None, :]
    return rc


def _tiles(w):
    n = w.shape[1] // 128
    return np.ascontiguousarray(w.reshape(8, 128, n, 128).transpose(2, 1, 0, 3).reshape(n, 128, 1024))


def _wo_halves(w):
    return np.ascontiguousarray(w.reshape(8, 128, 2, 512).transpose(2, 1, 0, 3).reshape(2, 128, 4096))


_NC_CACHE = {}


def kernel(x, pre_norm, post_norm, a_w_in, a_sinks, b_pool_w, b_pool_scale, ab_w_out,
           c_w_in, c_dw_w, c_dw_b, c_ln_g, c_ln_b, c_w_out):
    f = lambda v: np.asarray(v, dtype=np.float32)
    x, pre_norm, post_norm = f(x), f(pre_norm), f(post_norm)
    a_w_in, a_sinks, b_pool_w, b_pool_scale, ab_w_out = f(a_w_in), f(a_sinks), f(b_pool_w), f(b_pool_scale), f(ab_w_out)
    c_w_in, c_dw_w, c_dw_b, c_ln_g, c_ln_b, c_w_out = f(c_w_in), f(c_dw_w), f(c_dw_b), f(c_ln_g), f(c_ln_b), f(c_w_out)
    cols, rows = _perm_l0()
    wt = np.concatenate([_tiles(a_w_in[0][:, cols]), _tiles(c_w_in[0])], axis=0)
    wo = np.concatenate([_wo_halves(ab_w_out[0][rows, :]), _wo_halves(c_w_out[0])], axis=0)
    gpreT = np.ascontiguousarray(pre_norm.reshape(2, 8, 128).transpose(2, 0, 1))
    gpost = np.ascontiguousarray(np.broadcast_to(post_norm[None, :, :], (128, 2, D)))
    sk = a_sinks[0]
    sinkT = np.ascontiguousarray(np.repeat(sk, 128)[None, :])
    pscaleT = np.ascontiguousarray(b_pool_scale[0].reshape(4, 128).T)
    dwp = np.zeros((32, D), np.float32)
    dwp[:CONV_K] = c_dw_w[0][:, 0, :]
    wsh = np.ascontiguousarray(dwp.reshape(8, 4, 8, 4, 32).transpose(1, 4, 2, 3, 0).reshape(128, 8, 32))
    id4 = np.ascontiguousarray(np.tile(np.eye(32, dtype=np.float32), (4, 1)))
    cparT = np.ascontiguousarray(np.stack([c_dw_b[0], c_ln_g[0], c_ln_b[0]], axis=0).reshape(3, 8, 128).transpose(2, 0, 1))
    pw = np.ascontiguousarray(b_pool_w[0])
    in_maps = []
    for c in range(NCORES):
        b, hlf = c // 2, c % 2
        own = x[b, hlf * TOK_CORE:(hlf + 1) * TOK_CORE]
        halo = x[b, TOK_CORE - 256:TOK_CORE] if hlf == 1 else np.zeros((256, D), np.float32)
        xin = np.ascontiguousarray(np.concatenate([halo, own], axis=0))
        fh = (hlf == 0)
        in_maps.append({"xin": xin, "wt": wt, "wo": wo, "pool_w": pw, "gpreT": gpreT, "gpost": gpost, "sinkT": sinkT,
                        "pscaleT": pscaleT, "btab": _btab(fh), "rc0": _rc0(fh), "wsh": wsh, "id4": id4, "cparT": cparT})
    if "nc" not in _NC_CACHE:
        _NC_CACHE["nc"] = build_fused()
    res = run_bass_kernel_spmd(_NC_CACHE["nc"], in_maps, core_ids=list(range(NCORES)))
    out = np.zeros((4, 8192, D), np.float32)
    for c in range(NCORES):
        b, hlf = c // 2, c % 2
        out[b, hlf * TOK_CORE:(hlf + 1) * TOK_CORE] = res.results[c]["yout"]
    return out
```

```python
import numpy as np
from contextlib import ExitStack
import concourse.bass as bass
import concourse.mybir as mybir
from concourse.bass_utils import run_bass_kernel_spmd

F32 = mybir.dt.float32
BF16 = mybir.dt.bfloat16
AF = mybir.ActivationFunctionType
ALU = mybir.AluOpType

D = 1024
NCORES = 8
TOK_CORE = 4096
EPS = 1e-6
NEG = -1e30
CONV_K = 31


class Prog:
    def __init__(self, nc, stack):
        self.nc = nc
        self.eng = {"pe": nc.tensor, "act": nc.scalar, "dve": nc.vector, "pool": nc.gpsimd, "sp": nc.sync}
        self.sem = {}
        for n in ["pe", "act", "dve", "pool"]:
            self.sem[n] = stack.enter_context(nc.semaphore("s_" + n))
        self.ND = {"sp": 24, "pool": 16, "act": 2}
        for q, n in self.ND.items():
            for i in range(n):
                self.sem[("d", q, i)] = stack.enter_context(nc.semaphore("s_d%s%d" % (q, i)))
        self.dma_cnt = {"sp": 0, "pool": 0, "act": 0}
        self.cnt = {n: 0 for n in ["pe", "act", "dve", "pool"]}
        self.dma_i = 0
        self.waited = {n: {} for n in self.eng}
        self.res = {}
        self.out_events = []

    def _deps(self, reads, writes):
        deps = {}

        def add(ev):
            if ev is None:
                return
            k, v = ev
            if deps.get(k, 0) < v:
                deps[k] = v

        for r in reads:
            st = self.res.get(r)
            if st:
                add(st["w"])
        for w in writes:
            st = self.res.get(w)
            if st:
                add(st["w"])
                for k, v in st["r"].items():
                    add((k, v))
        return deps

    def _wait(self, en, deps):
        e = self.eng[en]
        for k, v in deps.items():
            if k == "pe" and en == "pe":
                continue
            if self.waited[en].get(k, 0) >= v:
                continue
            e.wait_ge(self.sem[k], v)
            self.waited[en][k] = v

    def _record(self, ev, reads, writes):
        k, v = ev
        for r in reads:
            st = self.res.setdefault(r, {"w": None, "r": {}})
            if st["r"].get(k, 0) < v:
                st["r"][k] = v
        for w in writes:
            self.res[w] = {"w": ev, "r": {}}

    def op(self, en, fn, reads=(), writes=(), inc=True):
        self._wait(en, self._deps(reads, writes))
        ins = fn(self.eng[en])
        ev = (en, self.cnt[en] + 1)
        if inc:
            self.cnt[en] += 1
            ins.then_inc(self.sem[en], 1)
        self._record(ev, reads, writes)
        return ins

    def dma(self, out, in_, reads=(), writes=(), queue="sp", is_output=False):
        n = self.dma_cnt[queue]
        self.dma_cnt[queue] += 1
        slot = n % self.ND[queue]
        k = ("d", queue, slot)
        prev = 16 * (n // self.ND[queue])
        deps = self._deps(reads, writes)
        if prev > 0 and deps.get(k, 0) < prev:
            deps[k] = prev
        self._wait(queue, deps)
        ins = self.eng[queue].dma_start(out=out, in_=in_)
        ins.then_inc(self.sem[k], 16)
        ev = (k, prev + 16)
        self.dma_i += 1
        self._record(ev, reads, writes)
        if is_output:
            self.out_events.append(ev)
        return ev

    def finish(self):
        deps = {}
        for k, v in self.out_events:
            deps[k] = max(deps.get(k, 0), v)
        self.waited["sp"] = {}
        self._wait("sp", deps)


NW = 6
NXS = 8
T_Q, T_K, T_GA, T_U, T_GB, T_V = 0, 4, 5, 9, 13, 17
T_A, T_B, T_G = 18, 26, 34
NTILE = 42


def build_fused(n_own_blocks=32, NB=4):
    nc = bass.Bass("TRN2", target_bir_lowering=False)
    NT = 128 * (n_own_blocks + 2)
    dr = lambda n, s: nc.dram_tensor(n, s, F32, kind="ExternalInput").ap()
    xin = dr("xin", [NT, D])
    wt_d = dr("wt", [NTILE, 128, 1024])
    wo_d = dr("wo", [4, 128, 4096])
    pool_w = dr("pool_w", [4, 128, 128])
    gpreT_d = dr("gpreT", [128, 2, 8])
    gpost_d = dr("gpost", [128, 2, D])
    sink_d = dr("sinkT", [1, 1024])
    pscale_d = dr("pscaleT", [128, 4])
    btab_d = dr("btab", [128, 6, 512])
    rc0_d = dr("rc0", [128, 4, 16])
    wsh_d = dr("wsh", [128, 8, 32])
    id4_d = dr("id4", [128, 32])
    cp_d = dr("cparT", [128, 3, 8])
    yout = nc.dram_tensor("yout", [NT - 256, D], F32, kind="ExternalOutput").ap()
    wt_bf = nc.dram_tensor("wt_bf", [NTILE, 128, 1024], BF16, kind="Internal").ap()
    wo_bf = nc.dram_tensor("wo_bf", [4, 128, 4096], BF16, kind="Internal").ap()

    with ExitStack() as stack:
        P = Prog(nc, stack)
        a = nc.alloc_sbuf_tensor
        NTC = 128 * NB
        HL = CONV_K - 1
        ident = a("ident", [128, 128], BF16)
        ones = a("ones", [128, 128], BF16)
        epsc = a("epsc", [128, 1], F32)
        st = a("stt", [128, 16], F32)
        btab = a("btab_s", [128, 6, 512], BF16)
        gpost = a("gpost_s", [128, 2, D], F32)
        gpreT = a("gpreT_s", [128, 2, 8], F32)
        ES = a("ES", [1, 2, 1024], BF16)
        eshi = ES[0:1, 0, :]
        eslo = ES[0:1, 1, :]
        pscale = a("pscale", [128, 4], F32)
        rc0 = a("rc0_s", [128, 4, 16], F32)
        wsh = a("wsh_s", [128, 8, 32], F32)
        id4 = a("id4_s", [128, 32], F32)
        DG2 = a("DG2", [128, 8, 32, 32], BF16)
        GS = a("GS", [128, 2, 4, NTC + 28], BF16)
        cp = a("cp", [128, 3, 8], F32)
        PW = a("PW", [128, 4, 128], BF16)
        X = a("X", [128, NXS, D], F32)
        Hbf = a("Hbf", [128, 2, D], BF16)
        HT = a("HT", [128, 2, 8, NTC], BF16)
        TMP = a("TMP", [128, D], F32)
        Wr = a("Wr", [128, NW, 8, 128], BF16)
        WOr = a("WOr", [128, 2, 8, 512], BF16)
        KT = a("KT", [128, 128 + NTC], BF16)
        VS = a("VS", [128, NB + 1, 128], BF16)
        U = a("U", [128, 4, 16 + NTC], F32)
        PT = [a("PT%d" % i, [128, 512], BF16) for i in range(12)]
        SA = a("SA", [128, 16 + NTC], F32)
        SB = a("SB", [128, 16 + NTC], F32)
        T16 = a("T16", [128, 16], F32)
        DS = a("DS", [128, 512], F32)
        YN = a("YN", [128, 512], F32)
        R1 = a("R1", [128, 8, NTC], BF16)
        R2 = a("R2", [128, 8, NTC], F32)
        R2v = R2[:].bitcast(BF16).rearrange("p a (b c) -> p (a b) c", c=NTC)
        GLU = a("GLU", [128, 8, HL + NTC], BF16)
        SIG = [a("SIG%d" % i, [128, NTC], F32) for i in range(2)]
        CFB = [a("CFB%d" % i, [128, NTC], BF16) for i in range(2)]
        SQB = [a("SQB%d" % i, [128, NTC], BF16) for i in range(2)]
        MU = a("MU", [128, NTC], F32)
        RS = a("RS", [128, NTC], F32)
        MSQ = SA
        SCN = [a("SCN%d" % i, [128, NTC], BF16) for i in range(2)]
        psT = nc.alloc_psum_tensor("psT", [128, 8, 128], BF16)
        psA = [nc.alloc_psum_tensor("psA%d" % i, [128, 512], F32) for i in range(2)]
        psB = [nc.alloc_psum_tensor("psB%d" % i, [128, 512], F32) for i in range(2)]
        psN = nc.alloc_psum_tensor("psN", [128, 512], F32)
        psD = nc.alloc_psum_tensor("psD", [128, 512], F32)
        psV = nc.alloc_psum_tensor("psV", [128, 512], F32)

        QTk = lambda g: ("R2", g)
        GAk = lambda g: ("R2", 4 + g)
        GBk = lambda g: ("R2", 8 + g)
        PLk = lambda g: ("GS", 0, 0, g)
        PLks = lambda g: [("GS", 0, r, g) for r in range(4)]
        YAk = lambda g, i: ("R1", g, i)
        YBk = lambda g: [("R1", 4 + g, i) for i in range(NB)]
        GATEk = lambda ct: [("R1", ct, i) for i in range(NB)]
        CFk = lambda ct: [("R2", 2 * ct), ("R2", 2 * ct + 1)]
        QT = lambda g: R2v[:, g, :]
        GA = lambda g: R2v[:, 4 + g, :]
        GB = lambda g: R2v[:, 8 + g, :]
        PL = lambda g: GS[:, 0, g, 0:NTC]
        YA = lambda g: R1[:, g, :]
        YB = lambda g: R1[:, 4 + g, :]
        GATE = lambda ct: R1[:, ct, :]
        CF = lambda ct: R2[:, ct, :]

        conv_t = lambda t: P.dma(wt_bf[t], wt_d[t], writes=[("wtbf", t)], queue="pool")
        P.op("pool", lambda e: e.memset(ident[:], 0.0), writes=["ident"])
        P.op("pool", lambda e: e.affine_select(out=ident[:], in_=ident[:], pattern=[[-1, 128]], compare_op=ALU.not_equal,
                                                fill=1.0, base=0, channel_multiplier=1), reads=["ident"], writes=["ident"])
        P.op("pool", lambda e: e.memset(ones[:], 1.0), writes=["ones"])
        P.op("pool", lambda e: e.memset(epsc[:], EPS), writes=["epsc"])
        P.dma(gpreT[:], gpreT_d, writes=["gpreT"])
        P.dma(gpost[:], gpost_d, writes=["gpost"])
        P.dma(TMP[0:1, :], sink_d, writes=[("TMP", 0), ("TMP", 1)])
        P.dma(pscale[:], pscale_d, writes=["pscale"])
        P.dma(rc0[:], rc0_d, writes=["rc0"])
        P.dma(wsh[:], wsh_d, writes=["wsh"])
        P.dma(id4[:], id4_d, writes=["id4"])
        P.dma(cp[:], cp_d, writes=["cp"])
        for t in [T_K, T_V] + [T_U + g for g in range(4)]:
            conv_t(t)
        P.op("pool", lambda e: e.memset(U[:], 0.0), writes=[("U", g) for g in range(4)])
        P.op("pool", lambda e: e.memset(KT[:, 0:128], 0.0), writes=["KTh"])
        P.op("pool", lambda e: e.memset(VS[:, 0, :], 0.0), writes=[("VS", 0)])
        for t in [T_Q + g for g in range(4)]:
            conv_t(t)
        for t in range(6):
            P.dma(btab[:, t, :], btab_d[:, t, :], writes=["btab"], queue="pool")
        P.dma(PW[:], pool_w.rearrange("g c d -> c g d"), writes=["PW"], queue="pool")
        for t in range(6):
            P.op("act", lambda e, t=t: e.activation(out=btab[:, t, :], in_=btab[:, t, :], func=AF.Exp), reads=["btab"], writes=["btab"])
        for t in [T_GA + g for g in range(4)] + [T_GB + g for g in range(4)]:
            conv_t(t)
        for h in range(2):
            P.dma(wo_bf[h], wo_d[h], writes=[("wobf", h)], queue="pool")
        P.op("pool", lambda e: e.memset(GLU[:], 0.0), writes=[("GLU", ct) for ct in range(8)])
        for ct in range(2):
            for t in (T_B + ct, T_A + ct, T_G + ct):
                conv_t(t)
        P.op("pool", lambda e: e.memset(GS[:], 0.0), writes=[("GS", gi, r, cg) for gi in range(2) for r in range(4) for cg in range(4)])
        for ct in range(8):
            P.op("pool", lambda e, ct=ct: e.tensor_tensor(out=DG2[:, ct], in0=wsh[:, ct, :].unsqueeze(2).broadcast_to([128, 32, 32]),
                                                         in1=id4[:].unsqueeze(1).broadcast_to([128, 32, 32]), op=ALU.mult),
                 reads=["wsh", "id4"], writes=["DG2"])
        for ct in range(2, 8):
            for t in (T_B + ct, T_A + ct, T_G + ct):
                conv_t(t)
        for h in range(2, 4):
            P.dma(wo_bf[h], wo_d[h], writes=[("wobf", h)], queue="pool")
        tk = [("TMP", 0), ("TMP", 1)]
        P.op("act", lambda e: e.activation(out=TMP[0:1, :], in_=TMP[0:1, :], func=AF.Exp), reads=tk, writes=tk)
        P.op("dve", lambda e: e.tensor_copy(out=eshi, in_=TMP[0:1, :]), reads=tk, writes=["eshi"])
        P.op("dve", lambda e: e.tensor_tensor(out=TMP[0:1, :], in0=TMP[0:1, :], in1=eshi, op=ALU.subtract), reads=tk + ["eshi"], writes=tk)
        P.op("dve", lambda e: e.tensor_copy(out=eslo, in_=TMP[0:1, :]), reads=tk, writes=["eslo"])

        ctr = {"x": 0, "hb": 0, "psA": 0, "w": 0, "sc": 0, "sig": 0, "dg": 0, "pc": 0, "cfb": 0, "scn": 0}

        def nxt(k, m):
            v = ctr[k] % m
            ctr[k] += 1
            return v

        def wtile(t):
            s = nxt("w", NW)
            P.dma(Wr[:, s].rearrange("p a b -> p (a b)"), wt_bf[t], reads=[("wtbf", t)], writes=[("Wr", s)])
            return s

        def wo_load(l):
            for h in range(2):
                P.dma(WOr[:, h].rearrange("p a b -> p (a b)"), wo_bf[l * 2 + h], reads=[("wobf", l * 2 + h)], writes=[("WOr", h)])

        class Chunk:
            pass

        def norm_a(ch, i, l):
            s = ch.slots[i]
            hb = nxt("hb", 2)
            ch.hb[(i, l)] = hb
            P.op("act", lambda e: e.activation(out=Hbf[:, hb, :], in_=X[:, s, :], func=AF.Square, accum_out=st[:, 0:1]),
                 reads=[("X", s)], writes=[("Hbf", hb), "st0"])
            P.op("act", lambda e: e.activation(out=st[:, 1:2], in_=st[:, 0:1], func=AF.Ln, bias=epsc[:], scale=1.0 / D),
                 reads=["st0", "epsc"], writes=["st1"])
            P.op("act", lambda e: e.activation(out=st[:, 2:3], in_=st[:, 1:2], func=AF.Exp, scale=-0.5), reads=["st1"], writes=["st2"])
            P.op("dve", lambda e: e.tensor_scalar(out=Hbf[:, hb, :], in0=X[:, s, :], scalar1=st[:, 2:3], scalar2=None, op0=ALU.mult),
                 reads=[("X", s), "st2"], writes=[("Hbf", hb)])

        def norm_b(ch, i, l):
            hb = ch.hb[(i, l)]
            for kt in range(8):
                P.op("pe", lambda e, kt=kt: e.transpose(psT[:, kt, :], Hbf[:, hb, kt * 128:(kt + 1) * 128], ident[:]),
                     reads=[("Hbf", hb), "ident"], writes=["psT"], inc=(kt == 7))
            gb = gpreT[:, l, :].unsqueeze(2).broadcast_to([128, 8, 128])
            P.op("dve", lambda e: e.tensor_tensor(out=HT[:, l, :, i * 128:(i + 1) * 128], in0=psT[:], in1=gb, op=ALU.mult),
                 reads=["psT", "gpreT"], writes=[("HT", l)])

        pref = {}

        def inproj(t, ntok, l):
            ws = pref.pop(t) if t in pref else wtile(t)
            bi = nxt("psA", 2)
            for kt in range(8):
                P.op("pe", lambda e, kt=kt: e.matmul(psA[bi][:, 0:ntok], lhsT=Wr[:, ws, kt, :], rhs=HT[:, l, kt, 0:ntok],
                                                     start=(kt == 0), stop=(kt == 7)),
                     reads=[("Wr", ws), ("HT", l)], writes=[("psA", bi)], inc=(kt == 7))
            return bi

        def outproj_post(s, lhs_fn, lhs_reads, l, out_ap, bank_list, bank_key):
            for half in range(2):
                for kt in range(8):
                    P.op("pe", lambda e, kt=kt, half=half: e.matmul(bank_list[half][:], lhsT=lhs_fn(kt), rhs=WOr[:, half, kt, :],
                                                                   start=(kt == 0), stop=(kt == 7)),
                         reads=list(lhs_reads) + [("WOr", half)], writes=[(bank_key, half)], inc=(kt == 7))
            for half in range(2):
                P.op("act", lambda e, half=half: e.activation(out=TMP[:, half * 512:(half + 1) * 512], in_=bank_list[half][:], func=AF.Square,
                                                             accum_out=st[:, 4 + half:5 + half]),
                     reads=[(bank_key, half)], writes=[("TMP", half), ("st4", half)])
            P.op("dve", lambda e: e.tensor_tensor(out=st[:, 6:7], in0=st[:, 4:5], in1=st[:, 5:6], op=ALU.add),
                 reads=[("st4", 0), ("st4", 1)], writes=["st6"])
            P.op("act", lambda e: e.activation(out=st[:, 7:8], in_=st[:, 6:7], func=AF.Ln, bias=epsc[:], scale=1.0 / D),
                 reads=["st6", "epsc"], writes=["st7"])
            P.op("act", lambda e: e.activation(out=st[:, 8:9], in_=st[:, 7:8], func=AF.Exp, scale=-0.5), reads=["st7"], writes=["st8"])
            for half in range(2):
                hs = slice(half * 512, (half + 1) * 512)
                P.op("dve", lambda e, half=half, hs=hs: e.tensor_tensor(out=TMP[:, hs], in0=bank_list[half][:], in1=gpost[:, l, hs], op=ALU.mult),
                     reads=[(bank_key, half), "gpost"], writes=[("TMP", half)])
                P.op("dve", lambda e, hs=hs: e.scalar_tensor_tensor(out=X[:, s, hs], in0=TMP[:, hs], scalar=st[:, 8:9], in1=X[:, s, hs],
                                                                    op0=ALU.mult, op1=ALU.add),
                     reads=[("TMP", half), "st8", ("X", s)], writes=[("X", s)])
            if out_ap is not None:
                P.dma(out_ap, X[:, s, :], reads=[("X", s)], queue="pool", is_output=True)

        def x_load(ch):
            ch.slots = []
            for b in ch.blocks:
                s = nxt("x", NXS)
                P.dma(X[:, s, :], xin[b * 128:(b + 1) * 128, :], writes=[("X", s)])
                ch.slots.append(s)

        def l0_tile_q(ch, g):
            bi = inproj(T_Q + g, ch.ntok, 0)
            P.op("act", lambda e: e.activation(out=QT(g)[:, 0:ch.ntok], in_=psA[bi][:, 0:ch.ntok], func=AF.Copy, scale=0.125),
                 reads=[("psA", bi)], writes=[QTk(g)])

        def l0_tile_k(ch):
            bi = inproj(T_K, ch.ntok, 0)
            P.op("act", lambda e: e.activation(out=KT[:, 128:128 + ch.ntok], in_=psA[bi][:, 0:ch.ntok], func=AF.Copy), reads=[("psA", bi)], writes=["KTn"])

        def l0_tile_ga(ch, g):
            bi = inproj(T_GA + g, ch.ntok, 0)
            P.op("act", lambda e: e.activation(out=GA(g)[:, 0:ch.ntok], in_=psA[bi][:, 0:ch.ntok], func=AF.Silu),
                 reads=[("psA", bi)], writes=[GAk(g)])

        def l0_tile_gb(ch, g):
            bi = inproj(T_GB + g, ch.ntok, 0)
            P.op("act", lambda e: e.activation(out=GB(g)[:, 0:ch.ntok], in_=psA[bi][:, 0:ch.ntok], func=AF.Silu),
                 reads=[("psA", bi)], writes=[GBk(g)])

        def l0_tile_u(ch, g):
            bi = inproj(T_U + g, ch.ntok, 0)
            P.op("act", lambda e: e.activation(out=U[:, g, 16:16 + ch.ntok], in_=psA[bi][:, 0:ch.ntok], func=AF.Copy),
                 reads=[("psA", bi)], writes=[("U", g)])

        def l0_tile_v(ch):
            ws = wtile(T_V)
            for i in range(ch.nb):
                for kt in range(8):
                    P.op("pe", lambda e, kt=kt, i=i: e.matmul(psV[:, 0:128], lhsT=HT[:, 0, kt, i * 128:(i + 1) * 128], rhs=Wr[:, ws, kt, :],
                                                              start=(kt == 0), stop=(kt == 7)),
                         reads=[("Wr", ws), ("HT", 0)], writes=["psV"], inc=(kt == 7))
                P.op("act", lambda e, i=i: e.activation(out=VS[:, i + 1, :], in_=psV[:, 0:128], func=AF.Copy), reads=["psV"], writes=[("VS", i + 1)])

        def l0_scores(ch, i):
            tb = slice(i * 128, (i + 1) * 128)
            for c in range(2):
                kcols = slice(i * 128 + c * 128, i * 128 + c * 128 + 128)
                pis = [nxt("sc", 2), nxt("sc", 2)]
                for g in range(4):
                    for kv in range(2):
                        pi = pis[kv]
                        pr = slice(kv * 64, kv * 64 + 64)
                        P.op("pe", lambda e, g=g, pi=pi, pr=pr: e.matmul(psB[pi][:, g * 128:(g + 1) * 128], lhsT=KT[pr, kcols], rhs=QT(g)[pr, tb],
                                                                        start=True, stop=True),
                             reads=["KTn", "KTh", QTk(g)], writes=[("psB", pi)], inc=(g == 3))
                for kv in range(2):
                    pi = pis[kv]
                    tab = kv * 2 + c
                    if c == 0 and ch.first_own and i == 0:
                        tab = 4 + kv
                    u = (i % 3) * 4 + kv * 2 + c
                    P.op("act", lambda e, u=u, pi=pi: e.activation(out=PT[u][:], in_=psB[pi][:], func=AF.Exp), reads=[("psB", pi)], writes=[("PT", u)])
                    P.op("pool", lambda e, u=u, tab=tab: e.tensor_tensor(out=PT[u][:], in0=PT[u][:], in1=btab[:, tab, :], op=ALU.mult),
                         reads=[("PT", u), "btab"], writes=[("PT", u)])

        def l0_pv(ch, i):
            tb = slice(i * 128, (i + 1) * 128)
            for kv in range(2):
                for c in range(2):
                    u = (i % 3) * 4 + kv * 2 + c
                    P.op("pe", lambda e, kv=kv, c=c, u=u: e.matmul(psD[kv * 64:kv * 64 + 64, :], lhsT=ones[:, 0:64], rhs=PT[u][:],
                                                                  start=(c == 0), stop=False, tile_position=(0, kv * 64)),
                         reads=["ones", ("PT", u)], writes=["psD"], inc=False)
                P.op("pe", lambda e, kv=kv: e.matmul(psD[kv * 64:kv * 64 + 64, :], lhsT=ones[0:1, 0:64], rhs=eshi[:, kv * 512:(kv + 1) * 512],
                                                     start=False, stop=False, tile_position=(0, kv * 64)),
                     reads=["ones", "eshi"], writes=["psD"], inc=False)
                P.op("pe", lambda e, kv=kv: e.matmul(psD[kv * 64:kv * 64 + 64, :], lhsT=ones[0:1, 0:64], rhs=eslo[:, kv * 512:(kv + 1) * 512],
                                                     start=False, stop=True, tile_position=(0, kv * 64)),
                     reads=["ones", "eslo"], writes=["psD"], inc=(kv == 1))
            for kv in range(2):
                for c in range(2):
                    u = (i % 3) * 4 + kv * 2 + c
                    P.op("pe", lambda e, kv=kv, c=c, u=u: e.matmul(psN[kv * 64:kv * 64 + 64, :], lhsT=VS[:, i + c, kv * 64:kv * 64 + 64], rhs=PT[u][:],
                                                                  start=(c == 0), stop=(c == 1), tile_position=(0, kv * 64)),
                         reads=[("VS", i + c), ("PT", u)], writes=["psN"], inc=(kv == 1 and c == 1))
            P.op("act", lambda e: e.activation(out=DS[:], in_=psD[:], func=AF.Ln), reads=["psD"], writes=["DS"])
            P.op("act", lambda e: e.activation(out=DS[:], in_=DS[:], func=AF.Exp, scale=-1.0), reads=["DS"], writes=["DS"])
            P.op("dve", lambda e: e.tensor_tensor(out=YN[:], in0=psN[:], in1=DS[:], op=ALU.mult), reads=["psN", "DS"], writes=["YN"])
            for g in range(4):
                P.op("pool", lambda e, g=g: e.tensor_tensor(out=YA(g)[:, tb], in0=YN[:, g * 128:(g + 1) * 128], in1=GA(g)[:, tb], op=ALU.mult),
                     reads=["YN", GAk(g)], writes=[YAk(g, i)])

        def l0_pool_pre(ch, g):
            ntok = ch.ntok
            w = 2 << g
            E = 16 + ntok
            cur = None
            step = 1
            bufs = [SA, SB]
            k = 0
            while step < w:
                lo = 2 * step - 1
                dst = bufs[k % 2]
                srcap = U[:, g, :] if cur is None else cur[:]
                rk = [("U", g)] if cur is None else [("SAB", (k - 1) % 2)]
                P.op("pool", lambda e, dst=dst, srcap=srcap, lo=lo, step=step: e.tensor_tensor(
                    out=dst[:, lo:E], in0=srcap[:, lo:E], in1=srcap[:, lo - step:E - step], op=ALU.add),
                     reads=rk, writes=[("SAB", k % 2)])
                cur = dst
                k += 1
                step *= 2
            ch.pool_cur[g] = (cur, ("SAB", (k - 1) % 2))

        def l0_pool_fin(ch, g):
            ntok = ch.ntok
            w = 2 << g
            E = 16 + ntok
            cur, lastk = ch.pool_cur[g]
            P.op("dve", lambda e: e.scalar_tensor_tensor(out=PL(g)[:, 0:ntok], in0=cur[:, 16:E], scalar=1.0 / w, in1=U[:, g, 16:E],
                                                         op0=ALU.mult, op1=ALU.subtract),
                 reads=[lastk, ("U", g)], writes=PLks(g))
            if ch.first_own:
                P.op("dve", lambda e: e.tensor_tensor(out=T16[:], in0=cur[:, 16:32], in1=rc0[:, g, :], op=ALU.mult),
                     reads=[lastk, "rc0"], writes=["T16"])
                P.op("dve", lambda e: e.tensor_tensor(out=PL(g)[:, 0:16], in0=T16[:], in1=U[:, g, 16:32], op=ALU.subtract),
                     reads=["T16", ("U", g)] + PLks(g), writes=PLks(g))

        def l0_pool_mm(ch, g):
            ntok = ch.ntok
            bk, bkey = [(psV, "psV"), (psA[0], ("psA", 0)), (psA[1], ("psA", 1)), (psV, "psV")][g]
            P.op("pe", lambda e: e.matmul(bk[:, 0:ntok], lhsT=PW[:, g, :], rhs=PL(g)[:, 0:ntok], start=True, stop=True),
                 reads=["PW"] + PLks(g), writes=[bkey])
            P.op("dve", lambda e: e.scalar_tensor_tensor(out=YB(g)[:, 0:ntok], in0=bk[:, 0:ntok], scalar=pscale[:, g:g + 1],
                                                         in1=GB(g)[:, 0:ntok], op0=ALU.mult, op1=ALU.mult),
                 reads=[bkey, "pscale", GBk(g)], writes=YBk(g))

        def l0_out(ch, i):
            tb = slice(i * 128, (i + 1) * 128)
            lhs = lambda kt: (YA(kt)[:, tb] if kt < 4 else YB(kt - 4)[:, tb])
            bl_, bk_ = (psA, "psA") if i % 2 == 0 else (psB, "psB")
            outproj_post(ch.slots[i], lhs, [YAk(g, i) for g in range(4)] + [("R1", 4 + g, i) for g in range(4)], 0, None, bl_, bk_)

        def l0_carry(ch):
            ntok, nb = ch.ntok, ch.nb
            P.op("pool", lambda e: e.tensor_copy(out=KT[:, 0:128], in_=KT[:, ntok:ntok + 128]), reads=["KTn"], writes=["KTh"])
            P.op("pool", lambda e: e.tensor_copy(out=VS[:, 0, :], in_=VS[:, nb, :]), reads=[("VS", nb)], writes=[("VS", 0)])
            for g in range(4):
                P.op("pool", lambda e, g=g: e.tensor_copy(out=U[:, g, 0:16], in_=U[:, g, ntok:ntok + 16]), reads=[("U", g)], writes=[("U", g)])

        def l1_tile_ba(ch, ct):
            ntok = ch.ntok
            si = nxt("sig", 2)
            bi = inproj(T_B + ct, ntok, 1)
            P.op("act", lambda e: e.activation(out=SIG[si][:, 0:ntok], in_=psA[bi][:, 0:ntok], func=AF.Tanh, scale=0.5),
                 reads=[("psA", bi)], writes=[("SIG", si)])
            bi2 = inproj(T_A + ct, ntok, 1)
            P.op("dve", lambda e: e.scalar_tensor_tensor(out=GLU[:, ct, HL:HL + ntok], in0=SIG[si][:, 0:ntok], scalar=1.0,
                                                         in1=psA[bi2][:, 0:ntok], op0=ALU.add, op1=ALU.mult),
                 reads=[("psA", bi2), ("SIG", si)], writes=[("GLU", ct)])

        def l1_tile_g(ch, ct):
            bi3 = inproj(T_G + ct, ch.ntok, 1)
            P.op("act", lambda e: e.activation(out=GATE(ct)[:, 0:ch.ntok], in_=psA[bi3][:, 0:ch.ntok], func=AF.Silu),
                 reads=[("psA", bi3)], writes=GATEk(ct))

        def l1_gs(ch, ct):
            ntok = ch.ntok
            gi = nxt("dg", 2)
            ch.gi[ct] = gi
            for r in range(4):
                Lr = ntok + 28 - (1 if r == 3 else 0)
                for cg in range(4):
                    dst = GS[r * 32:(r + 1) * 32, gi, cg, 0:Lr]
                    src = GLU[cg * 32:(cg + 1) * 32, ct, r:r + Lr]
                    if r < 3:
                        P.op("dve", lambda e, dst=dst, src=src: e.tensor_copy(out=dst, in_=src),
                             reads=[("GLU", ct)], writes=[("GS", gi, r, cg)], inc=(r == 2 and cg == 3))
                    else:
                        P.op("act", lambda e, dst=dst, src=src: e.activation(out=dst, in_=src, func=AF.Copy),
                             reads=[("GLU", ct)], writes=[("GS", gi, r, cg)], inc=(cg == 3))

        def l1_conv(ch, ct):
            ntok = ch.ntok
            gi = ch.gi[ct]
            pi = nxt("pc", 2)
            for tg in range(8):
                for cg in range(4):
                    P.op("pe", lambda e, tg=tg, cg=cg: e.matmul(psB[pi][cg * 32:(cg + 1) * 32, 0:ntok], lhsT=DG2[:, ct, cg * 8 + tg, :],
                                                                rhs=GS[:, gi, cg, 4 * tg:4 * tg + ntok], start=(tg == 0), stop=(tg == 7),
                                                                tile_position=(0, cg * 32)),
                         reads=["DG2"] + [("GS", gi, r, cg) for r in range(4)], writes=[("psB", pi)], inc=(tg == 7 and cg == 3))
            ci = nxt("cfb", 2)
            ch.ci[ct] = ci
            P.op("act", lambda e: e.activation(out=CFB[ci][:, 0:ntok], in_=psB[pi][:, 0:ntok], func=AF.Identity,
                                               bias=cp[:, 0, ct:ct + 1], scale=0.5), reads=[("psB", pi), "cp"], writes=[("CFB", ci)])
            P.op("act", lambda e: e.activation(out=SQB[ci][:, 0:ntok], in_=psB[pi][:, 0:ntok], func=AF.Square,
                                               bias=cp[:, 0, ct:ct + 1], scale=0.5), reads=[("psB", pi), "cp"], writes=[("SQB", ci)])
            P.op("act", lambda e: e.activation(out=CF(ct)[:, 0:ntok], in_=psB[pi][:, 0:ntok], func=AF.Identity,
                                               bias=cp[:, 0, ct:ct + 1], scale=0.5),
                 reads=[("psB", pi), "cp"], writes=CFk(ct))

        def l1_stats(ch, ct):
            ntok = ch.ntok
            ci = ch.ci[ct]
            P.op("pe", lambda e: e.matmul(psN[:, 0:ntok], lhsT=ones[:], rhs=CFB[ci][:, 0:ntok], start=(ct == 0), stop=(ct == 7)),
                 reads=["ones", ("CFB", ci)], writes=["psN"])
            P.op("pe", lambda e: e.matmul(psD[:, 0:ntok], lhsT=ones[:], rhs=SQB[ci][:, 0:ntok], start=(ct == 0), stop=(ct == 7)),
                 reads=["ones", ("SQB", ci)], writes=["psD"])

        def l1_carry(ch):
            for ct in range(8):
                P.op("pool", lambda e, ct=ct: e.tensor_copy(out=GLU[:, ct, 0:HL], in_=GLU[:, ct, ch.ntok:ch.ntok + HL]),
                     reads=[("GLU", ct)], writes=[("GLU", ct)])

        def l1_ln_chain(ch):
            ntok = ch.ntok
            P.op("act", lambda e: e.activation(out=MU[:, 0:ntok], in_=psN[:, 0:ntok], func=AF.Copy, scale=1.0 / D), reads=["psN"], writes=["MU"])
            P.op("dve", lambda e: e.tensor_tensor(out=MSQ[:, 0:ntok], in0=MU[:, 0:ntok], in1=MU[:, 0:ntok], op=ALU.mult), reads=["MU"], writes=[("SAB", 0)])
            P.op("dve", lambda e: e.scalar_tensor_tensor(out=RS[:, 0:ntok], in0=psD[:, 0:ntok], scalar=1.0 / D, in1=MSQ[:, 0:ntok],
                                                         op0=ALU.mult, op1=ALU.subtract), reads=["psD", ("SAB", 0)], writes=["RS"])
            P.op("act", lambda e: e.activation(out=RS[:, 0:ntok], in_=RS[:, 0:ntok], func=AF.Ln, bias=epsc[:], scale=1.0),
                 reads=["RS", "epsc"], writes=["RS"])
            P.op("act", lambda e: e.activation(out=RS[:, 0:ntok], in_=RS[:, 0:ntok], func=AF.Exp, scale=-0.5), reads=["RS"], writes=["RS"])

        def l1_ln_apply(ch, ct):
            ntok = ch.ntok
            P.op("dve", lambda e: e.tensor_tensor(out=CF(ct)[:, 0:ntok], in0=CF(ct)[:, 0:ntok], in1=MU[:, 0:ntok], op=ALU.subtract),
                 reads=CFk(ct) + ["MU"], writes=CFk(ct))
            P.op("dve", lambda e: e.tensor_tensor(out=CF(ct)[:, 0:ntok], in0=CF(ct)[:, 0:ntok], in1=RS[:, 0:ntok], op=ALU.mult),
                 reads=CFk(ct) + ["RS"], writes=CFk(ct))
            si = nxt("scn", 2)
            P.op("act", lambda e: e.activation(out=SCN[si][:, 0:ntok], in_=CF(ct)[:, 0:ntok], func=AF.Silu,
                                               bias=cp[:, 2, ct:ct + 1], scale=cp[:, 1, ct:ct + 1]),
                 reads=CFk(ct) + ["cp"], writes=[("SCN", si)])
            P.op("pool", lambda e: e.tensor_tensor(out=GATE(ct)[:, 0:ntok], in0=GATE(ct)[:, 0:ntok], in1=SCN[si][:, 0:ntok], op=ALU.mult),
                 reads=GATEk(ct) + [("SCN", si)], writes=GATEk(ct))

        def l1_out(ch, i):
            tb = slice(i * 128, (i + 1) * 128)
            ob = ch.blocks[i] - 2
            outproj_post(ch.slots[i], lambda kt: GATE(kt)[:, tb], [("R1", ct, i) for ct in range(8)], 1,
                         yout[ob * 128:(ob + 1) * 128, :], *((psB, "psB") if i % 2 == 0 else (psA, "psA")))

        def weave(A, B):
            na, nb_ = len(A), len(B)
            ia = ib = 0
            while ia < na or ib < nb_:
                if ia < na and (ib >= nb_ or ia * max(nb_, 1) <= ib * max(na, 1)):
                    A[ia]()
                    ia += 1
                else:
                    B[ib]()
                    ib += 1

        def l0_norms(ch):
            L = []
            for i in range(ch.nb + 1):
                if i < ch.nb:
                    L.append(lambda i=i: norm_a(ch, i, 0))
                if i >= 1:
                    L.append(lambda i=i: norm_b(ch, i - 1, 0))
            return L

        def l0_early(ch):
            L = [lambda: l0_tile_k(ch), lambda: l0_tile_v(ch)]
            for g in range(4):
                L.append(lambda g=g: l0_tile_u(ch, g))
                L.append(lambda g=g: (l0_pool_fin(ch, g - 1) if g >= 1 else None, l0_pool_pre(ch, g)))
            return L

        def l0_mid(ch):
            L = [lambda g=g: l0_tile_q(ch, g) for g in range(4)]
            L += [lambda g=g: l0_tile_ga(ch, g) for g in range(4)]
            L += [lambda g=g: l0_tile_gb(ch, g) for g in range(4)]
            return L

        def l0_late(ch):
            nb = ch.nb
            wo_load(0)
            l0_pool_fin(ch, 3)
            l0_scores(ch, 0)
            if nb > 1:
                l0_scores(ch, 1)
            outs = []

            def do_out(j):
                l0_out(ch, j)
                norm_a(ch, j, 1)
                if j >= 1:
                    norm_b(ch, j - 1, 1)

            nout = 0
            for i in range(nb):
                if i + 2 < nb:
                    l0_scores(ch, i + 2)
                l0_pv(ch, i)
                if i == min(1, nb - 1):
                    for g in range(4):
                        l0_pool_mm(ch, g)
                if i >= 1:
                    do_out(nout)
                    nout += 1
            while nout < nb:
                do_out(nout)
                nout += 1
            norm_b(ch, nb - 1, 1)
            l0_carry(ch)

        def l1_front(ch, nx):
            tail = l0_norms(nx) if nx is not None else []
            if ch.halo:
                for ct in range(8):
                    l1_tile_ba(ch, ct)
                for f in tail:
                    f()
                l1_carry(ch)
                return
            for ct in range(8):
                l1_tile_ba(ch, ct)
                if ct - 1 >= 0:
                    l1_gs(ch, ct - 1)
                if ct - 2 >= 0:
                    l1_conv(ch, ct - 2)
                if ct - 3 >= 0:
                    l1_stats(ch, ct - 3)
            per = [1, 1, 1, 1, 1, 1, 1, 1]
            ti = 0
            for k in range(8):
                l1_tile_g(ch, k)
                if k == 0:
                    l1_gs(ch, 7)
                    l1_conv(ch, 6)
                    l1_stats(ch, 5)
                elif k == 1:
                    l1_conv(ch, 7)
                    l1_stats(ch, 6)
                elif k == 2:
                    l1_stats(ch, 7)
                elif k == 3:
                    l1_ln_chain(ch)
                for f in tail[ti:ti + per[k]]:
                    f()
                ti += per[k]
            for f in tail[ti:]:
                f()
            l1_carry(ch)

        def s2_s3(ch, nx):
            E = l0_early(nx) if nx is not None else []
            Q = [lambda g=g: l0_tile_q(nx, g) for g in range(4)] if nx is not None else []
            GAl = [lambda g=g: l0_tile_ga(nx, g) for g in range(4)] if nx is not None else []
            GBl = [lambda g=g: l0_tile_gb(nx, g) for g in range(4)] if nx is not None else []
            if ch.halo:
                for f in E + Q + GAl + GBl:
                    f()
                return
            I = (E + Q + GAl + GBl) if nx is not None else []
            for ct in range(8):
                l1_ln_apply(ch, ct)
                for f in I[3 * ct:3 * ct + 3] if ct < 7 else I[21:]:
                    f()
            wo_load(1)
            for i in range(ch.nb):
                l1_out(ch, i)

        chunks = []
        bl = [[0, 1]]
        b = 2
        while b < n_own_blocks + 2:
            nb = min(NB, n_own_blocks + 2 - b)
            bl.append(list(range(b, b + nb)))
            b += nb
        for idx, blocks in enumerate(bl):
            ch = Chunk()
            ch.blocks, ch.nb, ch.ntok = blocks, len(blocks), 128 * len(blocks)
            ch.halo, ch.first_own = (idx == 0), (idx == 1)
            ch.gi, ch.ci, ch.hb, ch.pool_cur = {}, {}, {}, {}
            chunks.append(ch)

        x_load(chunks[0])
        for f in l0_norms(chunks[0]) + l0_early(chunks[0]) + l0_mid(chunks[0]):
            f()
        l0_late(chunks[0])
        for idx, ch in enumerate(chunks):
            nx = chunks[idx + 1] if idx + 1 < len(chunks) else None
            if nx is not None:
                x_load(nx)
            l1_front(ch, nx)
            s2_s3(ch, nx)
            if nx is not None:
                l0_late(nx)
        P.finish()
    return nc


def _perm_l0():
    q0, k0, v0, ga0, u0, gb0 = 0, 512, 640, 768, 1280, 1792
    cols = []
    for g in range(4):
        cols += list(range(q0 + g * 64, q0 + g * 64 + 64)) + list(range(q0 + (4 + g) * 64, q0 + (4 + g) * 64 + 64))
    cols += list(range(k0, k0 + 128))
    for g in range(4):
        cols += list(range(ga0 + g * 64, ga0 + g * 64 + 64)) + list(range(ga0 + (4 + g) * 64, ga0 + (4 + g) * 64 + 64))
    cols += list(range(u0, u0 + 512))
    cols += list(range(gb0, gb0 + 512))
    cols += list(range(v0, v0 + 128))
    rows = []
    for g in range(4):
        rows += list(range(g * 64, g * 64 + 64)) + list(range((4 + g) * 64, (4 + g) * 64 + 64))
    rows += list(range(512, 1024))
    return np.array(cols), np.array(rows)


def _btab(first_half):
    s = np.arange(128)[:, None]
    q = np.arange(128)[None, :]
    tabs = np.zeros((128, 6, 4, 128), np.float32)
    for kv in range(2):
        for g in range(4):
            h = kv * 4 + g
            slope = 2.0 ** (-(h + 1))
            cur = np.where(q >= s, -slope * (q - s).astype(np.float32), NEG)
            prev = np.where(s > q, -slope * (q + 128 - s).astype(np.float32), NEG)
            tabs[:, kv * 2 + 0, g, :] = prev
            tabs[:, kv * 2 + 1, g, :] = cur
            tabs[:, 4 + kv, g, :] = NEG if first_half else prev
    return np.ascontiguousarray(tabs.reshape(128, 6, 512))


def _rc0(first_half):
    rc = np.zeros((128, 4, 16), np.float32)
    for g in range(4):
        w = 2 << g
        t = np.arange(16)
        cntv = np.minimum(w, t + 1) if first_half else np.full(16, w)
        rc[:, g, :] = (1.0 / cntv.astype(np.float32))[None, :]
    return rc


def _tiles(w):
    n = w.shape[1] // 128
    return np.ascontiguousarray(w.reshape(8, 128, n, 128).transpose(2, 1, 0, 3).reshape(n, 128, 1024))


def _wo_halves(w):
    return np.ascontiguousarray(w.reshape(8, 128, 2, 512).transpose(2, 1, 0, 3).reshape(2, 128, 4096))


_NC_CACHE = {}


def kernel(x, pre_norm, post_norm, a_w_in, a_sinks, b_pool_w, b_pool_scale, ab_w_out,
           c_w_in, c_dw_w, c_dw_b, c_ln_g, c_ln_b, c_w_out):
    f = lambda v: np.asarray(v, dtype=np.float32)
    x, pre_norm, post_norm = f(x), f(pre_norm), f(post_norm)
    a_w_in, a_sinks, b_pool_w, b_pool_scale, ab_w_out = f(a_w_in), f(a_sinks), f(b_pool_w), f(b_pool_scale), f(ab_w_out)
    c_w_in, c_dw_w, c_dw_b, c_ln_g, c_ln_b, c_w_out = f(c_w_in), f(c_dw_w), f(c_dw_b), f(c_ln_g), f(c_ln_b), f(c_w_out)
    cols, rows = _perm_l0()
    wt = np.concatenate([_tiles(a_w_in[0][:, cols]), _tiles(c_w_in[0])], axis=0)
    wo = np.concatenate([_wo_halves(ab_w_out[0][rows, :]), _wo_halves(c_w_out[0])], axis=0)
    gpreT = np.ascontiguousarray(pre_norm.reshape(2, 8, 128).transpose(2, 0, 1))
    gpost = np.ascontiguousarray(np.broadcast_to(post_norm[None, :, :], (128, 2, D)))
    sk = a_sinks[0]
    sinkT = np.ascontiguousarray(np.repeat(sk, 128)[None, :])
    pscaleT = np.ascontiguousarray(b_pool_scale[0].reshape(4, 128).T)
    dwp = np.zeros((32, D), np.float32)
    dwp[:CONV_K] = c_dw_w[0][:, 0, :]
    wsh = np.ascontiguousarray(dwp.reshape(8, 4, 8, 4, 32).transpose(1, 4, 2, 3, 0).reshape(128, 8, 32))
    id4 = np.ascontiguousarray(np.tile(np.eye(32, dtype=np.float32), (4, 1)))
    cparT = np.ascontiguousarray(np.stack([c_dw_b[0], c_ln_g[0], c_ln_b[0]], axis=0).reshape(3, 8, 128).transpose(2, 0, 1))
    pw = np.ascontiguousarray(b_pool_w[0])
    in_maps = []
    for c in range(NCORES):
        b, hlf = c // 2, c % 2
        own = x[b, hlf * TOK_CORE:(hlf + 1) * TOK_CORE]
        halo = x[b, TOK_CORE - 256:TOK_CORE] if hlf == 1 else np.zeros((256, D), np.float32)
        xin = np.ascontiguousarray(np.concatenate([halo, own], axis=0))
        fh = (hlf == 0)
        in_maps.append({"xin": xin, "wt": wt, "wo": wo, "pool_w": pw, "gpreT": gpreT, "gpost": gpost, "sinkT": sinkT,
                        "pscaleT": pscaleT, "btab": _btab(fh), "rc0": _rc0(fh), "wsh": wsh, "id4": id4, "cparT": cparT})
    if "nc" not in _NC_CACHE:
        _NC_CACHE["nc"] = build_fused()
    res = run_bass_kernel_spmd(_NC_CACHE["nc"], in_maps, core_ids=list(range(NCORES)))
    out = np.zeros((4, 8192, D), np.float32)
    for c in range(NCORES):
        b, hlf = c // 2, c % 2
        out[b, hlf * TOK_CORE:(hlf + 1) * TOK_CORE] = res.results[c]["yout"]
    return out
```
